# Optimizing a Trainium2 kernel written in Bass

```python
import jax, jax.numpy as jnp
from jax import lax
import numpy as np

D_MODEL = 2048
BATCH = 4
SEQ = 4096
DEPTH = 1

PLE_DIM = 256
RW_HEADS = 16
RW_HEAD_DIM = 64
RW_WIDTH = RW_HEADS * RW_HEAD_DIM
DECAY_LORA = 64
AAA_LORA = 64
FOX_HEADS = 16
FOX_HEAD_DIM = 64
FOX_WIDTH = FOX_HEADS * FOX_HEAD_DIM
Q_BLOCK = 128
NORM_EPS = 1e-6
GN_EPS = 64e-5

RW_COLS = 4 * RW_WIDTH + DECAY_LORA + AAA_LORA
FOX_COLS = 4 * FOX_WIDTH + FOX_HEADS
GATE_COLS = 2 * D_MODEL
N_IN = RW_COLS + FOX_COLS + GATE_COLS

kernel_name = 'hybrid_rwkv7_fox_gated_merge'


def _rmsnorm(x, g):
    xf = x.astype(jnp.float32)
    y = xf * lax.rsqrt(jnp.mean(xf * xf, axis=-1, keepdims=True) + NORM_EPS)
    return (y * g.astype(jnp.float32)).astype(x.dtype)


def _token_shift(z, mu):
    z_prev = jnp.pad(z, ((0, 0), (1, 0), (0, 0)))[:, :-1]
    return z + (z_prev - z) * mu


def _rwkv7_scan(r, w, k, v, kk, a):
    def step(S, inp):
        r_t, w_t, k_t, v_t, kk_t, a_t = inp
        s_kk = jnp.einsum('bhvk,bhk->bhv', S, kk_t)
        S = (S * w_t[:, :, None, :]
             - s_kk[..., None] * (kk_t * a_t)[:, :, None, :]
             + v_t[..., None] * k_t[:, :, None, :])
        y_t = jnp.einsum('bhvk,bhk->bhv', S, r_t)
        return S, y_t
    B, T, H, N = r.shape
    S0 = jnp.zeros((B, H, N, N), jnp.float32)
    xs = tuple(jnp.swapaxes(t, 0, 1) for t in (r, w, k, v, kk, a))
    _, ys = lax.scan(step, S0, xs)
    return jnp.swapaxes(ys, 0, 1)


def _rwkv7_branch(z, w0, w_lora_up, a0, a_lora_up, k_k, k_a, r_k, ln_g, ln_b):
    B, T, _ = z.shape
    f32 = jnp.float32
    C = RW_WIDTH
    r = z[..., 0:C]
    k = z[..., C:2 * C]
    v = z[..., 2 * C:3 * C]
    g = z[..., 3 * C:4 * C]
    wl = z[..., 4 * C:4 * C + DECAY_LORA]
    al = z[..., 4 * C + DECAY_LORA:]
    w_raw = (w0 + jnp.tanh(wl) @ w_lora_up).astype(f32)
    decay = jnp.exp(-jnp.exp(-jax.nn.softplus(-w_raw) - 0.5))
    a = jax.nn.sigmoid((a0 + al @ a_lora_up).astype(f32))
    hs = lambda t: t.astype(f32).reshape(B, T, RW_HEADS, RW_HEAD_DIM)
    r_h, k_h, v_h, a_h, w_h = hs(r), hs(k), hs(v), hs(a), hs(decay)
    kk = k_h * k_k.astype(f32).reshape(RW_HEADS, RW_HEAD_DIM)
    kk = kk / jnp.maximum(jnp.sqrt(jnp.sum(kk * kk, axis=-1, keepdims=True)), 1e-12)
    k_h = k_h * (1.0 + (a_h - 1.0) * k_a.astype(f32).reshape(RW_HEADS, RW_HEAD_DIM))
    y = _rwkv7_scan(r_h, w_h, k_h, v_h, kk, a_h)
    mu = jnp.mean(y, axis=-1, keepdims=True)
    var = jnp.mean(jnp.square(y - mu), axis=-1, keepdims=True)
    y = ((y - mu) * lax.rsqrt(var + GN_EPS)).reshape(B, T, C)
    y = y * ln_g.astype(f32) + ln_b.astype(f32)
    bonus = jnp.sum(r_h * k_h * r_k.astype(f32), axis=-1, keepdims=True) * v_h
    y = y + bonus.reshape(B, T, C)
    return (y * jax.nn.silu(g.astype(f32))).astype(z.dtype)


def _fox_branch(z, b_f):
    B, T, _ = z.shape
    f32 = jnp.float32
    C = FOX_WIDTH
    to_heads = lambda t: t.astype(f32).reshape(B, T, FOX_HEADS, FOX_HEAD_DIM).transpose(0, 2, 1, 3)
    qh = to_heads(z[..., 0:C])
    kh = to_heads(z[..., C:2 * C])
    vh = to_heads(z[..., 2 * C:3 * C])
    g = z[..., 3 * C:4 * C]
    fl = z[..., 4 * C:]
    log_f = jax.nn.log_sigmoid((fl + b_f).astype(f32))
    c = jnp.cumsum(log_f, axis=1).transpose(0, 2, 1)
    scale = FOX_HEAD_DIM ** -0.5
    outs = []
    for blk in range(T // Q_BLOCK):
        s = blk * Q_BLOCK
        e = s + Q_BLOCK
        logits = jnp.einsum('bhqd,bhkd->bhqk', qh[:, :, s:e], kh[:, :, :e]) * scale
        logits = logits + (c[:, :, s:e, None] - c[:, :, None, :e])
        causal = jnp.arange(s, e)[:, None] >= jnp.arange(e)[None, :]
        logits = jnp.where(causal, logits, -jnp.inf)
        probs = jax.nn.softmax(logits, axis=-1)
        outs.append(jnp.einsum('bhqk,bhkd->bhqd', probs, vh[:, :, :e]))
    o = jnp.concatenate(outs, axis=2).transpose(0, 2, 1, 3).reshape(B, T, C)
    return (o * jax.nn.silu(g.astype(f32))).astype(z.dtype)


def setup_inputs(seed: int = 0) -> dict:
    key = jax.random.key(seed)
    ks = jax.random.split(key, 24)
    f32 = jnp.float32
    nrm = lambda k, shape, s: jax.random.normal(k, shape, f32) * s
    C = RW_WIDTH
    return {
        'x': nrm(ks[0], (BATCH, SEQ, D_MODEL), 1.0),
        'p': nrm(ks[1], (DEPTH, BATCH, SEQ, PLE_DIM), 1.0),
        'norm_g': 1.0 + nrm(ks[2], (DEPTH, D_MODEL), 0.02),
        'w_in': nrm(ks[3], (DEPTH, D_MODEL, N_IN), D_MODEL ** -0.5),
        'rw_shift_mu': jax.random.uniform(ks[4], (DEPTH, RW_COLS), f32),
        'rw_w0': jax.random.uniform(ks[5], (DEPTH, C), f32, -6.0, -1.0),
        'rw_w_lora_up': nrm(ks[6], (DEPTH, DECAY_LORA, C), 0.1 * DECAY_LORA ** -0.5),
        'rw_a0': nrm(ks[7], (DEPTH, C), 0.1),
        'rw_a_lora_up': nrm(ks[8], (DEPTH, AAA_LORA, C), 0.1 * AAA_LORA ** -0.5),
        'rw_k_k': 0.85 + nrm(ks[9], (DEPTH, C), 0.02),
        'rw_k_a': 1.0 + nrm(ks[10], (DEPTH, C), 0.02),
        'rw_r_k': nrm(ks[11], (DEPTH, RW_HEADS, RW_HEAD_DIM), 0.1),
        'rw_ln_g': 1.0 + nrm(ks[12], (DEPTH, C), 0.02),
        'rw_ln_b': nrm(ks[13], (DEPTH, C), 0.02),
        'fox_b_f': jax.random.uniform(ks[14], (DEPTH, FOX_HEADS), f32, 1.0, 5.0),
        'w_up_rwkv': nrm(ks[15], (DEPTH, RW_WIDTH, D_MODEL), RW_WIDTH ** -0.5),
        'w_up_fox': nrm(ks[16], (DEPTH, FOX_WIDTH, D_MODEL), FOX_WIDTH ** -0.5),
        'w_out': nrm(ks[17], (DEPTH, D_MODEL, D_MODEL), D_MODEL ** -0.5),
        'ple_proj': nrm(ks[18], (DEPTH, PLE_DIM, D_MODEL), PLE_DIM ** -0.5),
        'ple_gate_w': nrm(ks[19], (DEPTH, D_MODEL, D_MODEL), D_MODEL ** -0.5),
        'ple_norm_g': 1.0 + nrm(ks[20], (DEPTH, D_MODEL), 0.02),
        'final_norm_g': 1.0 + nrm(ks[21], (D_MODEL,), 0.02),
    }


def reference(x, p, norm_g, w_in, rw_shift_mu, rw_w0, rw_w_lora_up, rw_a0, rw_a_lora_up,
              rw_k_k, rw_k_a, rw_r_k, rw_ln_g, rw_ln_b, fox_b_f, w_up_rwkv, w_up_fox, w_out,
              ple_proj, ple_gate_w, ple_norm_g, final_norm_g):
    for i in range(DEPTH):
        h = _rmsnorm(x, norm_g[i])
        z = h @ w_in[i]
        z_rw = _token_shift(z[..., :RW_COLS], rw_shift_mu[i])
        z_fox = z[..., RW_COLS:RW_COLS + FOX_COLS]
        z_gate = z[..., RW_COLS + FOX_COLS:]
        y_rw = _rwkv7_branch(z_rw, rw_w0[i], rw_w_lora_up[i], rw_a0[i], rw_a_lora_up[i],
                             rw_k_k[i], rw_k_a[i], rw_r_k[i], rw_ln_g[i], rw_ln_b[i])
        y_fox = _fox_branch(z_fox, fox_b_f[i])
        u_rw = y_rw @ w_up_rwkv[i]
        u_fox = y_fox @ w_up_fox[i]
        merged = (jax.nn.sigmoid(z_gate[..., :D_MODEL]) * u_rw
                  + jax.nn.sigmoid(z_gate[..., D_MODEL:]) * u_fox)
        x = x + merged @ w_out[i]
        ple = p[i] @ ple_proj[i]
        x = x + ple * jax.nn.sigmoid(_rmsnorm(x, ple_norm_g[i]) @ ple_gate_w[i])
    return _rmsnorm(x, final_norm_g)
```

```python
import numpy as np
import concourse.bass as bass
import concourse.mybir as mybir
from concourse.bass_utils import run_bass_kernel_spmd
from contextlib import ExitStack

F32 = mybir.dt.float32
BF16 = mybir.dt.bfloat16
AF = mybir.ActivationFunctionType
ALU = mybir.AluOpType

ENGS = ('pe', 'act', 'dve', 'pool', 'sp')
NDSEM = 8
D = 2048
NIN = 12432
C0 = 0.6065306597126334
NEG = -30000.0


class Sched:
    def __init__(self, nc, es):
        self.nc = nc
        self.q = {e: [] for e in ENGS}
        self.cnt = {e: 0 for e in ENGS}
        self.seen = {e: {} for e in ENGS}
        self.state = {}
        self.dma_n = {e: 0 for e in ENGS}
        self.sem = {}
        for e in ('pe', 'act', 'dve', 'pool'):
            self.sem[e] = es.enter_context(nc.semaphore('s_' + e))
        for e in ('sp', 'pool', 'act'):
            for i in range(NDSEM):
                self.sem['d%s%d' % (e, i)] = es.enter_context(nc.semaphore('sd_%s%d' % (e, i)))

    def _deps(self, eng, reads, writes):
        deps = {}

        def need(sv):
            if sv is None:
                return
            s, v = sv
            if eng == 'pe' and s == 'pe':
                return
            if deps.get(s, 0) < v:
                deps[s] = v
        for k in reads:
            st = self.state.get(k)
            if st:
                need(st[0])
        for k in writes:
            st = self.state.get(k)
            if st:
                need(st[0])
                for s, v in st[1].items():
                    need((s, v))
        out = []
        for s, v in deps.items():
            if self.seen[eng].get(s, 0) < v:
                self.seen[eng][s] = v
                out.append((s, v))
        return out

    def _commit(self, token, reads, writes):
        for k in writes:
            self.state[k] = [token, {}]
        s, v = token
        for k in reads:
            st = self.state.setdefault(k, [None, {}])
            if st[1].get(s, 0) < v:
                st[1][s] = v

    def op(self, eng, fn, reads=(), writes=(), inc=True):
        waits = self._deps(eng, reads, writes)
        if inc:
            self.cnt[eng] += 1
            token = (eng, self.cnt[eng])
        else:
            token = (eng, self.cnt[eng] + 1)
        self.q[eng].append((fn, waits, (eng, 1) if inc else None))
        self._commit(token, reads, writes)

    def dma(self, q, fn, reads=(), writes=()):
        n = self.dma_n[q]
        self.dma_n[q] += 1
        slot, gen = n % NDSEM, n // NDSEM
        sn = 'd%s%d' % (q, slot)
        waits = self._deps(q, reads, writes)
        if gen > 0 and self.seen[q].get(sn, 0) < 16 * gen:
            self.seen[q][sn] = 16 * gen
            waits.append((sn, 16 * gen))
        token = (sn, 16 * (gen + 1))
        self.q[q].append((fn, waits, (sn, 16)))
        self._commit(token, reads, writes)

    def _allvals(self):
        vals = {e: self.cnt[e] for e in ('pe', 'act', 'dve', 'pool')}
        for q in ('sp', 'pool', 'act'):
            n = self.dma_n[q]
            for slot in range(min(n, NDSEM)):
                gens = (n - slot + NDSEM - 1) // NDSEM
                vals['d%s%d' % (q, slot)] = 16 * gens
        return vals

    def barrier(self, skip_pool=False, final=False):
        vals = self._allvals()
        if skip_pool:
            vals = {k: v for k, v in vals.items() if not k.startswith('dpool')}
        for e in ENGS:
            waits = []
            for s, v in vals.items():
                if e == 'pe' and s == 'pe':
                    continue
                if v > 0 and self.seen[e].get(s, 0) < v:
                    self.seen[e][s] = v
                    waits.append((s, v))
            self.q[e].append((None, waits, None))

    def emit(self, block):
        sem = self.sem

        def run(name, e):
            for fn, waits, inc in self.q[name]:
                for s, v in waits:
                    e.wait_ge(sem[s], v)
                if fn is None:
                    continue
                ins = fn(e)
                if inc is not None:
                    ins.then_inc(sem[inc[0]], inc[1])

        @block.tensor
        def _(e):
            run('pe', e)

        @block.scalar
        def _(e):
            run('act', e)

        @block.vector
        def _(e):
            run('dve', e)

        @block.gpsimd
        def _(e):
            run('pool', e)

        @block.sync
        def _(e):
            run('sp', e)


def build(T, dbg=False, do_rw=True, do_fox=True, do_post=True):
    nc = bass.Bass("TRN2", target_bir_lowering=False)
    NT = T // 128
    NM = T // 512

    def din(name, shape):
        return nc.dram_tensor(name, shape, F32, kind="ExternalInput").ap()
    x = din("x", [T, D])
    p = din("p", [T, 256])
    norm_g = din("norm_g", [1, D])
    w_in = din("w_in", [D, NIN])
    mu = din("rw_shift_mu", [1, 4224])
    rw_w0 = din("rw_w0", [1, 1024])
    rw_wl = din("rw_w_lora_up", [64, 1024])
    rw_a0 = din("rw_a0", [1, 1024])
    rw_al = din("rw_a_lora_up", [64, 1024])
    rw_kk = din("rw_k_k", [1, 1024])
    rw_ka = din("rw_k_a", [1, 1024])
    rw_rk = din("rw_r_k", [1, 1024])
    rw_lng = din("rw_ln_g", [1, 1024])
    rw_lnb = din("rw_ln_b", [1, 1024])
    fox_bf = din("fox_b_f", [1, 16])
    w_uprw = din("w_up_rwkv", [1024, D])
    w_upfox = din("w_up_fox", [1024, D])
    w_out = din("w_out", [D, D])
    ple_proj = din("ple_proj", [256, D])
    ple_gw = din("ple_gate_w", [D, D])
    ple_ng = din("ple_norm_g", [1, D])
    fin_g = din("final_norm_g", [1, D])
    x_post = din("x_post", [T // 2, D])
    p_post = din("p_post", [T // 2, 256])
    role_in = din("role", [128, 2])
    out = nc.dram_tensor("out", [T // 2, D], F32, kind="ExternalOutput").ap()

    def dsc(name, shape, dt=BF16, ext=False):
        if ext:
            return nc.dram_tensor(name, shape, dt, kind="ExternalOutput").ap()
        return nc.dram_tensor(name, shape, dt).ap()
    wb_rw = dsc("wb_rw", [D, 8, 512])
    wb_lo = dsc("wb_lo", [D, 128])
    wb_fox = dsc("wb_fox", [D, 16, 256])
    wb_fl = dsc("wb_fl", [D, 16])
    wb_gate = dsc("wb_gate", [D, 4096])
    wb_uprw = dsc("wb_uprw", [1024, D])
    wb_upfox = dsc("wb_upfox", [1024, D])
    wb_out = dsc("wb_out", [D, D])
    wb_pp = dsc("wb_pp", [256, D])
    wb_pg = dsc("wb_pg", [D, D])
    wb_wl = dsc("wb_wl", [64, 1024])
    wb_al = dsc("wb_al", [64, 1024])
    yT_d = dsc("yT_d", [2048, T], BF16, ext=dbg)
    hT_d = dsc("hT_d", [T // 512, 128, 16 * 512])
    caug_d = dsc("caug_d", [16, 3, T])
    ncaug_d = dsc("ncaug_d", [16, 3, T])

    with ExitStack() as es:
        S = Sched(nc, es)

        def sb(name, shape, dt, st=es):
            return st.enter_context(nc.sbuf_tensor(name, shape, dt))

        cur = [None]

        def fsz(ap):
            n = 1
            for d_ in ap.shape[1:]:
                n *= int(d_)
            return n

        def OP(eng, fn, r, w, inc=True, cost=300.0):
            if cur[0] is None:
                S.op(eng, fn, r, w, inc)
            else:
                cur[0].append((0, eng, fn, tuple(r), tuple(w), inc, cost))

        def DM(q, fn, r, w):
            if cur[0] is None:
                S.dma(q, fn, r, w)
            else:
                cur[0].append((1, q, fn, tuple(r), tuple(w), True, 60.0))

        DMA_LAT = 3500.0
        XLAT = 400.0

        def run_threads(fns, stagger=0.0):
            import heapq
            lists = []
            for f in fns:
                cur[0] = []
                f()
                lists.append(cur[0])
                cur[0] = None
            th = []
            for L in lists:
                atoms, a = [], []
                for it in L:
                    a.append(it)
                    if it[5]:
                        atoms.append(a)
                        a = []
                assert not a
                n = len(atoms)
                eng = [at[0][1] for at in atoms]
                dur = [sum(it[6] for it in at) for at in atoms]
                isd = [at[0][0] == 1 for at in atoms]
                deps = [set() for _ in range(n)]
                state = {}
                for i, at in enumerate(atoms):
                    for it in at:
                        for k in it[3]:
                            st = state.get(k)
                            if st and st[0] is not None:
                                deps[i].add(st[0])
                        for k in it[4]:
                            st = state.get(k)
                            if st:
                                if st[0] is not None:
                                    deps[i].add(st[0])
                                deps[i].update(st[1])
                    for it in at:
                        for k in it[4]:
                            state[k] = [i, set()]
                        for k in it[3]:
                            state.setdefault(k, [None, set()])[1].add(i)
                    deps[i].discard(i)
                succ = [[] for _ in range(n)]
                for i in range(n):
                    for d_ in deps[i]:
                        succ[d_].append(i)
                th.append({'atoms': atoms, 'eng': eng, 'dur': dur, 'isd': isd, 'deps': deps, 'succ': succ,
                           'indeg': [len(deps[i]) for i in range(n)], 'fin': [0.0] * n, 'rdy': [0.0] * n})
            heaps = {}
            for ti, t_ in enumerate(th):
                for i in range(len(t_['atoms'])):
                    if t_['indeg'][i] == 0:
                        t_['rdy'][i] = ti * stagger
                        heapq.heappush(heaps.setdefault(t_['eng'][i], []), (t_['rdy'][i], i, ti))
            efree = {}
            order = []
            total = sum(len(t_['atoms']) for t_ in th)
            while total:
                best = None
                for e, hp in heaps.items():
                    if not hp:
                        continue
                    ef = efree.get(e, 0.0)
                    cand = hp[0]
                    if cand[0] <= ef:
                        k_ = min((c_ for c_ in hp if c_[0] <= ef), key=lambda c_: (c_[1], c_[2]))
                        st_ = ef
                    else:
                        k_ = cand
                        st_ = cand[0]
                    if best is None or st_ < best[0]:
                        best = (st_, e, k_)
                assert best is not None
                st_, e, k_ = best
                hp = heaps[e]
                hp.remove(k_)
                heapq.heapify(hp)
                _, i, ti = k_
                t_ = th[ti]
                efree[e] = st_ + t_['dur'][i]
                t_['fin'][i] = st_ + (DMA_LAT if t_['isd'][i] else t_['dur'][i])
                total -= 1
                order.append(t_['atoms'][i])
                for j in t_['succ'][i]:
                    lat = 60.0 if t_['eng'][j] == e else XLAT
                    if t_['fin'][i] + lat > t_['rdy'][j]:
                        t_['rdy'][j] = t_['fin'][i] + lat
                    t_['indeg'][j] -= 1
                    if t_['indeg'][j] == 0:
                        heapq.heappush(heaps.setdefault(t_['eng'][j], []), (t_['rdy'][j], j, ti))
            for at in order:
                for it in at:
                    if it[0] == 0:
                        S.op(it[1], it[2], it[3], it[4], it[5])
                    else:
                        S.dma(it[1], it[2], it[3], it[4])

        def ACT(out_, in_, func, r, w, **kw):
            OP('act', lambda e: e.activation(out=out_, in_=in_, func=func, **kw), r, w, cost=220.0 + fsz(out_) / 1.4)

        def TT(out_, a, b, op, r, w, eng='dve'):
            OP(eng, lambda e: e.tensor_tensor(out=out_, in0=a, in1=b, op=op), r, w,
               cost=(70.0 + fsz(out_) * 1.1) * (3.0 if eng == 'pool' else 1.0))

        def TS(out_, a, s1, s2, op0, op1, r, w, eng='dve'):
            OP(eng, lambda e: e.tensor_scalar(out=out_, in0=a, scalar1=s1, scalar2=s2, op0=op0, op1=op1), r, w,
               cost=70.0 + fsz(out_) * 0.75)

        def STT(out_, a, sc_, b, op0, op1, r, w):
            OP('dve', lambda e: e.scalar_tensor_tensor(out=out_, in0=a, scalar=sc_, in1=b, op0=op0, op1=op1), r, w,
               cost=70.0 + fsz(out_) * 1.1)

        def CP(out_, in_, r, w, eng='dve'):
            OP(eng, lambda e: e.tensor_copy(out=out_, in_=in_), r, w,
               cost=(70.0 + fsz(out_) * 1.0) * (3.0 if eng == 'pool' else 1.0))

        def MM(out_, lhsT, rhs, start, stop, r, w, inc=True):
            n_ = max(64, fsz(rhs))
            c_ = n_ / 1.2 * (4.0 if rhs.dtype == F32 else 1.0) + 20.0
            OP('pe', lambda e: e.matmul(out_, lhsT=lhsT, rhs=rhs, start=start, stop=stop), r, w, inc=inc, cost=c_)

        def TR(out_, in_, ident_, r, w, inc=True):
            OP('pe', lambda e: e.transpose(out_, in_, ident_), r, w, inc=inc, cost=70.0)

        def DMA(q, out_, in_, r, w, **kw):
            DM(q, lambda e: e.dma_start(out=out_, in_=in_, **kw), r, w)

        def RCP(out_, in_, r, w):
            OP('dve', lambda e: e.reciprocal(out=out_, in_=in_), r, w, cost=100.0 + fsz(out_) * 5.5)

        def SCAN(out_, d0, d1, init, r, w):
            OP('dve', lambda e: e.tensor_tensor_scan(out=out_, data0=d0, data1=d1, initial=init,
                                                    op0=ALU.mult, op1=ALU.add), r, w, cost=70.0 + fsz(out_) * 2.1)

        def MSET(ap, val, r, w, eng='pool'):
            OP(eng, lambda e: e.memset(ap, val), r, w, cost=100.0 + fsz(ap) * 0.5)

        B = [es.enter_context(nc.psum_tensor("B%d" % i, [128, 512], F32)) for i in range(8)]
        Bn = ["B%d" % i for i in range(8)]

        def b16(i):
            return B[i][:].bitcast(BF16)

        DMA('pool', wb_lo, w_in[:, 4096:4224], [], ['wb_lo'])
        DMA('pool', wb_fl, w_in[:, 8320:8336], [], ['wb_fl'])
        DMA('pool', wb_wl, rw_wl, [], ['wb_wl'])
        DMA('pool', wb_al, rw_al, [], ['wb_al'])
        identf = sb("identf", [128, 128], F32)
        ident = sb("ident", [128, 128], BF16)
        mask512 = sb("mask512", [128, 512], BF16)
        maskL = sb("maskL", [128, 4, 128], BF16)
        ident4 = sb("ident4", [128, 4, 128], BF16)
        maskneg = sb("maskneg", [128, 128], BF16)
        ones64 = sb("ones64", [64, 64], BF16)
        onesm = sb("onesm", [64, 64], BF16)
        onesf = sb("onesf", [128, 64], F32)
        rmask = sb("rmask", [128, 512], F32)
        onesbd = sb("onesbd", [128, 128], BF16)
        onesbdm = sb("onesbdm", [128, 128], BF16)
        cb = sb("cb", [128, 2], F32)
        mtmp = sb("mtmp", [128, 128], F32)

        def asel(tile_ap, cmp, base, cm, pat, fill, key):
            S.op('pool', lambda e: e.affine_select(out=tile_ap, in_=tile_ap, pattern=pat, compare_op=cmp,
                                                   fill=fill, base=base, channel_multiplier=cm), [key], [key])
        MSET(identf[:], 1.0, [], ['identf'])
        asel(identf[:], ALU.is_equal, 0, 1, [[-1, 128]], 0.0, 'identf')
        CP(ident[:], identf[:], ['identf'], ['ident'])
        for c in range(4):
            CP(ident4[:, c, :], identf[:], ['identf'], ['ident4'])
        MSET(mtmp[:], 1.0, [], ['mtmp'])
        asel(mtmp[:], ALU.is_gt, 0, -1, [[1, 128]], 0.0, 'mtmp')
        CP(mask512[:, 0:128], mtmp[:], ['mtmp'], ['mask512'])
        TS(mask512[:, 256:384], mtmp[:], -1.0, None, ALU.mult, ALU.bypass, ['mtmp'], ['mask512'])
        MSET(mtmp[:], 1.0, ['mtmp'], ['mtmp'])
        asel(mtmp[:], ALU.is_ge, 0, -1, [[1, 128]], 0.0, 'mtmp')
        CP(mask512[:, 128:256], mtmp[:], ['mtmp'], ['mask512'])
        CP(mask512[:, 384:512], mtmp[:], ['mtmp'], ['mask512'])
        TS(maskneg[:], mtmp[:], -1.0, -NEG, ALU.add, ALU.mult, ['mtmp'], ['maskneg'])
        MSET(mtmp[:], -1.0, ['mtmp'], ['mtmp'])
        asel(mtmp[:], ALU.is_gt, 0, 1, [[-1, 128]], 0.0, 'mtmp')
        for c in range(4):
            CP(maskL[:, c, :], mtmp[:], ['mtmp'], ['maskL'])
        MSET(ones64[:], 1.0, [], ['ones64'])
        MSET(onesm[:], 1.0 / 64.0, [], ['onesm'])
        MSET(onesf[:], 1.0, [], ['onesf'])
        MSET(onesbd[:], 0.0, [], ['onesbd'])
        MSET(onesbdm[:], 0.0, [], ['onesbdm'])
        for hh_ in range(2):
            hs_ = slice(hh_ * 64, (hh_ + 1) * 64)
            MSET(onesbd[hs_, hs_], 1.0, ['onesbd'], ['onesbd'])
            MSET(onesbdm[hs_, hs_], 1.0 / 64.0, ['onesbdm'], ['onesbdm'])
        MSET(cb[:, 0:1], 13.862943611198906, [], ['cb'])
        MSET(cb[:, 1:2], 64e-5, ['cb'], ['cb'])
        MSET(rmask[:], 1.0, [], ['rmask'])
        MSET(rmask[:].rearrange("p (c t) -> p c t", t=128)[:, :, 0:1], 0.0, ['rmask'], ['rmask'])

        for rg in range(16):
            rs = slice(rg * 128, (rg + 1) * 128)
            for g4 in range(4):
                DMA('pool', wb_rw[rs, :, g4 * 128:(g4 + 1) * 128],
                    w_in[rs, g4 * 1024:(g4 + 1) * 1024].rearrange("r (q c) -> r q c", c=128), [], ['wb_rw'])
        for rg in range(16):
            rs = slice(rg * 128, (rg + 1) * 128)
            for g4 in range(4):
                DMA('pool', wb_fox[rs, :, g4 * 64:(g4 + 1) * 64],
                    w_in[rs, 4224 + g4 * 1024:4224 + (g4 + 1) * 1024].rearrange("r (h c) -> r h c", c=64), [], ['wb_fox'])
        for rg in range(16):
            rs = slice(rg * 128, (rg + 1) * 128)
            DMA('pool', wb_gate[rs].rearrange("r (a c) -> r a c", c=1024),
                w_in[rs, 8336:12432].rearrange("r (a c) -> r a c", c=1024), [], ['wb_gate'])
        for (dst, src, nm, nr) in ((wb_uprw, w_uprw, 'wb_uprw', 1024), (wb_upfox, w_upfox, 'wb_upfox', 1024),
                                   (wb_out, w_out, 'wb_out', D), (wb_pp, ple_proj, 'wb_pp', 256),
                                   (wb_pg, ple_gw, 'wb_pg', D)):
            for r0 in range(0, nr, 256):
                DMA('pool', dst[r0:r0 + 256].rearrange("r (a c) -> r a c", c=1024),
                    src[r0:r0 + 256].rearrange("r (a c) -> r a c", c=1024), [], [nm])


        par2 = sb("par2", [128, 8, 8], F32)
        for i, src in enumerate((rw_w0, rw_a0, rw_kk, rw_ka, rw_ka, rw_rk, rw_lng, rw_lnb)):
            DMA('sp', par2[:, :, i:i + 1], src.rearrange("o (q p) -> p q o", p=128), [], ['par2'],
                allow_slow_non_contiguous=True)
        TS(par2[:, :, 0:2], par2[:, :, 0:2], -1.0, None, ALU.mult, ALU.bypass, ['par2'], ['par2'])
        TS(par2[:, :, 4:5], par2[:, :, 4:5], -1.0, 1.0, ALU.mult, ALU.add, ['par2'], ['par2'])
        mu4 = sb("mu4", [128, 4, 8, 2], F32)
        for g4 in range(4):
            DMA('sp', mu4[:, g4, :, 0:1], mu[:, g4 * 1024:(g4 + 1) * 1024].rearrange("o (q p) -> p q o", p=128), [], ['mu4'],
                allow_slow_non_contiguous=True)
        TS(mu4[:, :, :, 1:2], mu4[:, :, :, 0:1], -1.0, 1.0, ALU.mult, ALU.add, ['mu4'], ['mu4'])
        mu_lo = sb("mu_lo", [128, 2], F32)
        DMA('sp', mu_lo[:, 0:1], mu[:, 4096:4224].rearrange("o (p q) -> p (o q)", q=1), [], ['mu_lo'],
            allow_slow_non_contiguous=True)
        TS(mu_lo[:, 1:2], mu_lo[:, 0:1], -1.0, 1.0, ALU.mult, ALU.add, ['mu_lo'], ['mu_lo'])
        nbf = sb("nbf", [16, 1], F32)
        DMA('sp', nbf[:], fox_bf.rearrange("o h -> h o"), [], ['nbf'], allow_slow_non_contiguous=True)
        TS(nbf[:], nbf[:], -1.0, None, ALU.mult, ALU.bypass, ['nbf'], ['nbf'])

        hTt = [sb("hTt%d" % i, [128, 16, 512], BF16) for i in range(2)]
        esl = ExitStack()
        loraT = sb("loraT", [128, T], BF16, esl)

        def hTw(hb, t4, half):
            return ('hTt', hb, t4, half)

        def hTr(hb):
            return [('hTt', hb, t4, half) for t4 in range(4) for half in range(2)]

        def shift_evac(bank, tmp, ktmp, mucol, omucol, cap, kcar, first):
            ACT(tmp[:], B[bank][:], AF.Identity, [Bn[bank]], [ktmp], scale=omucol)
            STT(tmp[:, 1:512], B[bank][:, 0:511], mucol, tmp[:, 1:512], ALU.mult, ALU.add, [Bn[bank], ktmp], [ktmp])
            if not first:
                STT(tmp[:, 0:1], cap, mucol, tmp[:, 0:1], ALU.mult, ALU.add, [kcar, ktmp], [ktmp])
            CP(cap, B[bank][:, 511:512], [Bn[bank]], [kcar])

        with ExitStack() as esp:
            grep = sb("grep", [128, D], F32, esp)
            DMA('sp', grep[:], norm_g.partition_broadcast(128), [], ['grep'])
            xt = [sb("xt%d" % i, [128, D], F32, esp) for i in range(2)]
            xnb1 = [sb("xnb%d" % i, [128, D], BF16, esp) for i in range(2)]
            st1 = [sb("st%d" % i, [128, 4], F32, esp) for i in range(2)]
            wlo = sb("wlo", [128, 16, 128], BF16, esp)
            DMA('sp', wlo[:], wb_lo.rearrange("(k p) n -> p k n", p=128), ['wb_lo'], ['wlo'])
            wfl = sb("wfl", [128, 16, 16], BF16, esp)
            DMA('sp', wfl[:], wb_fl.rearrange("(k p) n -> p k n", p=128), ['wb_fl'], ['wfl'])
            tmpLO = sb("tmpLO", [128, 512], F32, esp)
            carlo = sb("carlo", [128, 1], F32, esp)
            cS = sb("cS", [16, 512], F32, esp)
            cprev = sb("cprev", [16, 1], F32, esp)
            ct = [sb("ct%d" % i, [16, 512], F32, esp) for i in range(2)]
            c3 = sb("c3", [16, 3, 512], BF16, esp)
            n3 = sb("n3", [16, 3, 512], BF16, esp)
            onesr = sb("onesr", [16, 512], F32, esp)
            MSET(onesr[:], 1.0, [], ['onesr'], eng='dve')
            MSET(cprev[:], 0.0, [], ['cprev'], eng='dve')
            for mt in range(NM):
                hb = mt % 2
                ts_ = slice(mt * 512, (mt + 1) * 512)
                for t4 in range(4):
                    tt = mt * 4 + t4
                    i = tt % 2
                    kx, kn, ks = 'xt%d' % i, 'xnb%d' % i, 'st%d' % i
                    DMA('sp', xt[i][:], x[tt * 128:(tt + 1) * 128, :], [], [kx])
                    ACT(xnb1[i][:], xt[i][:], AF.Square, [kx], [kn, ks], accum_out=st1[i][:, 0:1])
                    TS(st1[i][:, 1:2], st1[i][:, 0:1], 1.0 / D, 1e-6, ALU.mult, ALU.add, [ks], [ks])
                    ACT(st1[i][:, 2:3], st1[i][:, 1:2], AF.Sqrt, [ks], [ks])
                    RCP(st1[i][:, 3:4], st1[i][:, 2:3], [ks], [ks])
                    STT(xnb1[i][:], xt[i][:], st1[i][:, 3:4], grep[:], ALU.mult, ALU.mult, [kx, ks, 'grep'], [kn])
                    bi = 2 * i
                    for j in range(16):
                        bk = bi + j // 8
                        TR(b16(bk)[:, (j % 8) * 128:(j % 8 + 1) * 128], xnb1[i][:, j * 128:(j + 1) * 128], ident[:],
                           [kn, 'ident'], [Bn[bk]], inc=(j % 8 == 7))
                    ACT(hTt[hb][:, 0:8, t4 * 128:(t4 + 1) * 128], b16(bi)[:].rearrange("p (j t) -> p j t", t=128),
                        AF.Identity, [Bn[bi]], [hTw(hb, t4, 0)])
                    CP(hTt[hb][:, 8:16, t4 * 128:(t4 + 1) * 128], b16(bi + 1)[:].rearrange("p (j t) -> p j t", t=128),
                       [Bn[bi + 1]], [hTw(hb, t4, 1)])
                DMA('sp', hT_d[mt], hTt[hb][:].rearrange("p k t -> p (k t)"), hTr(hb), [('hT_d', mt)])
                for kc in range(16):
                    MM(B[4][:], wlo[:, kc, :], hTt[hb][:, kc, :], kc == 0, kc == 15, ['wlo'] + hTr(hb), [Bn[4]], inc=(kc == 15))
                shift_evac(4, tmpLO, 'tmpLO', mu_lo[:, 0:1], mu_lo[:, 1:2], carlo[:, 0:1], 'carlo', mt == 0)
                ACT(loraT[0:64, ts_], tmpLO[0:64, :], AF.Tanh, ['tmpLO'], [('loraT', mt)])
                CP(loraT[64:128, ts_], tmpLO[64:128, :], ['tmpLO'], [('loraT', mt)])
                for kc in range(16):
                    MM(B[5][0:16, :], wfl[:, kc, :], hTt[hb][:, kc, :], kc == 0, kc == 15, ['wfl'] + hTr(hb), [Bn[5]], inc=(kc == 15))
                ACT(ct[0][:], B[5][0:16, :], AF.Exp, [Bn[5], 'nbf'], ['ct0'], scale=-1.0, bias=nbf[:, 0:1])
                ACT(ct[0][:], ct[0][:], AF.Ln, ['ct0'], ['ct0'], bias=1.0)
                SCAN(cS[:], onesr[:], ct[0][:], cprev[:, 0:1], ['onesr', 'ct0', 'cprev'], ['cS'])
                CP(cprev[:], cS[:, 511:512], ['cS'], ['cprev'])
                CP(c3[:, 0, :], cS[:], ['cS'], ['c3'])
                TT(ct[0][:], cS[:], c3[:, 0, :], ALU.subtract, ['cS', 'c3'], ['ct0'])
                CP(c3[:, 1, :], ct[0][:], ['ct0'], ['c3'])
                TT(ct[1][:], ct[0][:], c3[:, 1, :], ALU.subtract, ['ct0', 'c3'], ['ct1'])
                CP(c3[:, 2, :], ct[1][:], ['ct1'], ['c3'])
                TS(n3[:], c3[:], -1.0, None, ALU.mult, ALU.bypass, ['c3'], ['n3'])
                DMA('sp', caug_d[:, :, ts_], c3[:], ['c3'], ['caug_d'])
                DMA('sp', ncaug_d[:, :, ts_], n3[:], ['n3'], ['ncaug_d'])
        S.barrier(skip_pool=True)

        if True:
            with ExitStack() as e2:
                lup = sb("lup", [128, 1024], BF16, e2)
                DMA('sp', lup[0:64, :], wb_wl, ['wb_wl'], ['lup'])
                DMA('sp', lup[64:128, :], wb_al, ['wb_al'], ['lup'])
                Whg = sb("Whg", [128, 1, 16, 512], BF16, e2)
                H32a = sb("H32a", [128, 8, 64], F32, e2)
                Hba = sb("Hba", [128, 8, 64], BF16, e2)
                cara = sb("cara", [128, 8, 4], F32, e2)
                MSET(H32a[:], 0.0, [], ['H32a'], eng='dve')
                MSET(Hba[:], 0.0, [], ['Hba'], eng='dve')
                ctxs = []
                for t in range(1):
                    c = {'bo': 4 * t, 't': t}
                    n_ = lambda s_: "%s_%d" % (s_, t)
                    c['tmp'] = [sb(n_("tmp%d" % i), [128, 512], F32, e2) for i in range(4)]
                    c['sc'] = [sb(n_("sc%d" % i), [128, 512], F32, e2) for i in range(7)]
                    c['bonus'] = sb(n_("bonus"), [128, 512], F32, e2)
                    c['gs'] = sb(n_("gs"), [128, 512], F32, e2)
                    c['pc'] = sb(n_("pc"), [128, 4], F32, e2)
                    c['AR'] = sb(n_("AR"), [128, 2, 512], BF16, e2)
                    c['KB'] = sb(n_("KB"), [128, 2, 512], BF16, e2)
                    c['KH'] = sb(n_("KH"), [128, 3, 512], BF16, e2)
                    c['sqb'] = sb(n_("sqb"), [128, 512], BF16, e2)
                    c['MS'] = [sb(n_("MS%d" % i), [128, 4, 512], BF16, e2) for i in range(2)]
                    c['Yk'] = [sb(n_("Yk%d" % i), [128, 4, 128], BF16, e2) for i in range(2)]
                    c['YTk'] = [sb(n_("YTk%d" % i), [128, 4, 128], BF16, e2) for i in range(2)]
                    c['G'] = [sb(n_("G%d" % i), [128, 4, 128], BF16, e2) for i in range(2)]
                    c['TOK'] = [sb(n_("TOK%d" % i), [128, 4, 3, 64], BF16, e2) for i in range(2)]
                    c['Wsb'] = sb(n_("Wsb"), [128, 64], BF16, e2)
                    c['Un'] = sb(n_("Un"), [128, 64], BF16, e2)
                    c['ycat'] = sb(n_("ycat"), [128, 2, 512], BF16, e2)
                    c['yout'] = [sb(n_("yout%d" % i), [128, 512], BF16, e2) for i in range(2)]
                    c['n'] = 0
                    ctxs.append(c)

                def sigmoid_(dst, kd, src, ks, extra_r=(), **kw):
                    ACT(dst, src, AF.Exp, [ks] + list(extra_r), [kd], **kw)
                    ACT(dst, dst, AF.Ln, [kd], [kd], bias=1.0)
                    ACT(dst, dst, AF.Exp, [kd], [kd], scale=-1.0)

                def rw_pair(q, mt, hb, c):
                    t = c['t']
                    o = c['bo']
                    K_ = lambda s_: "%s_%d" % (s_, t)
                    Bk = lambda i: B[o + i]
                    Kb = lambda i: Bn[o + i]
                    ql = 0
                    ts_ = slice(mt * 512, (mt + 1) * 512)
                    hk = hTr(hb)
                    tmp, sc, bonus, gs, pc = c['tmp'], c['sc'], c['bonus'], c['gs'], c['pc']
                    AR, KB, KH, sqb, Yk, YTk = c['AR'], c['KB'], c['KH'], c['sqb'], c['Yk'], c['YTk']
                    Wsb, Un, ycat, yout = c['Wsb'], c['Un'], c['ycat'], c['yout']
                    sk = [K_("sc%d" % i) for i in range(7)]
                    kt = [K_("tmp%d" % i) for i in range(4)]
                    kAR, kKB, kKH = K_('AR'), K_('KB'), K_('KH')
                    P = lambda i: par2[:, q, i:i + 1]
                    first = (mt == 0)
                    for g in range(4):
                        for kc in range(16):
                            MM(Bk(g)[:], Whg[:, ql, kc, g * 128:(g + 1) * 128], hTt[hb][:, kc, :], kc == 0, kc == 15,
                               ['Whg'] + hk, [Kb(g)], inc=(kc == 15))
                    for g in range(4):
                        shift_evac(o + g, tmp[g], kt[g], mu4[:, g, q, 0:1], mu4[:, g, q, 1:2], cara[:, q, g:g + 1], ('car', q, g), first)
                    r_, k_, v_, g_ = tmp[0][:], tmp[1][:], tmp[2][:], tmp[3][:]
                    MM(Bk(0)[:], lup[0:64, q * 128:(q + 1) * 128], loraT[0:64, ts_], True, True, ['lup', ('loraT', mt)], [Kb(0)])
                    MM(Bk(1)[:], lup[64:128, q * 128:(q + 1) * 128], loraT[64:128, ts_], True, True, ['lup', ('loraT', mt)], [Kb(1)])
                    sigmoid_(gs[:], K_('gs'), g_, kt[3], scale=-1.0)
                    TT(gs[:], gs[:], g_, ALU.mult, [K_('gs'), kt[3]], [K_('gs')])
                    sigmoid_(sc[1][:], sk[1], Bk(0)[:], Kb(0), ['par2'], scale=-1.0, bias=P(0))
                    sigmoid_(sc[2][:], sk[2], Bk(1)[:], Kb(1), ['par2'], scale=-1.0, bias=P(1))
                    SCAN(sc[3][:], rmask[:], sc[1][:], 0.0, ['rmask', sk[1]], [sk[3]])
                    TT(sc[4][:], sc[3][:], sc[1][:], ALU.subtract, [sk[3], sk[1]], [sk[4]])
                    ACT(sc[1][:], sc[3][:], AF.Exp, [sk[3]], [sk[1]], scale=-C0)
                    ACT(sc[5][:], sc[3][:], AF.Exp, [sk[3]], [sk[5]], scale=C0)
                    ACT(sc[4][:], sc[4][:], AF.Exp, [sk[4]], [sk[4]], scale=-C0)
                    CP(pc[:], sc[1][:].rearrange("p (c t) -> p c t", t=128)[:, :, 127], [sk[1]], [K_('pc')])
                    TS(sc[3][:], k_, P(2), None, ALU.mult, ALU.bypass, [kt[1], 'par2'], [sk[3]])
                    TT(sqb[:], sc[3][:], sc[3][:], ALU.mult, [sk[3]], [K_('sqb')])
                    MM(Bk(2)[:], onesbd[:], sqb[:], True, True, ['onesbd', K_('sqb')], [Kb(2)])
                    TS(sc[6][:], Bk(2)[:], 1e-24, None, ALU.max, ALU.bypass, [Kb(2)], [sk[6]])
                    ACT(sc[6][:], sc[6][:], AF.Ln, [sk[6]], [sk[6]], scale=float(2.0 ** 40))
                    ACT(sc[6][:], sc[6][:], AF.Exp, [sk[6], 'cb'], [sk[6]], scale=-0.5, bias=cb[:, 0:1])
                    TT(sc[3][:], sc[3][:], sc[6][:], ALU.mult, [sk[3], sk[6]], [sk[3]])
                    TS(sc[6][:], sc[2][:], P(3), P(4), ALU.mult, ALU.add, [sk[2], 'par2'], [sk[6]])
                    TT(sc[6][:], sc[6][:], k_, ALU.mult, [sk[6], kt[1]], [sk[6]])
                    TT(sc[2][:], sc[3][:], sc[2][:], ALU.mult, [sk[3], sk[2]], [sk[2]])
                    TT(AR[:, 0, :], sc[3][:], sc[4][:], ALU.mult, [sk[3], sk[4]], [kAR])
                    TT(AR[:, 1, :], r_, sc[1][:], ALU.mult, [kt[0], sk[1]], [kAR])
                    TT(sc[3][:], r_, sc[6][:], ALU.mult, [kt[0], sk[6]], [sk[3]])
                    ACT(sqb[:], sc[3][:], AF.Identity, [sk[3], 'par2'], [K_('sqb')], scale=P(5))
                    MM(Bk(3)[:], onesbd[:], sqb[:], True, True, ['onesbd', K_('sqb')], [Kb(3)])
                    TT(bonus[:], Bk(3)[:], v_, ALU.mult, [Kb(3), kt[2]], [K_('bonus')])
                    TT(sc[4][:], sc[6][:], sc[5][:], ALU.mult, [sk[6], sk[5]], [sk[4]])
                    TT(sc[1][:], sc[2][:], sc[5][:], ALU.mult, [sk[2], sk[5], K_('pc')], [sk[1]])
                    ACT(KB[:, 0, :], sc[4][:], AF.Identity, [sk[4]], [kKB])
                    ACT(KB[:, 1, :], sc[1][:], AF.Identity, [sk[1]], [kKB])
                    pcb = pc[:].unsqueeze(2).to_broadcast([128, 4, 128])
                    TT(KH[:, 0, :].rearrange("p (c t) -> p c t", t=128), sc[4][:].rearrange("p (c t) -> p c t", t=128), pcb,
                       ALU.mult, [sk[4], K_('pc')], [kKH])
                    TT(KH[:, 1, :].rearrange("p (c t) -> p c t", t=128), sc[1][:].rearrange("p (c t) -> p c t", t=128), pcb,
                       ALU.mult, [sk[1], K_('pc')], [kKH])
                    CP(KH[:, 2, :], v_, [kt[2]], [kKH])
                    for hh in range(2):
                        hs = slice(hh * 64, (hh + 1) * 64)
                        pt = b16(o + hh)[:, 0:768].rearrange("p (c g k) -> p c g k", c=4, g=3)
                        for cc in range(4):
                            for g3 in range(3):
                                TR(pt[:, cc, g3, :], KH[hs, g3, cc * 128:(cc + 1) * 128], ident[hs, hs], [kKH, 'ident'], [Kb(hh)],
                                   inc=(cc == 3 and g3 == 2))
                        CP(c['TOK'][hh][:], pt, [Kb(hh)], [K_('TOK%d' % hh)])
                    for hh in range(2):
                        h = 2 * q + hh
                        hs = slice(hh * 64, (hh + 1) * 64)
                        MS, G, TOK = c['MS'][hh], c['G'][hh], c['TOK'][hh]
                        kMSn, kG, kTOK = K_('MS%d' % hh), K_('G%d' % hh), K_('TOK%d' % hh)
                        kH32, kHb = ('H32', h), ('Hb', h)
                        H32 = H32a[hs, q, :]
                        Hb = Hba[hs, q, :]
                        for cc in range(4):
                            cs = slice(cc * 128, (cc + 1) * 128)
                            MM(Bk(2)[:, 0:256].rearrange('p (a t) -> p a t', a=2), KB[hs, 0, cs], AR[hs, :, cs], True, True, [kKB, kAR], [Kb(2)], inc=False)
                            MM(Bk(2)[:, 256:512].rearrange('p (a t) -> p a t', a=2), KB[hs, 1, cs], AR[hs, :, cs], True, True, [kKB, kAR], [Kb(2)])
                            MM(Bk(1)[:, cs], AR[hs, 0, cs], KB[hs, 1, cs], True, True, [kKB, kAR], [Kb(1)])
                            TT(MS[:, cc, :], Bk(2)[:], mask512[:], ALU.mult, [Kb(2), 'mask512'], [(kMSn, cc)])
                        mskeys = [(kMSn, cc) for cc in range(4)]
                        X0 = MS[:, :, 256:384]
                        TT(YTk[0][:], Bk(1)[:].rearrange("p (c t) -> p c t", t=128), maskL[:], ALU.mult, [Kb(1), 'maskL'], [K_('YTk0')])
                        CP(Yk[0][:], X0, mskeys, [K_('Yk0')])
                        TT(G[:], X0, ident4[:], ALU.add, mskeys + ['ident4'], [kG])
                        for k in range(1, 7):
                            a_, b_ = (k - 1) % 2, k % 2
                            kYa, kYTa, kYb, kYTb = K_('Yk%d' % a_), K_('YTk%d' % a_), K_('Yk%d' % b_), K_('YTk%d' % b_)
                            if k <= 5:
                                for cc in range(4):
                                    MM(Bk(0)[:, cc * 128:(cc + 1) * 128], YTk[a_][:, cc, :], Yk[a_][:, cc, :], True, True, [kYa, kYTa], [Kb(0)], inc=(cc == 3))
                            for cc in range(4):
                                MM(Bk(1)[:, cc * 128:(cc + 1) * 128], Yk[a_][:, cc, :], YTk[a_][:, cc, :], True, True, [kYa, kYTa], [Kb(1)], inc=(cc == 3))
                            if k <= 5:
                                ACT(Yk[b_][:], Bk(0)[:].rearrange("p (c t) -> p c t", t=128), AF.Identity, [Kb(0)], [kYb])
                            CP(YTk[b_][:], Bk(1)[:].rearrange("p (c t) -> p c t", t=128), [Kb(1)], [kYTb])
                            for cc in range(4):
                                MM(Bk(2)[:, cc * 128:(cc + 1) * 128], YTk[b_][:, cc, :], G[:, cc, :], True, True, [kYTb, kG], [Kb(2)], inc=(cc == 3))
                            TT(G[:], Bk(2)[:].rearrange("p (c t) -> p c t", t=128), G[:], ALU.add, [Kb(2), kG], [kG])
                        for cc in range(4):
                            cs = slice(cc * 128, (cc + 1) * 128)
                            kMS = (kMSn, cc)
                            MM(Bk(3)[:, 0:64], AR[hs, 0, cs], Hb, True, False, [kAR, kHb], [Kb(3)], inc=False)
                            MM(Bk(3)[:, 0:64], MS[:, cc, 0:128], TOK[:, cc, 2, :], False, True, [kMS, kTOK], [Kb(3)])
                            ACT(Wsb[:], Bk(3)[:, 0:64], AF.Identity, [Kb(3)], [K_('Wsb')])
                            MM(Bk(3)[:, 64:128], G[:, cc, :], Wsb[:], True, True, [kG, K_('Wsb')], [Kb(3)])
                            ACT(Un[:], Bk(3)[:, 64:128], AF.Identity, [Kb(3)], [K_('Un')], scale=-1.0)
                            MM(Bk(4)[hs, cs], Hb, AR[hs, 1, cs], True, False, [kHb, kAR], [Kb(4)], inc=False)
                            MM(Bk(4)[hs, cs], TOK[:, cc, 2, :], MS[:, cc, 128:256], False, False, [kTOK, kMS], [Kb(4)], inc=False)
                            MM(Bk(4)[hs, cs], Un[:], MS[:, cc, 384:512], False, True, [K_('Un'), kMS], [Kb(4)])
                            MM(Bk(3)[hs, 128:192], TOK[:, cc, 0, :], TOK[:, cc, 2, :], True, False, [kTOK], [Kb(3)], inc=False)
                            MM(Bk(3)[hs, 128:192], TOK[:, cc, 1, :], Un[:], False, True, [kTOK, K_('Un')], [Kb(3)])
                            STT(H32, H32, pc[hs, cc:cc + 1], Bk(3)[hs, 128:192], ALU.mult, ALU.add, [kH32, K_('pc'), Kb(3)], [kH32])
                            ACT(Hb, H32, AF.Identity, [kH32], [kHb])
                    ACT(ycat[:, 0, :], Bk(4)[:], AF.Identity, [Kb(4)], [K_('ycat')])
                    ACT(ycat[:, 1, :], Bk(4)[:], AF.Square, [Kb(4)], [K_('ycat')])
                    MM(Bk(1)[:], onesbdm[:], ycat[:, 0, :], True, True, ['onesbdm', K_('ycat')], [Kb(1)])
                    MM(Bk(2)[:], onesbdm[:], ycat[:, 1, :], True, True, ['onesbdm', K_('ycat')], [Kb(2)])
                    ACT(sc[1][:], Bk(1)[:], AF.Identity, [Kb(1)], [sk[1]])
                    TT(sc[2][:], sc[1][:], sc[1][:], ALU.mult, [sk[1]], [sk[2]])
                    TT(sc[2][:], Bk(2)[:], sc[2][:], ALU.subtract, [Kb(2), sk[2]], [sk[2]])
                    ACT(sc[2][:], sc[2][:], AF.Ln, [sk[2], 'cb'], [sk[2]], bias=cb[:, 1:2])
                    ACT(sc[2][:], sc[2][:], AF.Exp, [sk[2]], [sk[2]], scale=-0.5)
                    TT(sc[3][:], Bk(4)[:], sc[1][:], ALU.subtract, [Kb(4), sk[1]], [sk[3]])
                    TT(sc[3][:], sc[3][:], sc[2][:], ALU.mult, [sk[3], sk[2]], [sk[3]])
                    ACT(sc[3][:], sc[3][:], AF.Identity, [sk[3], 'par2'], [sk[3]], scale=P(6), bias=P(7))
                    TT(sc[3][:], sc[3][:], bonus[:], ALU.add, [sk[3], K_('bonus')], [sk[3]])
                    yi = c['n'] % 2
                    c['n'] += 1
                    TT(yout[yi][:], sc[3][:], gs[:], ALU.mult, [sk[3], K_('gs')], [K_('yout%d' % yi)])
                    DMA('sp', yT_d[q * 128:(q + 1) * 128, ts_], yout[yi][:], [K_('yout%d' % yi)], [('yT_d', mt)])

                Wfg = sb("Wfg", [128, 2, 16, 256], BF16, e2)
                kTa = [sb("kTa%d" % i, [70, T], BF16, e2) for i in range(2)]
                Va = [sb("Va%d" % i, [128, T // 128, 65], BF16, e2) for i in range(2)]
                for i in range(2):
                    MSET(Va[i][:], 1.0, [], [('Va', i)])
                    MSET(kTa[i][64:70, :], 1.0, [], [('kTa', i)])
                fctx = []
                for t in range(1, 2):
                    c = {'bo': 4 * t, 't': t, 'n': 0}
                    n_ = lambda s_: "%s_f%d" % (s_, t)
                    c['qTa'] = [sb(n_("qTa%d" % i), [70, 512], BF16, e2) for i in range(2)]
                    for i in range(2):
                        MSET(c['qTa'][i][64:70, :], 1.0, [], [n_("qTa%d" % i)])
                    c['vb'] = sb(n_("vb"), [64, 512], BF16, e2)
                    c['gsf'] = sb(n_("gsf"), [64, 512], F32, e2)
                    c['PTt'] = [sb(n_("PTt%d" % i), [128, 512], BF16, e2) for i in range(2)]
                    c['rec'] = sb(n_("rec"), [65, 512], F32, e2)
                    c['fo'] = sb(n_("fo"), [64, 512], F32, e2)
                    c['yf'] = [sb(n_("yf%d" % i), [64, 512], BF16, e2) for i in range(2)]
                    fctx.append(c)

                def fox_unit(h, I, hb, c):
                    t = c['t']
                    o = c['bo']
                    K_ = lambda s_: "%s_f%d" % (s_, t)
                    fb = {0: 5, 1: 5, 2: 6, 3: 7}
                    Bk = lambda i: B[fb[i]]
                    Kb = lambda i: Bn[fb[i]]
                    hl = h % 2
                    ts_ = slice(I * 512, (I + 1) * 512)
                    hk = hTr(hb)
                    u = c['n']
                    c['n'] += 1
                    qT = c['qTa'][u % 2]
                    kq = K_("qTa%d" % (u % 2))
                    kT, kkT = kTa[hl], ('kTa', hl)
                    V_, kV = Va[hl], ('Va', hl)
                    vb, gsf, PTt, rec, fo, yf = c['vb'], c['gsf'], c['PTt'], c['rec'], c['fo'], c['yf']
                    for kc in range(16):
                        MM(Bk(0)[:], Wfg[:, hl, kc, 0:128], hTt[hb][:, kc, :], kc == 0, kc == 15, ['Wfg'] + hk, [Kb(0)], inc=(kc == 15))
                    ACT(qT[0:64, :], Bk(0)[0:64, :], AF.Identity, [Kb(0)], [kq], scale=0.125)
                    DMA('sp', qT[64:67, :], ncaug_d[h, :, ts_], ['ncaug_d'], [kq])
                    CP(kT[0:64, ts_], Bk(0)[64:128, :], [Kb(0)], [kkT])
                    for kc in range(16):
                        MM(Bk(1)[:], Wfg[:, hl, kc, 128:256], hTt[hb][:, kc, :], kc == 0, kc == 15, ['Wfg'] + hk, [Kb(1)], inc=(kc == 15))
                    CP(vb[:], Bk(1)[0:64, :], [Kb(1)], [K_('vb')])
                    CP(gsf[:], Bk(1)[64:128, :], [Kb(1)], [K_('gsf')])
                    ACT(fo[:], gsf[:], AF.Exp, [K_('gsf')], [K_('fo')], scale=-1.0)
                    ACT(fo[:], fo[:], AF.Ln, [K_('fo')], [K_('fo')], bias=1.0)
                    ACT(fo[:], fo[:], AF.Exp, [K_('fo')], [K_('fo')], scale=-1.0)
                    TT(gsf[:], gsf[:], fo[:], ALU.mult, [K_('gsf'), K_('fo')], [K_('gsf')], eng='dve')
                    pt = b16(5)[:, 0:256].rearrange("p (c k) -> p c k", c=4)
                    for cc in range(4):
                        TR(pt[:, cc, :], vb[:, cc * 128:(cc + 1) * 128], ident[0:64, 0:64], [K_('vb'), 'ident'], [Kb(0)], inc=(cc == 3))
                    CP(V_[:, I * 4:(I + 1) * 4, 0:64], pt, [Kb(0)], [kV])
                    nj = 4 * I + 4

                    def qk(j):
                        lo = max(0, j - 4 * I) * 128
                        bk = 2 + (j % 2)
                        diag = j >= 4 * I
                        MM(Bk(bk)[:, lo:512], kT[0:70, j * 128:(j + 1) * 128], qT[0:70, lo:512], True, not diag, [kkT, kq], [Kb(bk)], inc=not diag)
                        if diag:
                            MM(Bk(bk)[:, lo:lo + 128], ident[:], maskneg[:], False, True, ['ident', 'maskneg'], [Kb(bk)])
                    qk(0)
                    for j in range(nj):
                        lo = max(0, j - 4 * I) * 128
                        bk = 2 + (j % 2)
                        pi = j % 2
                        if j + 1 < nj:
                            qk(j + 1)
                        ACT(PTt[pi][:, lo:512], Bk(bk)[:, lo:512], AF.Exp, [Kb(bk)], [K_('PTt%d' % pi)])
                        MM(Bk(1)[0:65, lo:512], V_[:, j, :], PTt[pi][:, lo:512], j == 0, j == nj - 1, [kV, K_('PTt%d' % pi)], [Kb(1)], inc=(j == nj - 1))
                    ACT(rec[64:65, :], Bk(1)[64:65, :], AF.Ln, [Kb(1)], [K_('rec')])
                    ACT(rec[64:65, :], rec[64:65, :], AF.Exp, [K_('rec')], [K_('rec')], scale=-1.0)
                    MM(B[6][0:64, :], onesf[64:65, :], rec[64:65, :], True, True, ['onesf', K_('rec')], [Bn[6]])
                    TT(fo[:], Bk(1)[0:64, :], gsf[:], ALU.mult, [Kb(1), K_('gsf')], [K_('fo')])
                    yi = u % 2
                    TT(yf[yi][:], fo[:], B[6][0:64, :], ALU.mult, [K_('fo'), Bn[6]], [K_('yf%d' % yi)])
                    DMA('sp', yT_d[1024 + h * 64:1024 + (h + 1) * 64, ts_], yf[yi][:], [K_('yf%d' % yi)], [('yT_d', I)])


                for g8 in range(8):
                    DMA('sp', Whg[:, 0], wb_rw[:, g8, :].rearrange("(k p) n -> p k n", p=128), ['wb_rw'], ['Whg'])
                    for hl_ in range(2):
                        DMA('sp', Wfg[:, hl_], wb_fox[:, g8 * 2 + hl_, :].rearrange("(k p) n -> p k n", p=128), ['wb_fox'], ['Wfg'])
                        DMA('sp', kTa[hl_][67:70, :], caug_d[g8 * 2 + hl_], ['caug_d'], [('kTa', hl_)])
                    DMA('sp', hTt[0][:].rearrange("p k t -> p (k t)"), hT_d[0], [('hT_d', 0)], hTr(0))
                    for mt in range(NM):
                        hb = mt % 2
                        if mt + 1 < NM:
                            DMA('sp', hTt[1 - hb][:].rearrange("p k t -> p (k t)"), hT_d[mt + 1], [('hT_d', mt + 1)], hTr(1 - hb))

                        def thA(mt=mt, hb=hb, g8=g8):
                            rw_pair(g8, mt, hb, ctxs[0])

                        def thB(mt=mt, hb=hb, g8=g8):
                            for hh in range(2):
                                fox_unit(g8 * 2 + hh, mt, hb, fctx[0])
                        run_threads([thA, thB], stagger=0.0)
            S.barrier()
        esl.close()

        if do_post:
            with ExitStack() as e4:
                gp = sb("gp", [128, D], F32, e4)
                gf = sb("gf", [128, D], F32, e4)
                gn = sb("gn", [128, D], F32, e4)
                role = sb("role_sb", [128, 2], F32, e4)
                DMA('sp', gn[:], norm_g.partition_broadcast(128), [], ['gn'])
                DMA('sp', role[:], role_in, [], ['role'])
                DMA('sp', gp[:], ple_ng.partition_broadcast(128), [], ['gp'])
                DMA('sp', gf[:], fin_g.partition_broadcast(128), [], ['gf'])
                x1 = sb("x1", [128, 4, D], F32, e4)
                xnb = sb("xnb4", [128, D], BF16, e4)
                st = sb("st4", [128, 4], F32, e4)
                yTt = sb("yTt", [128, 16, 512], BF16, e4)
                mT = sb("mT", [128, 16, 512], BF16, e4)
                pT = sb("pT", [128, 2, 512], BF16, e4)
                pt32 = sb("pt32", [128, 256], F32, e4)
                ptb = sb("ptb", [128, 256], BF16, e4)
                wsm = [sb("wsm%d" % i, [128, 16, 128], BF16, e4) for i in range(5)]
                wbg = [sb("wbg%d" % i, [128, 16, 512], BF16, e4) for i in range(2)]
                wpp = sb("wpp", [128, 2, 512], BF16, e4)
                sg1 = [sb("sg1_%d" % i, [128, 512], F32, e4) for i in range(2)]
                sg2 = [sb("sg2_%d" % i, [128, 512], F32, e4) for i in range(2)]
                ot = sb("ot0", [128, D], F32, e4)
                nsm = [0]
                nbg = [0]

                def rms_rstd(src_ap, ksrc):
                    ACT(xnb[:], src_ap, AF.Square, [ksrc], ['xnb4', 'st4'], accum_out=st[:, 0:1])
                    TS(st[:, 1:2], st[:, 0:1], 1.0 / D, 1e-6, ALU.mult, ALU.add, ['st4'], ['st4'])
                    ACT(st[:, 2:3], st[:, 1:2], AF.Sqrt, ['st4'], ['st4'])
                    RCP(st[:, 3:4], st[:, 2:3], ['st4'], ['st4'])

                def norm_T(src_ap, ksrc, gtile, kg, dstT, kdst, s_):
                    rms_rstd(src_ap, ksrc)
                    STT(xnb[:], src_ap, st[:, 3:4], gtile[:], ALU.mult, ALU.mult, [ksrc, 'st4', kg], ['xnb4'])
                    for j in range(16):
                        bk = j // 8
                        TR(b16(bk)[:, (j % 8) * 128:(j % 8 + 1) * 128], xnb[:, j * 128:(j + 1) * 128], ident[:],
                           ['xnb4', 'ident'], [Bn[bk]], inc=(j % 8 == 7))
                    ACT(dstT[:, 0:8, s_ * 128:(s_ + 1) * 128], b16(0)[:].rearrange("p (j t) -> p j t", t=128),
                        AF.Identity, [Bn[0]], kdst[0:8])
                    CP(dstT[:, 8:16, s_ * 128:(s_ + 1) * 128], b16(1)[:].rearrange("p (j t) -> p j t", t=128),
                       [Bn[1]], kdst[8:16])

                NH = NM // 2
                for TT_ in range(NH):
                    ts_ = slice(TT_ * 512, (TT_ + 1) * 512)
                    tsB = slice((TT_ + NH) * 512, (TT_ + NH + 1) * 512)
                    hb = 0
                    hT4 = hTt[0]
                    yB = hTt[1]
                    DMA('sp', yTt[:], yT_d[:, ts_].rearrange("(k p) t -> p k t", p=128), [('yT_d', TT_)], ['yTt'])
                    DMA('sp', yB[:], yT_d[:, tsB].rearrange("(k p) t -> p k t", p=128), [('yT_d', TT_ + NH)], hTr(1))
                    yf_ = yTt[:].rearrange("p k t -> p (k t)")
                    TS(yf_, yf_, role[:, 0:1], None, ALU.mult, ALU.bypass, ['yTt', 'role'], ['yTt'])
                    STT(yf_, yB[:].rearrange("p k t -> p (k t)"), role[:, 1:2], yf_, ALU.mult, ALU.add, hTr(1) + ['yTt', 'role'], ['yTt'])
                    for s_ in range(4):
                        tok = slice(TT_ * 512 + s_ * 128, TT_ * 512 + (s_ + 1) * 128)
                        DMA('sp', x1[:, s_, :], x_post[tok, :], [], [('x1', s_)])
                        norm_T(x1[:, s_, :], ('x1', s_), gn, 'gn', hT4, [hTw(0, s_, 0)] * 8 + [hTw(0, s_, 1)] * 8, s_)
                        DMA('sp', pt32[:], p_post[tok, :], [], ['pt32'])
                        CP(ptb[:], pt32[:], ['pt32'], ['ptb'])
                        for j in range(2):
                            TR(b16(2)[:, j * 128:(j + 1) * 128], ptb[:, j * 128:(j + 1) * 128], ident[:], ['ptb', 'ident'],
                               [Bn[2]], inc=(j == 1))
                        CP(pT[:, :, s_ * 128:(s_ + 1) * 128], b16(2)[:, 0:256].rearrange("p (j t) -> p j t", t=128),
                           [Bn[2]], ['pT'])
                    for oc in range(16):
                        ocs = slice(oc * 128, (oc + 1) * 128)
                        par_ = oc % 2
                        w = []
                        for (src, nk, kk_) in ((wb_uprw[:, ocs], 8, 'wb_uprw'), (wb_upfox[:, ocs], 8, 'wb_upfox'),
                                               (wb_gate[:, ocs], 16, 'wb_gate'),
                                               (wb_gate[:, 2048 + oc * 128:2048 + (oc + 1) * 128], 16, 'wb_gate')):
                            wi = nsm[0] % 5
                            nsm[0] += 1
                            DMA('sp', wsm[wi][:, 0:nk, :], src.rearrange("(k p) n -> p k n", p=128), [kk_], ['wsm%d' % wi])
                            w.append(wi)
                        bb = 4 * par_
                        for kc in range(8):
                            MM(B[bb][:], wsm[w[0]][:, kc, :], yTt[:, kc, :], kc == 0, kc == 7, ['wsm%d' % w[0], 'yTt'], [Bn[bb]], inc=(kc == 7))
                        for kc in range(8):
                            MM(B[bb + 1][:], wsm[w[1]][:, kc, :], yTt[:, 8 + kc, :], kc == 0, kc == 7, ['wsm%d' % w[1], 'yTt'], [Bn[bb + 1]], inc=(kc == 7))
                        for kc in range(16):
                            MM(B[bb + 2][:], wsm[w[2]][:, kc, :], hT4[:, kc, :], kc == 0, kc == 15, ['wsm%d' % w[2]] + hTr(hb), [Bn[bb + 2]], inc=(kc == 15))
                        for kc in range(16):
                            MM(B[bb + 3][:], wsm[w[3]][:, kc, :], hT4[:, kc, :], kc == 0, kc == 15, ['wsm%d' % w[3]] + hTr(hb), [Bn[bb + 3]], inc=(kc == 15))
                        k1, k2 = 'sg1_%d' % par_, 'sg2_%d' % par_
                        ACT(sg1[par_][:], B[bb + 2][:], AF.Sigmoid, [Bn[bb + 2]], [k1])
                        ACT(sg2[par_][:], B[bb + 3][:], AF.Sigmoid, [Bn[bb + 3]], [k2])
                        TT(sg1[par_][:], B[bb][:], sg1[par_][:], ALU.mult, [Bn[bb], k1], [k1])
                        TT(sg2[par_][:], B[bb + 1][:], sg2[par_][:], ALU.mult, [Bn[bb + 1], k2], [k2])
                        TT(mT[:, oc, :], sg1[par_][:], sg2[par_][:], ALU.add, [k1, k2], [('mT', oc)], eng='pool')
                    mTk = [('mT', oc) for oc in range(16)]
                    for cg in range(4):
                        cgs = slice(cg * 512, (cg + 1) * 512)
                        wi = nbg[0] % 2
                        nbg[0] += 1
                        DMA('sp', wbg[wi][:], wb_out[:, cgs].rearrange("(k p) n -> p k n", p=128), ['wb_out'], ['wbg%d' % wi])
                        for s_ in range(4):
                            bk = s_ % 4
                            for kc in range(16):
                                MM(B[bk][:], mT[:, kc, s_ * 128:(s_ + 1) * 128], wbg[wi][:, kc, :], kc == 0, kc == 15,
                                   mTk + ['wbg%d' % wi], [Bn[bk]], inc=(kc == 15))
                            TT(x1[:, s_, cgs], x1[:, s_, cgs], B[bk][:], ALU.add, [('x1', s_), Bn[bk]], [('x1', s_)])
                    for s_ in range(4):
                        norm_T(x1[:, s_, :], ('x1', s_), gp, 'gp', mT, mTk, s_)
                    for cg in range(4):
                        cgs = slice(cg * 512, (cg + 1) * 512)
                        wi = nbg[0] % 2
                        nbg[0] += 1
                        DMA('sp', wbg[wi][:], wb_pg[:, cgs].rearrange("(k p) n -> p k n", p=128), ['wb_pg'], ['wbg%d' % wi])
                        DMA('sp', wpp[:], wb_pp[:, cgs].rearrange("(k p) n -> p k n", p=128), ['wb_pp'], ['wpp'])
                        for s_ in range(4):
                            bk = 2 + (s_ % 2)
                            bp = 4 + (s_ % 2)
                            ks_ = 'sg1_%d' % (s_ % 2)
                            for kc in range(16):
                                MM(B[bk][:], mT[:, kc, s_ * 128:(s_ + 1) * 128], wbg[wi][:, kc, :], kc == 0, kc == 15,
                                   mTk + ['wbg%d' % wi], [Bn[bk]], inc=(kc == 15))
                            for kc in range(2):
                                MM(B[bp][:], pT[:, kc, s_ * 128:(s_ + 1) * 128], wpp[:, kc, :], kc == 0, kc == 1,
                                   ['pT', 'wpp'], [Bn[bp]], inc=(kc == 1))
                            ACT(sg1[s_ % 2][:], B[bk][:], AF.Sigmoid, [Bn[bk]], [ks_])
                            TT(sg1[s_ % 2][:], B[bp][:], sg1[s_ % 2][:], ALU.mult, [Bn[bp], ks_], [ks_])
                            TT(x1[:, s_, cgs], x1[:, s_, cgs], sg1[s_ % 2][:], ALU.add, [('x1', s_), ks_], [('x1', s_)], eng='pool')
                    for s_ in range(4):
                        tok = slice(TT_ * 512 + s_ * 128, TT_ * 512 + (s_ + 1) * 128)
                        rms_rstd(x1[:, s_, :], ('x1', s_))
                        STT(ot[:], x1[:, s_, :], st[:, 3:4], gf[:], ALU.mult, ALU.mult, [('x1', s_), 'st4', 'gf'], ['ot0'])
                        DMA('sp', out[tok, :], ot[:], ['ot0'], ['out'])
        S.barrier(final=True)
        with nc.Block() as block:
            S.emit(block)
    return nc

_CACHE = {}


def _prep(inputs, b, T, r=0):
    f = lambda a: np.ascontiguousarray(np.asarray(a, dtype=np.float32))
    H = T // 2
    role = np.zeros((128, 2), np.float32)
    role[:, r] = 1.0
    m = {
        "x": f(inputs["x"][b, :T]),
        "p": f(inputs["p"][0, b, :T]),
        "x_post": f(inputs["x"][b, r * H:(r + 1) * H]),
        "p_post": f(inputs["p"][0, b, r * H:(r + 1) * H]),
        "role": role,
        "norm_g": f(inputs["norm_g"][0:1]),
        "w_in": f(inputs["w_in"][0]),
        "rw_shift_mu": f(inputs["rw_shift_mu"][0:1]),
        "rw_w0": f(inputs["rw_w0"][0:1]),
        "rw_w_lora_up": f(inputs["rw_w_lora_up"][0]),
        "rw_a0": f(inputs["rw_a0"][0:1]),
        "rw_a_lora_up": f(inputs["rw_a_lora_up"][0]),
        "rw_k_k": f(inputs["rw_k_k"][0:1]),
        "rw_k_a": f(inputs["rw_k_a"][0:1]),
        "rw_r_k": f(np.asarray(inputs["rw_r_k"][0]).reshape(1, 1024)),
        "rw_ln_g": f(inputs["rw_ln_g"][0:1]),
        "rw_ln_b": f(inputs["rw_ln_b"][0:1]),
        "fox_b_f": f(inputs["fox_b_f"][0:1]),
        "w_up_rwkv": f(inputs["w_up_rwkv"][0]),
        "w_up_fox": f(inputs["w_up_fox"][0]),
        "w_out": f(inputs["w_out"][0]),
        "ple_proj": f(inputs["ple_proj"][0]),
        "ple_gate_w": f(inputs["ple_gate_w"][0]),
        "ple_norm_g": f(inputs["ple_norm_g"][0:1]),
        "final_norm_g": f(np.asarray(inputs["final_norm_g"]).reshape(1, D)),
    }
    return m


def kernel(**inputs):
    T = 4096
    if T not in _CACHE:
        _CACHE[T] = build(T)
    nc = _CACHE[T]
    in_maps = [_prep(inputs, c % 4, T, c // 4) for c in range(8)]
    res = run_bass_kernel_spmd(nc, in_maps, core_ids=list(range(8)))
    H = T // 2
    out = np.empty((4, T, D), np.float32)
    for c in range(8):
        out[c % 4, (c // 4) * H:(c // 4 + 1) * H] = np.asarray(res.results[c]["out"], dtype=np.float32)
    return out
```

```python
import numpy as np
import concourse.bass as bass
import concourse.mybir as mybir
from concourse.bass_utils import run_bass_kernel_spmd
from contextlib import ExitStack

F32 = mybir.dt.float32
BF16 = mybir.dt.bfloat16
AF = mybir.ActivationFunctionType
ALU = mybir.AluOpType

ENGS = ('pe', 'act', 'dve', 'pool', 'sp')
NDSEM = 8
D = 2048
NIN = 12432
C0 = 0.6065306597126334
NEG = -30000.0


class Sched:
    def __init__(self, nc, es):
        self.nc = nc
        self.q = {e: [] for e in ENGS}
        self.cnt = {e: 0 for e in ENGS}
        self.seen = {e: {} for e in ENGS}
        self.state = {}
        self.dma_n = {e: 0 for e in ENGS}
        self.sem = {}
        for e in ('pe', 'act', 'dve', 'pool'):
            self.sem[e] = es.enter_context(nc.semaphore('s_' + e))
        for e in ('sp', 'pool', 'act'):
            for i in range(NDSEM):
                self.sem['d%s%d' % (e, i)] = es.enter_context(nc.semaphore('sd_%s%d' % (e, i)))

    def _deps(self, eng, reads, writes):
        deps = {}

        def need(sv):
            if sv is None:
                return
            s, v = sv
            if eng == 'pe' and s == 'pe':
                return
            if deps.get(s, 0) < v:
                deps[s] = v
        for k in reads:
            st = self.state.get(k)
            if st:
                need(st[0])
        for k in writes:
            st = self.state.get(k)
            if st:
                need(st[0])
                for s, v in st[1].items():
                    need((s, v))
        out = []
        for s, v in deps.items():
            if self.seen[eng].get(s, 0) < v:
                self.seen[eng][s] = v
                out.append((s, v))
        return out

    def _commit(self, token, reads, writes):
        for k in writes:
            self.state[k] = [token, {}]
        s, v = token
        for k in reads:
            st = self.state.setdefault(k, [None, {}])
            if st[1].get(s, 0) < v:
                st[1][s] = v

    def op(self, eng, fn, reads=(), writes=(), inc=True):
        waits = self._deps(eng, reads, writes)
        if inc:
            self.cnt[eng] += 1
            token = (eng, self.cnt[eng])
        else:
            token = (eng, self.cnt[eng] + 1)
        self.q[eng].append((fn, waits, (eng, 1) if inc else None))
        self._commit(token, reads, writes)

    def dma(self, q, fn, reads=(), writes=()):
        n = self.dma_n[q]
        self.dma_n[q] += 1
        slot, gen = n % NDSEM, n // NDSEM
        sn = 'd%s%d' % (q, slot)
        waits = self._deps(q, reads, writes)
        if gen > 0 and self.seen[q].get(sn, 0) < 16 * gen:
            self.seen[q][sn] = 16 * gen
            waits.append((sn, 16 * gen))
        token = (sn, 16 * (gen + 1))
        self.q[q].append((fn, waits, (sn, 16)))
        self._commit(token, reads, writes)

    def _allvals(self):
        vals = {e: self.cnt[e] for e in ('pe', 'act', 'dve', 'pool')}
        for q in ('sp', 'pool', 'act'):
            n = self.dma_n[q]
            for slot in range(min(n, NDSEM)):
                gens = (n - slot + NDSEM - 1) // NDSEM
                vals['d%s%d' % (q, slot)] = 16 * gens
        return vals

    def barrier(self, skip_pool=False, final=False):
        vals = self._allvals()
        if skip_pool:
            vals = {k: v for k, v in vals.items() if not k.startswith('dpool')}
        for e in ENGS:
            waits = []
            for s, v in vals.items():
                if e == 'pe' and s == 'pe':
                    continue
                if v > 0 and self.seen[e].get(s, 0) < v:
                    self.seen[e][s] = v
                    waits.append((s, v))
            self.q[e].append((None, waits, None))

    def emit(self, block):
        sem = self.sem

        def run(name, e):
            for fn, waits, inc in self.q[name]:
                for s, v in waits:
                    e.wait_ge(sem[s], v)
                if fn is None:
                    continue
                ins = fn(e)
                if inc is not None:
                    ins.then_inc(sem[inc[0]], inc[1])

        @block.tensor
        def _(e):
            run('pe', e)

        @block.scalar
        def _(e):
            run('act', e)

        @block.vector
        def _(e):
            run('dve', e)

        @block.gpsimd
        def _(e):
            run('pool', e)

        @block.sync
        def _(e):
            run('sp', e)


def build(T, dbg=False, do_rw=True, do_fox=True, do_post=True):
    nc = bass.Bass("TRN2", target_bir_lowering=False)
    NT = T // 128
    NM = T // 512

    def din(name, shape):
        return nc.dram_tensor(name, shape, F32, kind="ExternalInput").ap()
    x = din("x", [T, D])
    p = din("p", [T, 256])
    norm_g = din("norm_g", [1, D])
    w_in = din("w_in", [D, NIN])
    mu = din("rw_shift_mu", [1, 4224])
    rw_w0 = din("rw_w0", [1, 1024])
    rw_wl = din("rw_w_lora_up", [64, 1024])
    rw_a0 = din("rw_a0", [1, 1024])
    rw_al = din("rw_a_lora_up", [64, 1024])
    rw_kk = din("rw_k_k", [1, 1024])
    rw_ka = din("rw_k_a", [1, 1024])
    rw_rk = din("rw_r_k", [1, 1024])
    rw_lng = din("rw_ln_g", [1, 1024])
    rw_lnb = din("rw_ln_b", [1, 1024])
    fox_bf = din("fox_b_f", [1, 16])
    w_uprw = din("w_up_rwkv", [1024, D])
    w_upfox = din("w_up_fox", [1024, D])
    w_out = din("w_out", [D, D])
    ple_proj = din("ple_proj", [256, D])
    ple_gw = din("ple_gate_w", [D, D])
    ple_ng = din("ple_norm_g", [1, D])
    fin_g = din("final_norm_g", [1, D])
    x_post = din("x_post", [T // 2, D])
    p_post = din("p_post", [T // 2, 256])
    role_in = din("role", [128, 2])
    out = nc.dram_tensor("out", [T // 2, D], F32, kind="ExternalOutput").ap()

    def dsc(name, shape, dt=BF16, ext=False):
        if ext:
            return nc.dram_tensor(name, shape, dt, kind="ExternalOutput").ap()
        return nc.dram_tensor(name, shape, dt).ap()
    wb_rw = dsc("wb_rw", [D, 8, 512])
    wb_lo = dsc("wb_lo", [D, 128])
    wb_fox = dsc("wb_fox", [D, 16, 256])
    wb_fl = dsc("wb_fl", [D, 16])
    wb_gate = dsc("wb_gate", [D, 4096])
    wb_uprw = dsc("wb_uprw", [1024, D])
    wb_upfox = dsc("wb_upfox", [1024, D])
    wb_out = dsc("wb_out", [D, D])
    wb_pp = dsc("wb_pp", [256, D])
    wb_pg = dsc("wb_pg", [D, D])
    wb_wl = dsc("wb_wl", [64, 1024])
    wb_al = dsc("wb_al", [64, 1024])
    yT_d = dsc("yT_d", [2048, T], BF16, ext=dbg)
    hT_d = dsc("hT_d", [T // 512, 128, 16 * 512])
    caug_d = dsc("caug_d", [16, 3, T])
    ncaug_d = dsc("ncaug_d", [16, 3, T])

    with ExitStack() as es:
        S = Sched(nc, es)

        def sb(name, shape, dt, st=es):
            return st.enter_context(nc.sbuf_tensor(name, shape, dt))

        cur = [None]

        def fsz(ap):
            n = 1
            for d_ in ap.shape[1:]:
                n *= int(d_)
            return n

        def OP(eng, fn, r, w, inc=True, cost=300.0):
            if cur[0] is None:
                S.op(eng, fn, r, w, inc)
            else:
                cur[0].append((0, eng, fn, tuple(r), tuple(w), inc, cost))

        def DM(q, fn, r, w):
            if cur[0] is None:
                S.dma(q, fn, r, w)
            else:
                cur[0].append((1, q, fn, tuple(r), tuple(w), True, 60.0))

        DMA_LAT = 3500.0
        XLAT = 400.0

        def run_threads(fns, stagger=0.0):
            import heapq
            lists = []
            for f in fns:
                cur[0] = []
                f()
                lists.append(cur[0])
                cur[0] = None
            th = []
            for L in lists:
                atoms, a = [], []
                for it in L:
                    a.append(it)
                    if it[5]:
                        atoms.append(a)
                        a = []
                assert not a
                n = len(atoms)
                eng = [at[0][1] for at in atoms]
                dur = [sum(it[6] for it in at) for at in atoms]
                isd = [at[0][0] == 1 for at in atoms]
                deps = [set() for _ in range(n)]
                state = {}
                for i, at in enumerate(atoms):
                    for it in at:
                        for k in it[3]:
                            st = state.get(k)
                            if st and st[0] is not None:
                                deps[i].add(st[0])
                        for k in it[4]:
                            st = state.get(k)
                            if st:
                                if st[0] is not None:
                                    deps[i].add(st[0])
                                deps[i].update(st[1])
                    for it in at:
                        for k in it[4]:
                            state[k] = [i, set()]
                        for k in it[3]:
                            state.setdefault(k, [None, set()])[1].add(i)
                    deps[i].discard(i)
                succ = [[] for _ in range(n)]
                for i in range(n):
                    for d_ in deps[i]:
                        succ[d_].append(i)
                th.append({'atoms': atoms, 'eng': eng, 'dur': dur, 'isd': isd, 'deps': deps, 'succ': succ,
                           'indeg': [len(deps[i]) for i in range(n)], 'fin': [0.0] * n, 'rdy': [0.0] * n})
            heaps = {}
            for ti, t_ in enumerate(th):
                for i in range(len(t_['atoms'])):
                    if t_['indeg'][i] == 0:
                        t_['rdy'][i] = ti * stagger
                        heapq.heappush(heaps.setdefault(t_['eng'][i], []), (t_['rdy'][i], i, ti))
            efree = {}
            order = []
            total = sum(len(t_['atoms']) for t_ in th)
            while total:
                best = None
                for e, hp in heaps.items():
                    if not hp:
                        continue
                    ef = efree.get(e, 0.0)
                    cand = hp[0]
                    if cand[0] <= ef:
                        k_ = min((c_ for c_ in hp if c_[0] <= ef), key=lambda c_: (c_[1], c_[2]))
                        st_ = ef
                    else:
                        k_ = cand
                        st_ = cand[0]
                    if best is None or st_ < best[0]:
                        best = (st_, e, k_)
                assert best is not None
                st_, e, k_ = best
                hp = heaps[e]
                hp.remove(k_)
                heapq.heapify(hp)
                _, i, ti = k_
                t_ = th[ti]
                efree[e] = st_ + t_['dur'][i]
                t_['fin'][i] = st_ + (DMA_LAT if t_['isd'][i] else t_['dur'][i])
                total -= 1
                order.append(t_['atoms'][i])
                for j in t_['succ'][i]:
                    lat = 60.0 if t_['eng'][j] == e else XLAT
                    if t_['fin'][i] + lat > t_['rdy'][j]:
                        t_['rdy'][j] = t_['fin'][i] + lat
                    t_['indeg'][j] -= 1
                    if t_['indeg'][j] == 0:
                        heapq.heappush(heaps.setdefault(t_['eng'][j], []), (t_['rdy'][j], j, ti))
            for at in order:
                for it in at:
                    if it[0] == 0:
                        S.op(it[1], it[2], it[3], it[4], it[5])
                    else:
                        S.dma(it[1], it[2], it[3], it[4])

        def ACT(out_, in_, func, r, w, **kw):
            OP('act', lambda e: e.activation(out=out_, in_=in_, func=func, **kw), r, w, cost=220.0 + fsz(out_) / 1.4)

        def TT(out_, a, b, op, r, w, eng='dve'):
            OP(eng, lambda e: e.tensor_tensor(out=out_, in0=a, in1=b, op=op), r, w,
               cost=(70.0 + fsz(out_) * 1.1) * (3.0 if eng == 'pool' else 1.0))

        def TS(out_, a, s1, s2, op0, op1, r, w, eng='dve'):
            OP(eng, lambda e: e.tensor_scalar(out=out_, in0=a, scalar1=s1, scalar2=s2, op0=op0, op1=op1), r, w,
               cost=70.0 + fsz(out_) * 0.75)

        def STT(out_, a, sc_, b, op0, op1, r, w):
            OP('dve', lambda e: e.scalar_tensor_tensor(out=out_, in0=a, scalar=sc_, in1=b, op0=op0, op1=op1), r, w,
               cost=70.0 + fsz(out_) * 1.1)

        def CP(out_, in_, r, w, eng='dve'):
            OP(eng, lambda e: e.tensor_copy(out=out_, in_=in_), r, w,
               cost=(70.0 + fsz(out_) * 1.0) * (3.0 if eng == 'pool' else 1.0))

        def MM(out_, lhsT, rhs, start, stop, r, w, inc=True):
            n_ = max(64, fsz(rhs))
            c_ = n_ / 1.2 * (4.0 if rhs.dtype == F32 else 1.0) + 20.0
            OP('pe', lambda e: e.matmul(out_, lhsT=lhsT, rhs=rhs, start=start, stop=stop), r, w, inc=inc, cost=c_)

        def TR(out_, in_, ident_, r, w, inc=True):
            OP('pe', lambda e: e.transpose(out_, in_, ident_), r, w, inc=inc, cost=70.0)

        def DMA(q, out_, in_, r, w, **kw):
            DM(q, lambda e: e.dma_start(out=out_, in_=in_, **kw), r, w)

        def RCP(out_, in_, r, w):
            OP('dve', lambda e: e.reciprocal(out=out_, in_=in_), r, w, cost=100.0 + fsz(out_) * 5.5)

        def SCAN(out_, d0, d1, init, r, w):
            OP('dve', lambda e: e.tensor_tensor_scan(out=out_, data0=d0, data1=d1, initial=init,
                                                    op0=ALU.mult, op1=ALU.add), r, w, cost=70.0 + fsz(out_) * 2.1)

        def MSET(ap, val, r, w, eng='pool'):
            OP(eng, lambda e: e.memset(ap, val), r, w, cost=100.0 + fsz(ap) * 0.5)

        B = [es.enter_context(nc.psum_tensor("B%d" % i, [128, 512], F32)) for i in range(8)]
        Bn = ["B%d" % i for i in range(8)]

        def b16(i):
            return B[i][:].bitcast(BF16)

        DMA('pool', wb_lo, w_in[:, 4096:4224], [], ['wb_lo'])
        DMA('pool', wb_fl, w_in[:, 8320:8336], [], ['wb_fl'])
        DMA('pool', wb_wl, rw_wl, [], ['wb_wl'])
        DMA('pool', wb_al, rw_al, [], ['wb_al'])
        identf = sb("identf", [128, 128], F32)
        ident = sb("ident", [128, 128], BF16)
        mask512 = sb("mask512", [128, 512], BF16)
        maskL = sb("maskL", [128, 4, 128], BF16)
        ident4 = sb("ident4", [128, 4, 128], BF16)
        maskneg = sb("maskneg", [128, 128], BF16)
        ones64 = sb("ones64", [64, 64], BF16)
        onesm = sb("onesm", [64, 64], BF16)
        onesf = sb("onesf", [128, 64], F32)
        rmask = sb("rmask", [128, 512], F32)
        onesbd = sb("onesbd", [128, 128], BF16)
        onesbdm = sb("onesbdm", [128, 128], BF16)
        cb = sb("cb", [128, 2], F32)
        mtmp = sb("mtmp", [128, 128], F32)

        def asel(tile_ap, cmp, base, cm, pat, fill, key):
            S.op('pool', lambda e: e.affine_select(out=tile_ap, in_=tile_ap, pattern=pat, compare_op=cmp,
                                                   fill=fill, base=base, channel_multiplier=cm), [key], [key])
        MSET(identf[:], 1.0, [], ['identf'])
        asel(identf[:], ALU.is_equal, 0, 1, [[-1, 128]], 0.0, 'identf')
        CP(ident[:], identf[:], ['identf'], ['ident'])
        for c in range(4):
            CP(ident4[:, c, :], identf[:], ['identf'], ['ident4'])
        MSET(mtmp[:], 1.0, [], ['mtmp'])
        asel(mtmp[:], ALU.is_gt, 0, -1, [[1, 128]], 0.0, 'mtmp')
        CP(mask512[:, 0:128], mtmp[:], ['mtmp'], ['mask512'])
        TS(mask512[:, 256:384], mtmp[:], -1.0, None, ALU.mult, ALU.bypass, ['mtmp'], ['mask512'])
        MSET(mtmp[:], 1.0, ['mtmp'], ['mtmp'])
        asel(mtmp[:], ALU.is_ge, 0, -1, [[1, 128]], 0.0, 'mtmp')
        CP(mask512[:, 128:256], mtmp[:], ['mtmp'], ['mask512'])
        CP(mask512[:, 384:512], mtmp[:], ['mtmp'], ['mask512'])
        TS(maskneg[:], mtmp[:], -1.0, -NEG, ALU.add, ALU.mult, ['mtmp'], ['maskneg'])
        MSET(mtmp[:], -1.0, ['mtmp'], ['mtmp'])
        asel(mtmp[:], ALU.is_gt, 0, 1, [[-1, 128]], 0.0, 'mtmp')
        for c in range(4):
            CP(maskL[:, c, :], mtmp[:], ['mtmp'], ['maskL'])
        MSET(ones64[:], 1.0, [], ['ones64'])
        MSET(onesm[:], 1.0 / 64.0, [], ['onesm'])
        MSET(onesf[:], 1.0, [], ['onesf'])
        MSET(onesbd[:], 0.0, [], ['onesbd'])
        MSET(onesbdm[:], 0.0, [], ['onesbdm'])
        for hh_ in range(2):
            hs_ = slice(hh_ * 64, (hh_ + 1) * 64)
            MSET(onesbd[hs_, hs_], 1.0, ['onesbd'], ['onesbd'])
            MSET(onesbdm[hs_, hs_], 1.0 / 64.0, ['onesbdm'], ['onesbdm'])
        MSET(cb[:, 0:1], 13.862943611198906, [], ['cb'])
        MSET(cb[:, 1:2], 64e-5, ['cb'], ['cb'])
        MSET(rmask[:], 1.0, [], ['rmask'])
        MSET(rmask[:].rearrange("p (c t) -> p c t", t=128)[:, :, 0:1], 0.0, ['rmask'], ['rmask'])

        for rg in range(16):
            rs = slice(rg * 128, (rg + 1) * 128)
            for g4 in range(4):
                DMA('pool', wb_rw[rs, :, g4 * 128:(g4 + 1) * 128],
                    w_in[rs, g4 * 1024:(g4 + 1) * 1024].rearrange("r (q c) -> r q c", c=128), [], ['wb_rw'])
        for rg in range(16):
            rs = slice(rg * 128, (rg + 1) * 128)
            for g4 in range(4):
                DMA('pool', wb_fox[rs, :, g4 * 64:(g4 + 1) * 64],
                    w_in[rs, 4224 + g4 * 1024:4224 + (g4 + 1) * 1024].rearrange("r (h c) -> r h c", c=64), [], ['wb_fox'])
        for rg in range(16):
            rs = slice(rg * 128, (rg + 1) * 128)
            DMA('pool', wb_gate[rs].rearrange("r (a c) -> r a c", c=1024),
                w_in[rs, 8336:12432].rearrange("r (a c) -> r a c", c=1024), [], ['wb_gate'])
        for (dst, src, nm, nr) in ((wb_uprw, w_uprw, 'wb_uprw', 1024), (wb_upfox, w_upfox, 'wb_upfox', 1024),
                                   (wb_out, w_out, 'wb_out', D), (wb_pp, ple_proj, 'wb_pp', 256),
                                   (wb_pg, ple_gw, 'wb_pg', D)):
            for r0 in range(0, nr, 256):
                DMA('pool', dst[r0:r0 + 256].rearrange("r (a c) -> r a c", c=1024),
                    src[r0:r0 + 256].rearrange("r (a c) -> r a c", c=1024), [], [nm])


        par2 = sb("par2", [128, 8, 8], F32)
        for i, src in enumerate((rw_w0, rw_a0, rw_kk, rw_ka, rw_ka, rw_rk, rw_lng, rw_lnb)):
            DMA('sp', par2[:, :, i:i + 1], src.rearrange("o (q p) -> p q o", p=128), [], ['par2'],
                allow_slow_non_contiguous=True)
        TS(par2[:, :, 0:2], par2[:, :, 0:2], -1.0, None, ALU.mult, ALU.bypass, ['par2'], ['par2'])
        TS(par2[:, :, 4:5], par2[:, :, 4:5], -1.0, 1.0, ALU.mult, ALU.add, ['par2'], ['par2'])
        mu4 = sb("mu4", [128, 4, 8, 2], F32)
        for g4 in range(4):
            DMA('sp', mu4[:, g4, :, 0:1], mu[:, g4 * 1024:(g4 + 1) * 1024].rearrange("o (q p) -> p q o", p=128), [], ['mu4'],
                allow_slow_non_contiguous=True)
        TS(mu4[:, :, :, 1:2], mu4[:, :, :, 0:1], -1.0, 1.0, ALU.mult, ALU.add, ['mu4'], ['mu4'])
        mu_lo = sb("mu_lo", [128, 2], F32)
        DMA('sp', mu_lo[:, 0:1], mu[:, 4096:4224].rearrange("o (p q) -> p (o q)", q=1), [], ['mu_lo'],
            allow_slow_non_contiguous=True)
        TS(mu_lo[:, 1:2], mu_lo[:, 0:1], -1.0, 1.0, ALU.mult, ALU.add, ['mu_lo'], ['mu_lo'])
        nbf = sb("nbf", [16, 1], F32)
        DMA('sp', nbf[:], fox_bf.rearrange("o h -> h o"), [], ['nbf'], allow_slow_non_contiguous=True)
        TS(nbf[:], nbf[:], -1.0, None, ALU.mult, ALU.bypass, ['nbf'], ['nbf'])

        hTt = [sb("hTt%d" % i, [128, 16, 512], BF16) for i in range(2)]
        esl = ExitStack()
        loraT = sb("loraT", [128, T], BF16, esl)

        def hTw(hb, t4, half):
            return ('hTt', hb, t4, half)

        def hTr(hb):
            return [('hTt', hb, t4, half) for t4 in range(4) for half in range(2)]

        def shift_evac(bank, tmp, ktmp, mucol, omucol, cap, kcar, first):
            ACT(tmp[:], B[bank][:], AF.Identity, [Bn[bank]], [ktmp], scale=omucol)
            STT(tmp[:, 1:512], B[bank][:, 0:511], mucol, tmp[:, 1:512], ALU.mult, ALU.add, [Bn[bank], ktmp], [ktmp])
            if not first:
                STT(tmp[:, 0:1], cap, mucol, tmp[:, 0:1], ALU.mult, ALU.add, [kcar, ktmp], [ktmp])
            CP(cap, B[bank][:, 511:512], [Bn[bank]], [kcar])

        with ExitStack() as esp:
            grep = sb("grep", [128, D], F32, esp)
            DMA('sp', grep[:], norm_g.partition_broadcast(128), [], ['grep'])
            xt = [sb("xt%d" % i, [128, D], F32, esp) for i in range(2)]
            xnb1 = [sb("xnb%d" % i, [128, D], BF16, esp) for i in range(2)]
            st1 = [sb("st%d" % i, [128, 4], F32, esp) for i in range(2)]
            wlo = sb("wlo", [128, 16, 128], BF16, esp)
            DMA('sp', wlo[:], wb_lo.rearrange("(k p) n -> p k n", p=128), ['wb_lo'], ['wlo'])
            wfl = sb("wfl", [128, 16, 16], BF16, esp)
            DMA('sp', wfl[:], wb_fl.rearrange("(k p) n -> p k n", p=128), ['wb_fl'], ['wfl'])
            tmpLO = sb("tmpLO", [128, 512], F32, esp)
            carlo = sb("carlo", [128, 1], F32, esp)
            cS = sb("cS", [16, 512], F32, esp)
            cprev = sb("cprev", [16, 1], F32, esp)
            ct = [sb("ct%d" % i, [16, 512], F32, esp) for i in range(2)]
            c3 = sb("c3", [16, 3, 512], BF16, esp)
            n3 = sb("n3", [16, 3, 512], BF16, esp)
            onesr = sb("onesr", [16, 512], F32, esp)
            MSET(onesr[:], 1.0, [], ['onesr'], eng='dve')
            MSET(cprev[:], 0.0, [], ['cprev'], eng='dve')
            for mt in range(NM):
                hb = mt % 2
                ts_ = slice(mt * 512, (mt + 1) * 512)
                for t4 in range(4):
                    tt = mt * 4 + t4
                    i = tt % 2
                    kx, kn, ks = 'xt%d' % i, 'xnb%d' % i, 'st%d' % i
                    DMA('sp', xt[i][:], x[tt * 128:(tt + 1) * 128, :], [], [kx])
                    ACT(xnb1[i][:], xt[i][:], AF.Square, [kx], [kn, ks], accum_out=st1[i][:, 0:1])
                    TS(st1[i][:, 1:2], st1[i][:, 0:1], 1.0 / D, 1e-6, ALU.mult, ALU.add, [ks], [ks])
                    ACT(st1[i][:, 2:3], st1[i][:, 1:2], AF.Sqrt, [ks], [ks])
                    RCP(st1[i][:, 3:4], st1[i][:, 2:3], [ks], [ks])
                    STT(xnb1[i][:], xt[i][:], st1[i][:, 3:4], grep[:], ALU.mult, ALU.mult, [kx, ks, 'grep'], [kn])
                    bi = 2 * i
                    for j in range(16):
                        bk = bi + j // 8
                        TR(b16(bk)[:, (j % 8) * 128:(j % 8 + 1) * 128], xnb1[i][:, j * 128:(j + 1) * 128], ident[:],
                           [kn, 'ident'], [Bn[bk]], inc=(j % 8 == 7))
                    ACT(hTt[hb][:, 0:8, t4 * 128:(t4 + 1) * 128], b16(bi)[:].rearrange("p (j t) -> p j t", t=128),
                        AF.Identity, [Bn[bi]], [hTw(hb, t4, 0)])
                    CP(hTt[hb][:, 8:16, t4 * 128:(t4 + 1) * 128], b16(bi + 1)[:].rearrange("p (j t) -> p j t", t=128),
                       [Bn[bi + 1]], [hTw(hb, t4, 1)])
                DMA('sp', hT_d[mt], hTt[hb][:].rearrange("p k t -> p (k t)"), hTr(hb), [('hT_d', mt)])
                for kc in range(16):
                    MM(B[4][:], wlo[:, kc, :], hTt[hb][:, kc, :], kc == 0, kc == 15, ['wlo'] + hTr(hb), [Bn[4]], inc=(kc == 15))
                shift_evac(4, tmpLO, 'tmpLO', mu_lo[:, 0:1], mu_lo[:, 1:2], carlo[:, 0:1], 'carlo', mt == 0)
                ACT(loraT[0:64, ts_], tmpLO[0:64, :], AF.Tanh, ['tmpLO'], [('loraT', mt)])
                CP(loraT[64:128, ts_], tmpLO[64:128, :], ['tmpLO'], [('loraT', mt)])
                for kc in range(16):
                    MM(B[5][0:16, :], wfl[:, kc, :], hTt[hb][:, kc, :], kc == 0, kc == 15, ['wfl'] + hTr(hb), [Bn[5]], inc=(kc == 15))
                ACT(ct[0][:], B[5][0:16, :], AF.Exp, [Bn[5], 'nbf'], ['ct0'], scale=-1.0, bias=nbf[:, 0:1])
                ACT(ct[0][:], ct[0][:], AF.Ln, ['ct0'], ['ct0'], bias=1.0)
                SCAN(cS[:], onesr[:], ct[0][:], cprev[:, 0:1], ['onesr', 'ct0', 'cprev'], ['cS'])
                CP(cprev[:], cS[:, 511:512], ['cS'], ['cprev'])
                CP(c3[:, 0, :], cS[:], ['cS'], ['c3'])
                TT(ct[0][:], cS[:], c3[:, 0, :], ALU.subtract, ['cS', 'c3'], ['ct0'])
                CP(c3[:, 1, :], ct[0][:], ['ct0'], ['c3'])
                TT(ct[1][:], ct[0][:], c3[:, 1, :], ALU.subtract, ['ct0', 'c3'], ['ct1'])
                CP(c3[:, 2, :], ct[1][:], ['ct1'], ['c3'])
                TS(n3[:], c3[:], -1.0, None, ALU.mult, ALU.bypass, ['c3'], ['n3'])
                DMA('sp', caug_d[:, :, ts_], c3[:], ['c3'], ['caug_d'])
                DMA('sp', ncaug_d[:, :, ts_], n3[:], ['n3'], ['ncaug_d'])
        S.barrier(skip_pool=True)

        if True:
            with ExitStack() as e2:
                lup = sb("lup", [128, 1024], BF16, e2)
                DMA('sp', lup[0:64, :], wb_wl, ['wb_wl'], ['lup'])
                DMA('sp', lup[64:128, :], wb_al, ['wb_al'], ['lup'])
                Whg = sb("Whg", [128, 1, 16, 512], BF16, e2)
                H32a = sb("H32a", [128, 8, 64], F32, e2)
                Hba = sb("Hba", [128, 8, 64], BF16, e2)
                cara = sb("cara", [128, 8, 4], F32, e2)
                MSET(H32a[:], 0.0, [], ['H32a'], eng='dve')
                MSET(Hba[:], 0.0, [], ['Hba'], eng='dve')
                ctxs = []
                for t in range(1):
                    c = {'bo': 4 * t, 't': t}
                    n_ = lambda s_: "%s_%d" % (s_, t)
                    c['tmp'] = [sb(n_("tmp%d" % i), [128, 512], F32, e2) for i in range(4)]
                    c['sc'] = [sb(n_("sc%d" % i), [128, 512], F32, e2) for i in range(7)]
                    c['bonus'] = sb(n_("bonus"), [128, 512], F32, e2)
                    c['gs'] = sb(n_("gs"), [128, 512], F32, e2)
                    c['pc'] = sb(n_("pc"), [128, 4], F32, e2)
                    c['AR'] = sb(n_("AR"), [128, 2, 512], BF16, e2)
                    c['KB'] = sb(n_("KB"), [128, 2, 512], BF16, e2)
                    c['KH'] = sb(n_("KH"), [128, 3, 512], BF16, e2)
                    c['sqb'] = sb(n_("sqb"), [128, 512], BF16, e2)
                    c['MS'] = [sb(n_("MS%d" % i), [128, 4, 512], BF16, e2) for i in range(2)]
                    c['Yk'] = [sb(n_("Yk%d" % i), [128, 4, 128], BF16, e2) for i in range(2)]
                    c['YTk'] = [sb(n_("YTk%d" % i), [128, 4, 128], BF16, e2) for i in range(2)]
                    c['G'] = [sb(n_("G%d" % i), [128, 4, 128], BF16, e2) for i in range(2)]
                    c['TOK'] = [sb(n_("TOK%d" % i), [128, 4, 3, 64], BF16, e2) for i in range(2)]
                    c['Wsb'] = sb(n_("Wsb"), [128, 64], BF16, e2)
                    c['Un'] = sb(n_("Un"), [128, 64], BF16, e2)
                    c['ycat'] = sb(n_("ycat"), [128, 2, 512], BF16, e2)
                    c['yout'] = [sb(n_("yout%d" % i), [128, 512], BF16, e2) for i in range(2)]
                    c['n'] = 0
                    ctxs.append(c)

                def sigmoid_(dst, kd, src, ks, extra_r=(), **kw):
                    ACT(dst, src, AF.Exp, [ks] + list(extra_r), [kd], **kw)
                    ACT(dst, dst, AF.Ln, [kd], [kd], bias=1.0)
                    ACT(dst, dst, AF.Exp, [kd], [kd], scale=-1.0)

                def rw_pair(q, mt, hb, c):
                    t = c['t']
                    o = c['bo']
                    K_ = lambda s_: "%s_%d" % (s_, t)
                    Bk = lambda i: B[o + i]
                    Kb = lambda i: Bn[o + i]
                    ql = 0
                    ts_ = slice(mt * 512, (mt + 1) * 512)
                    hk = hTr(hb)
                    tmp, sc, bonus, gs, pc = c['tmp'], c['sc'], c['bonus'], c['gs'], c['pc']
                    AR, KB, KH, sqb, Yk, YTk = c['AR'], c['KB'], c['KH'], c['sqb'], c['Yk'], c['YTk']
                    Wsb, Un, ycat, yout = c['Wsb'], c['Un'], c['ycat'], c['yout']
                    sk = [K_("sc%d" % i) for i in range(7)]
                    kt = [K_("tmp%d" % i) for i in range(4)]
                    kAR, kKB, kKH = K_('AR'), K_('KB'), K_('KH')
                    P = lambda i: par2[:, q, i:i + 1]
                    first = (mt == 0)
                    for g in range(4):
                        for kc in range(16):
                            MM(Bk(g)[:], Whg[:, ql, kc, g * 128:(g + 1) * 128], hTt[hb][:, kc, :], kc == 0, kc == 15,
                               ['Whg'] + hk, [Kb(g)], inc=(kc == 15))
                    for g in range(4):
                        shift_evac(o + g, tmp[g], kt[g], mu4[:, g, q, 0:1], mu4[:, g, q, 1:2], cara[:, q, g:g + 1], ('car', q, g), first)
                    r_, k_, v_, g_ = tmp[0][:], tmp[1][:], tmp[2][:], tmp[3][:]
                    MM(Bk(0)[:], lup[0:64, q * 128:(q + 1) * 128], loraT[0:64, ts_], True, True, ['lup', ('loraT', mt)], [Kb(0)])
                    MM(Bk(1)[:], lup[64:128, q * 128:(q + 1) * 128], loraT[64:128, ts_], True, True, ['lup', ('loraT', mt)], [Kb(1)])
                    sigmoid_(gs[:], K_('gs'), g_, kt[3], scale=-1.0)
                    TT(gs[:], gs[:], g_, ALU.mult, [K_('gs'), kt[3]], [K_('gs')])
                    sigmoid_(sc[1][:], sk[1], Bk(0)[:], Kb(0), ['par2'], scale=-1.0, bias=P(0))
                    sigmoid_(sc[2][:], sk[2], Bk(1)[:], Kb(1), ['par2'], scale=-1.0, bias=P(1))
                    SCAN(sc[3][:], rmask[:], sc[1][:], 0.0, ['rmask', sk[1]], [sk[3]])
                    TT(sc[4][:], sc[3][:], sc[1][:], ALU.subtract, [sk[3], sk[1]], [sk[4]])
                    ACT(sc[1][:], sc[3][:], AF.Exp, [sk[3]], [sk[1]], scale=-C0)
                    ACT(sc[5][:], sc[3][:], AF.Exp, [sk[3]], [sk[5]], scale=C0)
                    ACT(sc[4][:], sc[4][:], AF.Exp, [sk[4]], [sk[4]], scale=-C0)
                    CP(pc[:], sc[1][:].rearrange("p (c t) -> p c t", t=128)[:, :, 127], [sk[1]], [K_('pc')])
                    TS(sc[3][:], k_, P(2), None, ALU.mult, ALU.bypass, [kt[1], 'par2'], [sk[3]])
                    TT(sqb[:], sc[3][:], sc[3][:], ALU.mult, [sk[3]], [K_('sqb')])
                    MM(Bk(2)[:], onesbd[:], sqb[:], True, True, ['onesbd', K_('sqb')], [Kb(2)])
                    TS(sc[6][:], Bk(2)[:], 1e-24, None, ALU.max, ALU.bypass, [Kb(2)], [sk[6]])
                    ACT(sc[6][:], sc[6][:], AF.Ln, [sk[6]], [sk[6]], scale=float(2.0 ** 40))
                    ACT(sc[6][:], sc[6][:], AF.Exp, [sk[6], 'cb'], [sk[6]], scale=-0.5, bias=cb[:, 0:1])
                    TT(sc[3][:], sc[3][:], sc[6][:], ALU.mult, [sk[3], sk[6]], [sk[3]])
                    TS(sc[6][:], sc[2][:], P(3), P(4), ALU.mult, ALU.add, [sk[2], 'par2'], [sk[6]])
                    TT(sc[6][:], sc[6][:], k_, ALU.mult, [sk[6], kt[1]], [sk[6]])
                    TT(sc[2][:], sc[3][:], sc[2][:], ALU.mult, [sk[3], sk[2]], [sk[2]])
                    TT(AR[:, 0, :], sc[3][:], sc[4][:], ALU.mult, [sk[3], sk[4]], [kAR])
                    TT(AR[:, 1, :], r_, sc[1][:], ALU.mult, [kt[0], sk[1]], [kAR])
                    TT(sc[3][:], r_, sc[6][:], ALU.mult, [kt[0], sk[6]], [sk[3]])
                    ACT(sqb[:], sc[3][:], AF.Identity, [sk[3], 'par2'], [K_('sqb')], scale=P(5))
                    MM(Bk(3)[:], onesbd[:], sqb[:], True, True, ['onesbd', K_('sqb')], [Kb(3)])
                    TT(bonus[:], Bk(3)[:], v_, ALU.mult, [Kb(3), kt[2]], [K_('bonus')])
                    TT(sc[4][:], sc[6][:], sc[5][:], ALU.mult, [sk[6], sk[5]], [sk[4]])
                    TT(sc[1][:], sc[2][:], sc[5][:], ALU.mult, [sk[2], sk[5], K_('pc')], [sk[1]])
                    ACT(KB[:, 0, :], sc[4][:], AF.Identity, [sk[4]], [kKB])
                    ACT(KB[:, 1, :], sc[1][:], AF.Identity, [sk[1]], [kKB])
                    pcb = pc[:].unsqueeze(2).to_broadcast([128, 4, 128])
                    TT(KH[:, 0, :].rearrange("p (c t) -> p c t", t=128), sc[4][:].rearrange("p (c t) -> p c t", t=128), pcb,
                       ALU.mult, [sk[4], K_('pc')], [kKH])
                    TT(KH[:, 1, :].rearrange("p (c t) -> p c t", t=128), sc[1][:].rearrange("p (c t) -> p c t", t=128), pcb,
                       ALU.mult, [sk[1], K_('pc')], [kKH])
                    CP(KH[:, 2, :], v_, [kt[2]], [kKH])
                    for hh in range(2):
                        hs = slice(hh * 64, (hh + 1) * 64)
                        pt = b16(o + hh)[:, 0:768].rearrange("p (c g k) -> p c g k", c=4, g=3)
                        for cc in range(4):
                            for g3 in range(3):
                                TR(pt[:, cc, g3, :], KH[hs, g3, cc * 128:(cc + 1) * 128], ident[hs, hs], [kKH, 'ident'], [Kb(hh)],
                                   inc=(cc == 3 and g3 == 2))
                        CP(c['TOK'][hh][:], pt, [Kb(hh)], [K_('TOK%d' % hh)])
                    for hh in range(2):
                        h = 2 * q + hh
                        hs = slice(hh * 64, (hh + 1) * 64)
                        MS, G, TOK = c['MS'][hh], c['G'][hh], c['TOK'][hh]
                        kMSn, kG, kTOK = K_('MS%d' % hh), K_('G%d' % hh), K_('TOK%d' % hh)
                        kH32, kHb = ('H32', h), ('Hb', h)
                        H32 = H32a[hs, q, :]
                        Hb = Hba[hs, q, :]
                        for cc in range(4):
                            cs = slice(cc * 128, (cc + 1) * 128)
                            MM(Bk(2)[:, 0:256].rearrange('p (a t) -> p a t', a=2), KB[hs, 0, cs], AR[hs, :, cs], True, True, [kKB, kAR], [Kb(2)], inc=False)
                            MM(Bk(2)[:, 256:512].rearrange('p (a t) -> p a t', a=2), KB[hs, 1, cs], AR[hs, :, cs], True, True, [kKB, kAR], [Kb(2)])
                            MM(Bk(1)[:, cs], AR[hs, 0, cs], KB[hs, 1, cs], True, True, [kKB, kAR], [Kb(1)])
                            TT(MS[:, cc, :], Bk(2)[:], mask512[:], ALU.mult, [Kb(2), 'mask512'], [(kMSn, cc)])
                        mskeys = [(kMSn, cc) for cc in range(4)]
                        X0 = MS[:, :, 256:384]
                        TT(YTk[0][:], Bk(1)[:].rearrange("p (c t) -> p c t", t=128), maskL[:], ALU.mult, [Kb(1), 'maskL'], [K_('YTk0')])
                        CP(Yk[0][:], X0, mskeys, [K_('Yk0')])
                        TT(G[:], X0, ident4[:], ALU.add, mskeys + ['ident4'], [kG])
                        for k in range(1, 7):
                            a_, b_ = (k - 1) % 2, k % 2
                            kYa, kYTa, kYb, kYTb = K_('Yk%d' % a_), K_('YTk%d' % a_), K_('Yk%d' % b_), K_('YTk%d' % b_)
                            if k <= 5:
                                for cc in range(4):
                                    MM(Bk(0)[:, cc * 128:(cc + 1) * 128], YTk[a_][:, cc, :], Yk[a_][:, cc, :], True, True, [kYa, kYTa], [Kb(0)], inc=(cc == 3))
                            for cc in range(4):
                                MM(Bk(1)[:, cc * 128:(cc + 1) * 128], Yk[a_][:, cc, :], YTk[a_][:, cc, :], True, True, [kYa, kYTa], [Kb(1)], inc=(cc == 3))
                            if k <= 5:
                                ACT(Yk[b_][:], Bk(0)[:].rearrange("p (c t) -> p c t", t=128), AF.Identity, [Kb(0)], [kYb])
                            CP(YTk[b_][:], Bk(1)[:].rearrange("p (c t) -> p c t", t=128), [Kb(1)], [kYTb])
                            for cc in range(4):
                                MM(Bk(2)[:, cc * 128:(cc + 1) * 128], YTk[b_][:, cc, :], G[:, cc, :], True, True, [kYTb, kG], [Kb(2)], inc=(cc == 3))
                            TT(G[:], Bk(2)[:].rearrange("p (c t) -> p c t", t=128), G[:], ALU.add, [Kb(2), kG], [kG])
                        for cc in range(4):
                            cs = slice(cc * 128, (cc + 1) * 128)
                            kMS = (kMSn, cc)
                            MM(Bk(3)[:, 0:64], AR[hs, 0, cs], Hb, True, False, [kAR, kHb], [Kb(3)], inc=False)
                            MM(Bk(3)[:, 0:64], MS[:, cc, 0:128], TOK[:, cc, 2, :], False, True, [kMS, kTOK], [Kb(3)])
                            ACT(Wsb[:], Bk(3)[:, 0:64], AF.Identity, [Kb(3)], [K_('Wsb')])
                            MM(Bk(3)[:, 64:128], G[:, cc, :], Wsb[:], True, True, [kG, K_('Wsb')], [Kb(3)])
                            ACT(Un[:], Bk(3)[:, 64:128], AF.Identity, [Kb(3)], [K_('Un')], scale=-1.0)
                            MM(Bk(4)[hs, cs], Hb, AR[hs, 1, cs], True, False, [kHb, kAR], [Kb(4)], inc=False)
                            MM(Bk(4)[hs, cs], TOK[:, cc, 2, :], MS[:, cc, 128:256], False, False, [kTOK, kMS], [Kb(4)], inc=False)
                            MM(Bk(4)[hs, cs], Un[:], MS[:, cc, 384:512], False, True, [K_('Un'), kMS], [Kb(4)])
                            MM(Bk(3)[hs, 128:192], TOK[:, cc, 0, :], TOK[:, cc, 2, :], True, False, [kTOK], [Kb(3)], inc=False)
                            MM(Bk(3)[hs, 128:192], TOK[:, cc, 1, :], Un[:], False, True, [kTOK, K_('Un')], [Kb(3)])
                            STT(H32, H32, pc[hs, cc:cc + 1], Bk(3)[hs, 128:192], ALU.mult, ALU.add, [kH32, K_('pc'), Kb(3)], [kH32])
                            ACT(Hb, H32, AF.Identity, [kH32], [kHb])
                    ACT(ycat[:, 0, :], Bk(4)[:], AF.Identity, [Kb(4)], [K_('ycat')])
                    ACT(ycat[:, 1, :], Bk(4)[:], AF.Square, [Kb(4)], [K_('ycat')])
                    MM(Bk(1)[:], onesbdm[:], ycat[:, 0, :], True, True, ['onesbdm', K_('ycat')], [Kb(1)])
                    MM(Bk(2)[:], onesbdm[:], ycat[:, 1, :], True, True, ['onesbdm', K_('ycat')], [Kb(2)])
                    ACT(sc[1][:], Bk(1)[:], AF.Identity, [Kb(1)], [sk[1]])
                    TT(sc[2][:], sc[1][:], sc[1][:], ALU.mult, [sk[1]], [sk[2]])
                    TT(sc[2][:], Bk(2)[:], sc[2][:], ALU.subtract, [Kb(2), sk[2]], [sk[2]])
                    ACT(sc[2][:], sc[2][:], AF.Ln, [sk[2], 'cb'], [sk[2]], bias=cb[:, 1:2])
                    ACT(sc[2][:], sc[2][:], AF.Exp, [sk[2]], [sk[2]], scale=-0.5)
                    TT(sc[3][:], Bk(4)[:], sc[1][:], ALU.subtract, [Kb(4), sk[1]], [sk[3]])
                    TT(sc[3][:], sc[3][:], sc[2][:], ALU.mult, [sk[3], sk[2]], [sk[3]])
                    ACT(sc[3][:], sc[3][:], AF.Identity, [sk[3], 'par2'], [sk[3]], scale=P(6), bias=P(7))
                    TT(sc[3][:], sc[3][:], bonus[:], ALU.add, [sk[3], K_('bonus')], [sk[3]])
                    yi = c['n'] % 2
                    c['n'] += 1
                    TT(yout[yi][:], sc[3][:], gs[:], ALU.mult, [sk[3], K_('gs')], [K_('yout%d' % yi)])
                    DMA('sp', yT_d[q * 128:(q + 1) * 128, ts_], yout[yi][:], [K_('yout%d' % yi)], [('yT_d', mt)])

                Wfg = sb("Wfg", [128, 2, 16, 256], BF16, e2)
                kTa = [sb("kTa%d" % i, [70, T], BF16, e2) for i in range(2)]
                Va = [sb("Va%d" % i, [128, T // 128, 65], BF16, e2) for i in range(2)]
                for i in range(2):
                    MSET(Va[i][:], 1.0, [], [('Va', i)])
                    MSET(kTa[i][64:70, :], 1.0, [], [('kTa', i)])
                fctx = []
                for t in range(1, 2):
                    c = {'bo': 4 * t, 't': t, 'n': 0}
                    n_ = lambda s_: "%s_f%d" % (s_, t)
                    c['qTa'] = [sb(n_("qTa%d" % i), [70, 512], BF16, e2) for i in range(2)]
                    for i in range(2):
                        MSET(c['qTa'][i][64:70, :], 1.0, [], [n_("qTa%d" % i)])
                    c['vb'] = sb(n_("vb"), [64, 512], BF16, e2)
                    c['gsf'] = sb(n_("gsf"), [64, 512], F32, e2)
                    c['PTt'] = [sb(n_("PTt%d" % i), [128, 512], BF16, e2) for i in range(2)]
                    c['rec'] = sb(n_("rec"), [65, 512], F32, e2)
                    c['fo'] = sb(n_("fo"), [64, 512], F32, e2)
                    c['yf'] = [sb(n_("yf%d" % i), [64, 512], BF16, e2) for i in range(2)]
                    fctx.append(c)

                def fox_unit(h, I, hb, c):
                    t = c['t']
                    o = c['bo']
                    K_ = lambda s_: "%s_f%d" % (s_, t)
                    fb = {0: 5, 1: 5, 2: 6, 3: 7}
                    Bk = lambda i: B[fb[i]]
                    Kb = lambda i: Bn[fb[i]]
                    hl = h % 2
                    ts_ = slice(I * 512, (I + 1) * 512)
                    hk = hTr(hb)
                    u = c['n']
                    c['n'] += 1
                    qT = c['qTa'][u % 2]
                    kq = K_("qTa%d" % (u % 2))
                    kT, kkT = kTa[hl], ('kTa', hl)
                    V_, kV = Va[hl], ('Va', hl)
                    vb, gsf, PTt, rec, fo, yf = c['vb'], c['gsf'], c['PTt'], c['rec'], c['fo'], c['yf']
                    for kc in range(16):
                        MM(Bk(0)[:], Wfg[:, hl, kc, 0:128], hTt[hb][:, kc, :], kc == 0, kc == 15, ['Wfg'] + hk, [Kb(0)], inc=(kc == 15))
                    ACT(qT[0:64, :], Bk(0)[0:64, :], AF.Identity, [Kb(0)], [kq], scale=0.125)
                    DMA('sp', qT[64:67, :], ncaug_d[h, :, ts_], ['ncaug_d'], [kq])
                    CP(kT[0:64, ts_], Bk(0)[64:128, :], [Kb(0)], [kkT])
                    for kc in range(16):
                        MM(Bk(1)[:], Wfg[:, hl, kc, 128:256], hTt[hb][:, kc, :], kc == 0, kc == 15, ['Wfg'] + hk, [Kb(1)], inc=(kc == 15))
                    CP(vb[:], Bk(1)[0:64, :], [Kb(1)], [K_('vb')])
                    CP(gsf[:], Bk(1)[64:128, :], [Kb(1)], [K_('gsf')])
                    ACT(fo[:], gsf[:], AF.Exp, [K_('gsf')], [K_('fo')], scale=-1.0)
                    ACT(fo[:], fo[:], AF.Ln, [K_('fo')], [K_('fo')], bias=1.0)
                    ACT(fo[:], fo[:], AF.Exp, [K_('fo')], [K_('fo')], scale=-1.0)
                    TT(gsf[:], gsf[:], fo[:], ALU.mult, [K_('gsf'), K_('fo')], [K_('gsf')], eng='dve')
                    pt = b16(5)[:, 0:256].rearrange("p (c k) -> p c k", c=4)
                    for cc in range(4):
                        TR(pt[:, cc, :], vb[:, cc * 128:(cc + 1) * 128], ident[0:64, 0:64], [K_('vb'), 'ident'], [Kb(0)], inc=(cc == 3))
                    CP(V_[:, I * 4:(I + 1) * 4, 0:64], pt, [Kb(0)], [kV])
                    nj = 4 * I + 4

                    def qk(j):
                        lo = max(0, j - 4 * I) * 128
                        bk = 2 + (j % 2)
                        diag = j >= 4 * I
                        MM(Bk(bk)[:, lo:512], kT[0:70, j * 128:(j + 1) * 128], qT[0:70, lo:512], True, not diag, [kkT, kq], [Kb(bk)], inc=not diag)
                        if diag:
                            MM(Bk(bk)[:, lo:lo + 128], ident[:], maskneg[:], False, True, ['ident', 'maskneg'], [Kb(bk)])
                    qk(0)
                    for j in range(nj):
                        lo = max(0, j - 4 * I) * 128
                        bk = 2 + (j % 2)
                        pi = j % 2
                        if j + 1 < nj:
                            qk(j + 1)
                        ACT(PTt[pi][:, lo:512], Bk(bk)[:, lo:512], AF.Exp, [Kb(bk)], [K_('PTt%d' % pi)])
                        MM(Bk(1)[0:65, lo:512], V_[:, j, :], PTt[pi][:, lo:512], j == 0, j == nj - 1, [kV, K_('PTt%d' % pi)], [Kb(1)], inc=(j == nj - 1))
                    ACT(rec[64:65, :], Bk(1)[64:65, :], AF.Ln, [Kb(1)], [K_('rec')])
                    ACT(rec[64:65, :], rec[64:65, :], AF.Exp, [K_('rec')], [K_('rec')], scale=-1.0)
                    MM(B[6][0:64, :], onesf[64:65, :], rec[64:65, :], True, True, ['onesf', K_('rec')], [Bn[6]])
                    TT(fo[:], Bk(1)[0:64, :], gsf[:], ALU.mult, [Kb(1), K_('gsf')], [K_('fo')])
                    yi = u % 2
                    TT(yf[yi][:], fo[:], B[6][0:64, :], ALU.mult, [K_('fo'), Bn[6]], [K_('yf%d' % yi)])
                    DMA('sp', yT_d[1024 + h * 64:1024 + (h + 1) * 64, ts_], yf[yi][:], [K_('yf%d' % yi)], [('yT_d', I)])


                for g8 in range(8):
                    DMA('sp', Whg[:, 0], wb_rw[:, g8, :].rearrange("(k p) n -> p k n", p=128), ['wb_rw'], ['Whg'])
                    for hl_ in range(2):
                        DMA('sp', Wfg[:, hl_], wb_fox[:, g8 * 2 + hl_, :].rearrange("(k p) n -> p k n", p=128), ['wb_fox'], ['Wfg'])
                        DMA('sp', kTa[hl_][67:70, :], caug_d[g8 * 2 + hl_], ['caug_d'], [('kTa', hl_)])
                    DMA('sp', hTt[0][:].rearrange("p k t -> p (k t)"), hT_d[0], [('hT_d', 0)], hTr(0))
                    for mt in range(NM):
                        hb = mt % 2
                        if mt + 1 < NM:
                            DMA('sp', hTt[1 - hb][:].rearrange("p k t -> p (k t)"), hT_d[mt + 1], [('hT_d', mt + 1)], hTr(1 - hb))

                        def thA(mt=mt, hb=hb, g8=g8):
                            rw_pair(g8, mt, hb, ctxs[0])

                        def thB(mt=mt, hb=hb, g8=g8):
                            for hh in range(2):
                                fox_unit(g8 * 2 + hh, mt, hb, fctx[0])
                        run_threads([thA, thB], stagger=0.0)
            S.barrier()
        esl.close()

        if do_post:
            with ExitStack() as e4:
                gp = sb("gp", [128, D], F32, e4)
                gf = sb("gf", [128, D], F32, e4)
                gn = sb("gn", [128, D], F32, e4)
                role = sb("role_sb", [128, 2], F32, e4)
                DMA('sp', gn[:], norm_g.partition_broadcast(128), [], ['gn'])
                DMA('sp', role[:], role_in, [], ['role'])
                DMA('sp', gp[:], ple_ng.partition_broadcast(128), [], ['gp'])
                DMA('sp', gf[:], fin_g.partition_broadcast(128), [], ['gf'])
                x1 = sb("x1", [128, 4, D], F32, e4)
                xnb = sb("xnb4", [128, D], BF16, e4)
                st = sb("st4", [128, 4], F32, e4)
                yTt = sb("yTt", [128, 16, 512], BF16, e4)
                mT = sb("mT", [128, 16, 512], BF16, e4)
                pT = sb("pT", [128, 2, 512], BF16, e4)
                pt32 = sb("pt32", [128, 256], F32, e4)
                ptb = sb("ptb", [128, 256], BF16, e4)
                wsm = [sb("wsm%d" % i, [128, 16, 128], BF16, e4) for i in range(5)]
                wbg = [sb("wbg%d" % i, [128, 16, 512], BF16, e4) for i in range(2)]
                wpp = sb("wpp", [128, 2, 512], BF16, e4)
                sg1 = [sb("sg1_%d" % i, [128, 512], F32, e4) for i in range(2)]
                sg2 = [sb("sg2_%d" % i, [128, 512], F32, e4) for i in range(2)]
                ot = sb("ot0", [128, D], F32, e4)
                nsm = [0]
                nbg = [0]

                def rms_rstd(src_ap, ksrc):
                    ACT(xnb[:], src_ap, AF.Square, [ksrc], ['xnb4', 'st4'], accum_out=st[:, 0:1])
                    TS(st[:, 1:2], st[:, 0:1], 1.0 / D, 1e-6, ALU.mult, ALU.add, ['st4'], ['st4'])
                    ACT(st[:, 2:3], st[:, 1:2], AF.Sqrt, ['st4'], ['st4'])
                    RCP(st[:, 3:4], st[:, 2:3], ['st4'], ['st4'])

                def norm_T(src_ap, ksrc, gtile, kg, dstT, kdst, s_):
                    rms_rstd(src_ap, ksrc)
                    STT(xnb[:], src_ap, st[:, 3:4], gtile[:], ALU.mult, ALU.mult, [ksrc, 'st4', kg], ['xnb4'])
                    for j in range(16):
                        bk = j // 8
                        TR(b16(bk)[:, (j % 8) * 128:(j % 8 + 1) * 128], xnb[:, j * 128:(j + 1) * 128], ident[:],
                           ['xnb4', 'ident'], [Bn[bk]], inc=(j % 8 == 7))
                    ACT(dstT[:, 0:8, s_ * 128:(s_ + 1) * 128], b16(0)[:].rearrange("p (j t) -> p j t", t=128),
                        AF.Identity, [Bn[0]], kdst[0:8])
                    CP(dstT[:, 8:16, s_ * 128:(s_ + 1) * 128], b16(1)[:].rearrange("p (j t) -> p j t", t=128),
                       [Bn[1]], kdst[8:16])

                NH = NM // 2
                for TT_ in range(NH):
                    def body(TT_=TT_):
                        ts_ = slice(TT_ * 512, (TT_ + 1) * 512)
                        tsB = slice((TT_ + NH) * 512, (TT_ + NH + 1) * 512)
                        hb = 0
                        hT4 = hTt[0]
                        yB = hTt[1]
                        DMA('sp', yTt[:], yT_d[:, ts_].rearrange("(k p) t -> p k t", p=128), [('yT_d', TT_)], ['yTt'])
                        DMA('sp', yB[:], yT_d[:, tsB].rearrange("(k p) t -> p k t", p=128), [('yT_d', TT_ + NH)], hTr(1))
                        yf_ = yTt[:].rearrange("p k t -> p (k t)")
                        TS(yf_, yf_, role[:, 0:1], None, ALU.mult, ALU.bypass, ['yTt', 'role'], ['yTt'])
                        STT(yf_, yB[:].rearrange("p k t -> p (k t)"), role[:, 1:2], yf_, ALU.mult, ALU.add, hTr(1) + ['yTt', 'role'], ['yTt'])
                        for s_ in range(4):
                            tok = slice(TT_ * 512 + s_ * 128, TT_ * 512 + (s_ + 1) * 128)
                            DMA('sp', x1[:, s_, :], x_post[tok, :], [], [('x1', s_)])
                            norm_T(x1[:, s_, :], ('x1', s_), gn, 'gn', hT4, [hTw(0, s_, 0)] * 8 + [hTw(0, s_, 1)] * 8, s_)
                            DMA('sp', pt32[:], p_post[tok, :], [], ['pt32'])
                            CP(ptb[:], pt32[:], ['pt32'], ['ptb'])
                            for j in range(2):
                                TR(b16(2)[:, j * 128:(j + 1) * 128], ptb[:, j * 128:(j + 1) * 128], ident[:], ['ptb', 'ident'],
                                   [Bn[2]], inc=(j == 1))
                            CP(pT[:, :, s_ * 128:(s_ + 1) * 128], b16(2)[:, 0:256].rearrange("p (j t) -> p j t", t=128),
                               [Bn[2]], ['pT'])
                        for oc in range(16):
                            ocs = slice(oc * 128, (oc + 1) * 128)
                            par_ = oc % 2
                            w = []
                            for (src, nk, kk_) in ((wb_uprw[:, ocs], 8, 'wb_uprw'), (wb_upfox[:, ocs], 8, 'wb_upfox'),
                                                   (wb_gate[:, ocs], 16, 'wb_gate'),
                                                   (wb_gate[:, 2048 + oc * 128:2048 + (oc + 1) * 128], 16, 'wb_gate')):
                                wi = nsm[0] % 5
                                nsm[0] += 1
                                DMA('sp', wsm[wi][:, 0:nk, :], src.rearrange("(k p) n -> p k n", p=128), [kk_], ['wsm%d' % wi])
                                w.append(wi)
                            bb = 4 * par_
                            for kc in range(8):
                                MM(B[bb][:], wsm[w[0]][:, kc, :], yTt[:, kc, :], kc == 0, kc == 7, ['wsm%d' % w[0], 'yTt'], [Bn[bb]], inc=(kc == 7))
                            for kc in range(8):
                                MM(B[bb + 1][:], wsm[w[1]][:, kc, :], yTt[:, 8 + kc, :], kc == 0, kc == 7, ['wsm%d' % w[1], 'yTt'], [Bn[bb + 1]], inc=(kc == 7))
                            for kc in range(16):
                                MM(B[bb + 2][:], wsm[w[2]][:, kc, :], hT4[:, kc, :], kc == 0, kc == 15, ['wsm%d' % w[2]] + hTr(hb), [Bn[bb + 2]], inc=(kc == 15))
                            for kc in range(16):
                                MM(B[bb + 3][:], wsm[w[3]][:, kc, :], hT4[:, kc, :], kc == 0, kc == 15, ['wsm%d' % w[3]] + hTr(hb), [Bn[bb + 3]], inc=(kc == 15))
                            k1, k2 = 'sg1_%d' % par_, 'sg2_%d' % par_
                            ACT(sg1[par_][:], B[bb + 2][:], AF.Sigmoid, [Bn[bb + 2]], [k1])
                            ACT(sg2[par_][:], B[bb + 3][:], AF.Sigmoid, [Bn[bb + 3]], [k2])
                            TT(sg1[par_][:], B[bb][:], sg1[par_][:], ALU.mult, [Bn[bb], k1], [k1])
                            TT(sg2[par_][:], B[bb + 1][:], sg2[par_][:], ALU.mult, [Bn[bb + 1], k2], [k2])
                            TT(mT[:, oc, :], sg1[par_][:], sg2[par_][:], ALU.add, [k1, k2], [('mT', oc)], eng='pool')
                        mTk = [('mT', oc) for oc in range(16)]
                        for cg in range(4):
                            cgs = slice(cg * 512, (cg + 1) * 512)
                            wi = nbg[0] % 2
                            nbg[0] += 1
                            DMA('sp', wbg[wi][:], wb_out[:, cgs].rearrange("(k p) n -> p k n", p=128), ['wb_out'], ['wbg%d' % wi])
                            for s_ in range(4):
                                bk = s_ % 4
                                for kc in range(16):
                                    MM(B[bk][:], mT[:, kc, s_ * 128:(s_ + 1) * 128], wbg[wi][:, kc, :], kc == 0, kc == 15,
                                       mTk + ['wbg%d' % wi], [Bn[bk]], inc=(kc == 15))
                                TT(x1[:, s_, cgs], x1[:, s_, cgs], B[bk][:], ALU.add, [('x1', s_), Bn[bk]], [('x1', s_)])
                        for s_ in range(4):
                            norm_T(x1[:, s_, :], ('x1', s_), gp, 'gp', mT, mTk, s_)
                        for cg in range(4):
                            cgs = slice(cg * 512, (cg + 1) * 512)
                            wi = nbg[0] % 2
                            nbg[0] += 1
                            DMA('sp', wbg[wi][:], wb_pg[:, cgs].rearrange("(k p) n -> p k n", p=128), ['wb_pg'], ['wbg%d' % wi])
                            DMA('sp', wpp[:], wb_pp[:, cgs].rearrange("(k p) n -> p k n", p=128), ['wb_pp'], ['wpp'])
                            for s_ in range(4):
                                bk = 2 + (s_ % 2)
                                bp = 4 + (s_ % 2)
                                ks_ = 'sg1_%d' % (s_ % 2)
                                for kc in range(16):
                                    MM(B[bk][:], mT[:, kc, s_ * 128:(s_ + 1) * 128], wbg[wi][:, kc, :], kc == 0, kc == 15,
                                       mTk + ['wbg%d' % wi], [Bn[bk]], inc=(kc == 15))
                                for kc in range(2):
                                    MM(B[bp][:], pT[:, kc, s_ * 128:(s_ + 1) * 128], wpp[:, kc, :], kc == 0, kc == 1,
                                       ['pT', 'wpp'], [Bn[bp]], inc=(kc == 1))
                                ACT(sg1[s_ % 2][:], B[bk][:], AF.Sigmoid, [Bn[bk]], [ks_])
                                TT(sg1[s_ % 2][:], B[bp][:], sg1[s_ % 2][:], ALU.mult, [Bn[bp], ks_], [ks_])
                                TT(x1[:, s_, cgs], x1[:, s_, cgs], sg1[s_ % 2][:], ALU.add, [('x1', s_), ks_], [('x1', s_)], eng='pool')
                        for s_ in range(4):
                            tok = slice(TT_ * 512 + s_ * 128, TT_ * 512 + (s_ + 1) * 128)
                            rms_rstd(x1[:, s_, :], ('x1', s_))
                            STT(ot[:], x1[:, s_, :], st[:, 3:4], gf[:], ALU.mult, ALU.mult, [('x1', s_), 'st4', 'gf'], ['ot0'])
                            DMA('sp', out[tok, :], ot[:], ['ot0'], ['out'])
                    run_threads([body])
        S.barrier(final=True)
        with nc.Block() as block:
            S.emit(block)
    return nc

_CACHE = {}


def _prep(inputs, b, T, r=0):
    f = lambda a: np.ascontiguousarray(np.asarray(a, dtype=np.float32))
    H = T // 2
    role = np.zeros((128, 2), np.float32)
    role[:, r] = 1.0
    m = {
        "x": f(inputs["x"][b, :T]),
        "p": f(inputs["p"][0, b, :T]),
        "x_post": f(inputs["x"][b, r * H:(r + 1) * H]),
        "p_post": f(inputs["p"][0, b, r * H:(r + 1) * H]),
        "role": role,
        "norm_g": f(inputs["norm_g"][0:1]),
        "w_in": f(inputs["w_in"][0]),
        "rw_shift_mu": f(inputs["rw_shift_mu"][0:1]),
        "rw_w0": f(inputs["rw_w0"][0:1]),
        "rw_w_lora_up": f(inputs["rw_w_lora_up"][0]),
        "rw_a0": f(inputs["rw_a0"][0:1]),
        "rw_a_lora_up": f(inputs["rw_a_lora_up"][0]),
        "rw_k_k": f(inputs["rw_k_k"][0:1]),
        "rw_k_a": f(inputs["rw_k_a"][0:1]),
        "rw_r_k": f(np.asarray(inputs["rw_r_k"][0]).reshape(1, 1024)),
        "rw_ln_g": f(inputs["rw_ln_g"][0:1]),
        "rw_ln_b": f(inputs["rw_ln_b"][0:1]),
        "fox_b_f": f(inputs["fox_b_f"][0:1]),
        "w_up_rwkv": f(inputs["w_up_rwkv"][0]),
        "w_up_fox": f(inputs["w_up_fox"][0]),
        "w_out": f(inputs["w_out"][0]),
        "ple_proj": f(inputs["ple_proj"][0]),
        "ple_gate_w": f(inputs["ple_gate_w"][0]),
        "ple_norm_g": f(inputs["ple_norm_g"][0:1]),
        "final_norm_g": f(np.asarray(inputs["final_norm_g"]).reshape(1, D)),
    }
    return m


def kernel(**inputs):
    T = 4096
    if T not in _CACHE:
        _CACHE[T] = build(T)
    nc = _CACHE[T]
    in_maps = [_prep(inputs, c % 4, T, c // 4) for c in range(8)]
    res = run_bass_kernel_spmd(nc, in_maps, core_ids=list(range(8)))
    H = T // 2
    out = np.empty((4, T, D), np.float32)
    for c in range(8):
        out[c % 4, (c // 4) * H:(c // 4 + 1) * H] = np.asarray(res.results[c]["out"], dtype=np.float32)
    return out
```

```python
import numpy as np
import concourse.bass as bass
import concourse.mybir as mybir
from concourse.bass_utils import run_bass_kernel_spmd
from contextlib import ExitStack

F32 = mybir.dt.float32
BF16 = mybir.dt.bfloat16
AF = mybir.ActivationFunctionType
ALU = mybir.AluOpType

ENGS = ('pe', 'act', 'dve', 'pool', 'sp')
NDSEM = 8
D = 2048
NIN = 12432
C0 = 0.6065306597126334
NEG = -30000.0


class Sched:
    def __init__(self, nc, es):
        self.nc = nc
        self.q = {e: [] for e in ENGS}
        self.cnt = {e: 0 for e in ENGS}
        self.seen = {e: {} for e in ENGS}
        self.state = {}
        self.dma_n = {e: 0 for e in ENGS}
        self.sem = {}
        for e in ('pe', 'act', 'dve', 'pool'):
            self.sem[e] = es.enter_context(nc.semaphore('s_' + e))
        for e in ('sp', 'pool', 'act'):
            for i in range(NDSEM):
                self.sem['d%s%d' % (e, i)] = es.enter_context(nc.semaphore('sd_%s%d' % (e, i)))

    def _deps(self, eng, reads, writes):
        deps = {}

        def need(sv):
            if sv is None:
                return
            s, v = sv
            if eng == 'pe' and s == 'pe':
                return
            if deps.get(s, 0) < v:
                deps[s] = v
        for k in reads:
            st = self.state.get(k)
            if st:
                need(st[0])
        for k in writes:
            st = self.state.get(k)
            if st:
                need(st[0])
                for s, v in st[1].items():
                    need((s, v))
        out = []
        for s, v in deps.items():
            if self.seen[eng].get(s, 0) < v:
                self.seen[eng][s] = v
                out.append((s, v))
        return out

    def _commit(self, token, reads, writes):
        for k in writes:
            self.state[k] = [token, {}]
        s, v = token
        for k in reads:
            st = self.state.setdefault(k, [None, {}])
            if st[1].get(s, 0) < v:
                st[1][s] = v

    def op(self, eng, fn, reads=(), writes=(), inc=True):
        waits = self._deps(eng, reads, writes)
        if inc:
            self.cnt[eng] += 1
            token = (eng, self.cnt[eng])
        else:
            token = (eng, self.cnt[eng] + 1)
        self.q[eng].append((fn, waits, (eng, 1) if inc else None))
        self._commit(token, reads, writes)

    def dma(self, q, fn, reads=(), writes=()):
        n = self.dma_n[q]
        self.dma_n[q] += 1
        slot, gen = n % NDSEM, n // NDSEM
        sn = 'd%s%d' % (q, slot)
        waits = self._deps(q, reads, writes)
        if gen > 0 and self.seen[q].get(sn, 0) < 16 * gen:
            self.seen[q][sn] = 16 * gen
            waits.append((sn, 16 * gen))
        token = (sn, 16 * (gen + 1))
        self.q[q].append((fn, waits, (sn, 16)))
        self._commit(token, reads, writes)

    def _allvals(self):
        vals = {e: self.cnt[e] for e in ('pe', 'act', 'dve', 'pool')}
        for q in ('sp', 'pool', 'act'):
            n = self.dma_n[q]
            for slot in range(min(n, NDSEM)):
                gens = (n - slot + NDSEM - 1) // NDSEM
                vals['d%s%d' % (q, slot)] = 16 * gens
        return vals

    def barrier(self, skip_pool=False, final=False):
        vals = self._allvals()
        if skip_pool:
            vals = {k: v for k, v in vals.items() if not k.startswith('dpool')}
        for e in ENGS:
            waits = []
            for s, v in vals.items():
                if e == 'pe' and s == 'pe':
                    continue
                if v > 0 and self.seen[e].get(s, 0) < v:
                    self.seen[e][s] = v
                    waits.append((s, v))
            self.q[e].append((None, waits, None))

    def emit(self, block):
        sem = self.sem

        def run(name, e):
            for fn, waits, inc in self.q[name]:
                for s, v in waits:
                    e.wait_ge(sem[s], v)
                if fn is None:
                    continue
                ins = fn(e)
                if inc is not None:
                    ins.then_inc(sem[inc[0]], inc[1])

        @block.tensor
        def _(e):
            run('pe', e)

        @block.scalar
        def _(e):
            run('act', e)

        @block.vector
        def _(e):
            run('dve', e)

        @block.gpsimd
        def _(e):
            run('pool', e)

        @block.sync
        def _(e):
            run('sp', e)


def build(T, dbg=False, do_rw=True, do_fox=True, do_post=True):
    nc = bass.Bass("TRN2", target_bir_lowering=False)
    NT = T // 128
    NM = T // 512

    def din(name, shape):
        return nc.dram_tensor(name, shape, F32, kind="ExternalInput").ap()
    x = din("x", [T, D])
    p = din("p", [T, 256])
    norm_g = din("norm_g", [1, D])
    w_in = din("w_in", [D, NIN])
    mu = din("rw_shift_mu", [1, 4224])
    rw_w0 = din("rw_w0", [1, 1024])
    rw_wl = din("rw_w_lora_up", [64, 1024])
    rw_a0 = din("rw_a0", [1, 1024])
    rw_al = din("rw_a_lora_up", [64, 1024])
    rw_kk = din("rw_k_k", [1, 1024])
    rw_ka = din("rw_k_a", [1, 1024])
    rw_rk = din("rw_r_k", [1, 1024])
    rw_lng = din("rw_ln_g", [1, 1024])
    rw_lnb = din("rw_ln_b", [1, 1024])
    fox_bf = din("fox_b_f", [1, 16])
    w_uprw = din("w_up_rwkv", [1024, D])
    w_upfox = din("w_up_fox", [1024, D])
    w_out = din("w_out", [D, D])
    ple_proj = din("ple_proj", [256, D])
    ple_gw = din("ple_gate_w", [D, D])
    ple_ng = din("ple_norm_g", [1, D])
    fin_g = din("final_norm_g", [1, D])
    x_post = din("x_post", [T // 2, D])
    p_post = din("p_post", [T // 2, 256])
    role_in = din("role", [128, 2])
    out = nc.dram_tensor("out", [T // 2, D], F32, kind="ExternalOutput").ap()

    def dsc(name, shape, dt=BF16, ext=False):
        if ext:
            return nc.dram_tensor(name, shape, dt, kind="ExternalOutput").ap()
        return nc.dram_tensor(name, shape, dt).ap()
    wb_rw = dsc("wb_rw", [D, 8, 512])
    wb_lo = dsc("wb_lo", [D, 128])
    wb_fox = dsc("wb_fox", [D, 16, 256])
    wb_fl = dsc("wb_fl", [D, 16])
    wb_gate = dsc("wb_gate", [D, 4096])
    wb_uprw = dsc("wb_uprw", [1024, D])
    wb_upfox = dsc("wb_upfox", [1024, D])
    wb_out = dsc("wb_out", [D, D])
    wb_pp = dsc("wb_pp", [256, D])
    wb_pg = dsc("wb_pg", [D, D])
    wb_wl = dsc("wb_wl", [64, 1024])
    wb_al = dsc("wb_al", [64, 1024])
    yT_d = dsc("yT_d", [2048, T], BF16, ext=dbg)
    hT_d = dsc("hT_d", [T // 512, 128, 16 * 512])
    caug_d = dsc("caug_d", [16, 3, T])
    ncaug_d = dsc("ncaug_d", [16, 3, T])

    with ExitStack() as es:
        S = Sched(nc, es)

        def sb(name, shape, dt, st=es):
            return st.enter_context(nc.sbuf_tensor(name, shape, dt))

        cur = [None]

        def fsz(ap):
            n = 1
            for d_ in ap.shape[1:]:
                n *= int(d_)
            return n

        def OP(eng, fn, r, w, inc=True, cost=300.0):
            if cur[0] is None:
                S.op(eng, fn, r, w, inc)
            else:
                cur[0].append((0, eng, fn, tuple(r), tuple(w), inc, cost))

        def DM(q, fn, r, w):
            if cur[0] is None:
                S.dma(q, fn, r, w)
            else:
                cur[0].append((1, q, fn, tuple(r), tuple(w), True, 60.0))

        DMA_LAT = 3500.0
        XLAT = 400.0

        def run_threads(fns, stagger=0.0):
            import heapq
            lists = []
            for f in fns:
                cur[0] = []
                f()
                lists.append(cur[0])
                cur[0] = None
            th = []
            for L in lists:
                atoms, a = [], []
                for it in L:
                    a.append(it)
                    if it[5]:
                        atoms.append(a)
                        a = []
                assert not a
                n = len(atoms)
                eng = [at[0][1] for at in atoms]
                dur = [sum(it[6] for it in at) for at in atoms]
                isd = [at[0][0] == 1 for at in atoms]
                deps = [set() for _ in range(n)]
                state = {}
                for i, at in enumerate(atoms):
                    for it in at:
                        for k in it[3]:
                            st = state.get(k)
                            if st and st[0] is not None:
                                deps[i].add(st[0])
                        for k in it[4]:
                            st = state.get(k)
                            if st:
                                if st[0] is not None:
                                    deps[i].add(st[0])
                                deps[i].update(st[1])
                    for it in at:
                        for k in it[4]:
                            state[k] = [i, set()]
                        for k in it[3]:
                            state.setdefault(k, [None, set()])[1].add(i)
                    deps[i].discard(i)
                succ = [[] for _ in range(n)]
                for i in range(n):
                    for d_ in deps[i]:
                        succ[d_].append(i)
                th.append({'atoms': atoms, 'eng': eng, 'dur': dur, 'isd': isd, 'deps': deps, 'succ': succ,
                           'indeg': [len(deps[i]) for i in range(n)], 'fin': [0.0] * n, 'rdy': [0.0] * n})
            heaps = {}
            for ti, t_ in enumerate(th):
                for i in range(len(t_['atoms'])):
                    if t_['indeg'][i] == 0:
                        t_['rdy'][i] = ti * stagger
                        heapq.heappush(heaps.setdefault(t_['eng'][i], []), (t_['rdy'][i], i, ti))
            efree = {}
            order = []
            total = sum(len(t_['atoms']) for t_ in th)
            while total:
                best = None
                for e, hp in heaps.items():
                    if not hp:
                        continue
                    ef = efree.get(e, 0.0)
                    cand = hp[0]
                    if cand[0] <= ef:
                        k_ = min((c_ for c_ in hp if c_[0] <= ef), key=lambda c_: (c_[1], c_[2]))
                        st_ = ef
                    else:
                        k_ = cand
                        st_ = cand[0]
                    if best is None or st_ < best[0]:
                        best = (st_, e, k_)
                assert best is not None
                st_, e, k_ = best
                hp = heaps[e]
                hp.remove(k_)
                heapq.heapify(hp)
                _, i, ti = k_
                t_ = th[ti]
                efree[e] = st_ + t_['dur'][i]
                t_['fin'][i] = st_ + (DMA_LAT if t_['isd'][i] else t_['dur'][i])
                total -= 1
                order.append(t_['atoms'][i])
                for j in t_['succ'][i]:
                    lat = 60.0 if t_['eng'][j] == e else XLAT
                    if t_['fin'][i] + lat > t_['rdy'][j]:
                        t_['rdy'][j] = t_['fin'][i] + lat
                    t_['indeg'][j] -= 1
                    if t_['indeg'][j] == 0:
                        heapq.heappush(heaps.setdefault(t_['eng'][j], []), (t_['rdy'][j], j, ti))
            for at in order:
                for it in at:
                    if it[0] == 0:
                        S.op(it[1], it[2], it[3], it[4], it[5])
                    else:
                        S.dma(it[1], it[2], it[3], it[4])

        def ACT(out_, in_, func, r, w, **kw):
            OP('act', lambda e: e.activation(out=out_, in_=in_, func=func, **kw), r, w, cost=220.0 + fsz(out_) / 1.4)

        def TT(out_, a, b, op, r, w, eng='dve'):
            OP(eng, lambda e: e.tensor_tensor(out=out_, in0=a, in1=b, op=op), r, w,
               cost=(70.0 + fsz(out_) * 1.1) * (3.0 if eng == 'pool' else 1.0))

        def TS(out_, a, s1, s2, op0, op1, r, w, eng='dve'):
            OP(eng, lambda e: e.tensor_scalar(out=out_, in0=a, scalar1=s1, scalar2=s2, op0=op0, op1=op1), r, w,
               cost=70.0 + fsz(out_) * 0.75)

        def STT(out_, a, sc_, b, op0, op1, r, w):
            OP('dve', lambda e: e.scalar_tensor_tensor(out=out_, in0=a, scalar=sc_, in1=b, op0=op0, op1=op1), r, w,
               cost=70.0 + fsz(out_) * 1.1)

        def CP(out_, in_, r, w, eng='dve'):
            OP(eng, lambda e: e.tensor_copy(out=out_, in_=in_), r, w,
               cost=(70.0 + fsz(out_) * 1.0) * (3.0 if eng == 'pool' else 1.0))

        def MM(out_, lhsT, rhs, start, stop, r, w, inc=True):
            n_ = max(64, fsz(rhs))
            c_ = n_ / 1.2 * (4.0 if rhs.dtype == F32 else 1.0) + 20.0
            OP('pe', lambda e: e.matmul(out_, lhsT=lhsT, rhs=rhs, start=start, stop=stop), r, w, inc=inc, cost=c_)

        def TR(out_, in_, ident_, r, w, inc=True):
            OP('pe', lambda e: e.transpose(out_, in_, ident_), r, w, inc=inc, cost=70.0)

        def DMA(q, out_, in_, r, w, **kw):
            DM(q, lambda e: e.dma_start(out=out_, in_=in_, **kw), r, w)

        def RCP(out_, in_, r, w):
            OP('dve', lambda e: e.reciprocal(out=out_, in_=in_), r, w, cost=100.0 + fsz(out_) * 5.5)

        def SCAN(out_, d0, d1, init, r, w):
            OP('dve', lambda e: e.tensor_tensor_scan(out=out_, data0=d0, data1=d1, initial=init,
                                                    op0=ALU.mult, op1=ALU.add), r, w, cost=70.0 + fsz(out_) * 2.1)

        def MSET(ap, val, r, w, eng='pool'):
            OP(eng, lambda e: e.memset(ap, val), r, w, cost=100.0 + fsz(ap) * 0.5)

        B = [es.enter_context(nc.psum_tensor("B%d" % i, [128, 512], F32)) for i in range(8)]
        Bn = ["B%d" % i for i in range(8)]

        def b16(i):
            return B[i][:].bitcast(BF16)

        DMA('pool', wb_lo, w_in[:, 4096:4224], [], ['wb_lo'])
        DMA('pool', wb_fl, w_in[:, 8320:8336], [], ['wb_fl'])
        DMA('pool', wb_wl, rw_wl, [], ['wb_wl'])
        DMA('pool', wb_al, rw_al, [], ['wb_al'])
        identf = sb("identf", [128, 128], F32)
        ident = sb("ident", [128, 128], BF16)
        mask512 = sb("mask512", [128, 512], BF16)
        maskL = sb("maskL", [128, 4, 128], BF16)
        ident4 = sb("ident4", [128, 4, 128], BF16)
        maskneg = sb("maskneg", [128, 128], BF16)
        ones64 = sb("ones64", [64, 64], BF16)
        onesm = sb("onesm", [64, 64], BF16)
        onesf = sb("onesf", [128, 64], F32)
        rmask = sb("rmask", [128, 512], F32)
        onesbd = sb("onesbd", [128, 128], BF16)
        onesbdm = sb("onesbdm", [128, 128], BF16)
        cb = sb("cb", [128, 2], F32)
        mtmp = sb("mtmp", [128, 128], F32)

        def asel(tile_ap, cmp, base, cm, pat, fill, key):
            S.op('pool', lambda e: e.affine_select(out=tile_ap, in_=tile_ap, pattern=pat, compare_op=cmp,
                                                   fill=fill, base=base, channel_multiplier=cm), [key], [key])
        MSET(identf[:], 1.0, [], ['identf'])
        asel(identf[:], ALU.is_equal, 0, 1, [[-1, 128]], 0.0, 'identf')
        CP(ident[:], identf[:], ['identf'], ['ident'])
        for c in range(4):
            CP(ident4[:, c, :], identf[:], ['identf'], ['ident4'])
        MSET(mtmp[:], 1.0, [], ['mtmp'])
        asel(mtmp[:], ALU.is_gt, 0, -1, [[1, 128]], 0.0, 'mtmp')
        CP(mask512[:, 0:128], mtmp[:], ['mtmp'], ['mask512'])
        TS(mask512[:, 256:384], mtmp[:], -1.0, None, ALU.mult, ALU.bypass, ['mtmp'], ['mask512'])
        MSET(mtmp[:], 1.0, ['mtmp'], ['mtmp'])
        asel(mtmp[:], ALU.is_ge, 0, -1, [[1, 128]], 0.0, 'mtmp')
        CP(mask512[:, 128:256], mtmp[:], ['mtmp'], ['mask512'])
        CP(mask512[:, 384:512], mtmp[:], ['mtmp'], ['mask512'])
        TS(maskneg[:], mtmp[:], -1.0, -NEG, ALU.add, ALU.mult, ['mtmp'], ['maskneg'])
        MSET(mtmp[:], -1.0, ['mtmp'], ['mtmp'])
        asel(mtmp[:], ALU.is_gt, 0, 1, [[-1, 128]], 0.0, 'mtmp')
        for c in range(4):
            CP(maskL[:, c, :], mtmp[:], ['mtmp'], ['maskL'])
        MSET(ones64[:], 1.0, [], ['ones64'])
        MSET(onesm[:], 1.0 / 64.0, [], ['onesm'])
        MSET(onesf[:], 1.0, [], ['onesf'])
        MSET(onesbd[:], 0.0, [], ['onesbd'])
        MSET(onesbdm[:], 0.0, [], ['onesbdm'])
        for hh_ in range(2):
            hs_ = slice(hh_ * 64, (hh_ + 1) * 64)
            MSET(onesbd[hs_, hs_], 1.0, ['onesbd'], ['onesbd'])
            MSET(onesbdm[hs_, hs_], 1.0 / 64.0, ['onesbdm'], ['onesbdm'])
        MSET(cb[:, 0:1], 13.862943611198906, [], ['cb'])
        MSET(cb[:, 1:2], 64e-5, ['cb'], ['cb'])
        MSET(rmask[:], 1.0, [], ['rmask'])
        MSET(rmask[:].rearrange("p (c t) -> p c t", t=128)[:, :, 0:1], 0.0, ['rmask'], ['rmask'])

        for rg in range(16):
            rs = slice(rg * 128, (rg + 1) * 128)
            for g4 in range(4):
                DMA('pool', wb_rw[rs, :, g4 * 128:(g4 + 1) * 128],
                    w_in[rs, g4 * 1024:(g4 + 1) * 1024].rearrange("r (q c) -> r q c", c=128), [], ['wb_rw'])
        for rg in range(16):
            rs = slice(rg * 128, (rg + 1) * 128)
            for g4 in range(4):
                DMA('pool', wb_fox[rs, :, g4 * 64:(g4 + 1) * 64],
                    w_in[rs, 4224 + g4 * 1024:4224 + (g4 + 1) * 1024].rearrange("r (h c) -> r h c", c=64), [], ['wb_fox'])
        for rg in range(16):
            rs = slice(rg * 128, (rg + 1) * 128)
            DMA('pool', wb_gate[rs].rearrange("r (a c) -> r a c", c=1024),
                w_in[rs, 8336:12432].rearrange("r (a c) -> r a c", c=1024), [], ['wb_gate'])
        for (dst, src, nm, nr) in ((wb_uprw, w_uprw, 'wb_uprw', 1024), (wb_upfox, w_upfox, 'wb_upfox', 1024),
                                   (wb_out, w_out, 'wb_out', D), (wb_pp, ple_proj, 'wb_pp', 256),
                                   (wb_pg, ple_gw, 'wb_pg', D)):
            for r0 in range(0, nr, 256):
                DMA('pool', dst[r0:r0 + 256].rearrange("r (a c) -> r a c", c=1024),
                    src[r0:r0 + 256].rearrange("r (a c) -> r a c", c=1024), [], [nm])


        par2 = sb("par2", [128, 8, 8], F32)
        for i, src in enumerate((rw_w0, rw_a0, rw_kk, rw_ka, rw_ka, rw_rk, rw_lng, rw_lnb)):
            DMA('sp', par2[:, :, i:i + 1], src.rearrange("o (q p) -> p q o", p=128), [], ['par2'],
                allow_slow_non_contiguous=True)
        TS(par2[:, :, 0:2], par2[:, :, 0:2], -1.0, None, ALU.mult, ALU.bypass, ['par2'], ['par2'])
        TS(par2[:, :, 4:5], par2[:, :, 4:5], -1.0, 1.0, ALU.mult, ALU.add, ['par2'], ['par2'])
        mu4 = sb("mu4", [128, 4, 8, 2], F32)
        for g4 in range(4):
            DMA('sp', mu4[:, g4, :, 0:1], mu[:, g4 * 1024:(g4 + 1) * 1024].rearrange("o (q p) -> p q o", p=128), [], ['mu4'],
                allow_slow_non_contiguous=True)
        TS(mu4[:, :, :, 1:2], mu4[:, :, :, 0:1], -1.0, 1.0, ALU.mult, ALU.add, ['mu4'], ['mu4'])
        mu_lo = sb("mu_lo", [128, 2], F32)
        DMA('sp', mu_lo[:, 0:1], mu[:, 4096:4224].rearrange("o (p q) -> p (o q)", q=1), [], ['mu_lo'],
            allow_slow_non_contiguous=True)
        TS(mu_lo[:, 1:2], mu_lo[:, 0:1], -1.0, 1.0, ALU.mult, ALU.add, ['mu_lo'], ['mu_lo'])
        nbf = sb("nbf", [16, 1], F32)
        DMA('sp', nbf[:], fox_bf.rearrange("o h -> h o"), [], ['nbf'], allow_slow_non_contiguous=True)
        TS(nbf[:], nbf[:], -1.0, None, ALU.mult, ALU.bypass, ['nbf'], ['nbf'])

        hTt = [sb("hTt%d" % i, [128, 16, 512], BF16) for i in range(2)]
        esl = ExitStack()
        loraT = sb("loraT", [128, T], BF16, esl)

        def hTw(hb, t4, half):
            return ('hTt', hb, t4, half)

        def hTr(hb):
            return [('hTt', hb, t4, half) for t4 in range(4) for half in range(2)]

        def shift_evac(bank, tmp, ktmp, mucol, omucol, cap, kcar, first):
            ACT(tmp[:], B[bank][:], AF.Identity, [Bn[bank]], [ktmp], scale=omucol)
            STT(tmp[:, 1:512], B[bank][:, 0:511], mucol, tmp[:, 1:512], ALU.mult, ALU.add, [Bn[bank], ktmp], [ktmp])
            if not first:
                STT(tmp[:, 0:1], cap, mucol, tmp[:, 0:1], ALU.mult, ALU.add, [kcar, ktmp], [ktmp])
            CP(cap, B[bank][:, 511:512], [Bn[bank]], [kcar])

        with ExitStack() as esp:
            grep = sb("grep", [128, D], F32, esp)
            DMA('sp', grep[:], norm_g.partition_broadcast(128), [], ['grep'])
            xt = [sb("xt%d" % i, [128, D], F32, esp) for i in range(2)]
            xnb1 = [sb("xnb%d" % i, [128, D], BF16, esp) for i in range(2)]
            st1 = [sb("st%d" % i, [128, 4], F32, esp) for i in range(2)]
            wlo = sb("wlo", [128, 16, 128], BF16, esp)
            DMA('sp', wlo[:], wb_lo.rearrange("(k p) n -> p k n", p=128), ['wb_lo'], ['wlo'])
            wfl = sb("wfl", [128, 16, 16], BF16, esp)
            DMA('sp', wfl[:], wb_fl.rearrange("(k p) n -> p k n", p=128), ['wb_fl'], ['wfl'])
            tmpLO = sb("tmpLO", [128, 512], F32, esp)
            carlo = sb("carlo", [128, 1], F32, esp)
            cS = sb("cS", [16, 512], F32, esp)
            cprev = sb("cprev", [16, 1], F32, esp)
            ct = [sb("ct%d" % i, [16, 512], F32, esp) for i in range(2)]
            c3 = sb("c3", [16, 3, 512], BF16, esp)
            n3 = sb("n3", [16, 3, 512], BF16, esp)
            onesr = sb("onesr", [16, 512], F32, esp)
            MSET(onesr[:], 1.0, [], ['onesr'], eng='dve')
            MSET(cprev[:], 0.0, [], ['cprev'], eng='dve')
            for mt in range(NM):
                hb = mt % 2
                ts_ = slice(mt * 512, (mt + 1) * 512)
                for t4 in range(4):
                    tt = mt * 4 + t4
                    i = tt % 2
                    kx, kn, ks = 'xt%d' % i, 'xnb%d' % i, 'st%d' % i
                    DMA('sp', xt[i][:], x[tt * 128:(tt + 1) * 128, :], [], [kx])
                    ACT(xnb1[i][:], xt[i][:], AF.Square, [kx], [kn, ks], accum_out=st1[i][:, 0:1])
                    TS(st1[i][:, 1:2], st1[i][:, 0:1], 1.0 / D, 1e-6, ALU.mult, ALU.add, [ks], [ks])
                    ACT(st1[i][:, 2:3], st1[i][:, 1:2], AF.Sqrt, [ks], [ks])
                    RCP(st1[i][:, 3:4], st1[i][:, 2:3], [ks], [ks])
                    STT(xnb1[i][:], xt[i][:], st1[i][:, 3:4], grep[:], ALU.mult, ALU.mult, [kx, ks, 'grep'], [kn])
                    bi = 2 * i
                    for j in range(16):
                        bk = bi + j // 8
                        TR(b16(bk)[:, (j % 8) * 128:(j % 8 + 1) * 128], xnb1[i][:, j * 128:(j + 1) * 128], ident[:],
                           [kn, 'ident'], [Bn[bk]], inc=(j % 8 == 7))
                    ACT(hTt[hb][:, 0:8, t4 * 128:(t4 + 1) * 128], b16(bi)[:].rearrange("p (j t) -> p j t", t=128),
                        AF.Identity, [Bn[bi]], [hTw(hb, t4, 0)])
                    CP(hTt[hb][:, 8:16, t4 * 128:(t4 + 1) * 128], b16(bi + 1)[:].rearrange("p (j t) -> p j t", t=128),
                       [Bn[bi + 1]], [hTw(hb, t4, 1)])
                DMA('sp', hT_d[mt], hTt[hb][:].rearrange("p k t -> p (k t)"), hTr(hb), [('hT_d', mt)])
                for kc in range(16):
                    MM(B[4][:], wlo[:, kc, :], hTt[hb][:, kc, :], kc == 0, kc == 15, ['wlo'] + hTr(hb), [Bn[4]], inc=(kc == 15))
                shift_evac(4, tmpLO, 'tmpLO', mu_lo[:, 0:1], mu_lo[:, 1:2], carlo[:, 0:1], 'carlo', mt == 0)
                ACT(loraT[0:64, ts_], tmpLO[0:64, :], AF.Tanh, ['tmpLO'], [('loraT', mt)])
                CP(loraT[64:128, ts_], tmpLO[64:128, :], ['tmpLO'], [('loraT', mt)])
                for kc in range(16):
                    MM(B[5][0:16, :], wfl[:, kc, :], hTt[hb][:, kc, :], kc == 0, kc == 15, ['wfl'] + hTr(hb), [Bn[5]], inc=(kc == 15))
                ACT(ct[0][:], B[5][0:16, :], AF.Exp, [Bn[5], 'nbf'], ['ct0'], scale=-1.0, bias=nbf[:, 0:1])
                ACT(ct[0][:], ct[0][:], AF.Ln, ['ct0'], ['ct0'], bias=1.0)
                SCAN(cS[:], onesr[:], ct[0][:], cprev[:, 0:1], ['onesr', 'ct0', 'cprev'], ['cS'])
                CP(cprev[:], cS[:, 511:512], ['cS'], ['cprev'])
                CP(c3[:, 0, :], cS[:], ['cS'], ['c3'])
                TT(ct[0][:], cS[:], c3[:, 0, :], ALU.subtract, ['cS', 'c3'], ['ct0'])
                CP(c3[:, 1, :], ct[0][:], ['ct0'], ['c3'])
                TT(ct[1][:], ct[0][:], c3[:, 1, :], ALU.subtract, ['ct0', 'c3'], ['ct1'])
                CP(c3[:, 2, :], ct[1][:], ['ct1'], ['c3'])
                TS(n3[:], c3[:], -1.0, None, ALU.mult, ALU.bypass, ['c3'], ['n3'])
                DMA('sp', caug_d[:, :, ts_], c3[:], ['c3'], ['caug_d'])
                DMA('sp', ncaug_d[:, :, ts_], n3[:], ['n3'], ['ncaug_d'])
        S.barrier(skip_pool=True)

        if True:
            with ExitStack() as e2:
                lup = sb("lup", [128, 1024], BF16, e2)
                DMA('sp', lup[0:64, :], wb_wl, ['wb_wl'], ['lup'])
                DMA('sp', lup[64:128, :], wb_al, ['wb_al'], ['lup'])
                Whg = sb("Whg", [128, 1, 16, 512], BF16, e2)
                H32a = sb("H32a", [128, 8, 64], F32, e2)
                Hba = sb("Hba", [128, 8, 64], BF16, e2)
                cara = sb("cara", [128, 8, 4], F32, e2)
                MSET(H32a[:], 0.0, [], ['H32a'], eng='dve')
                MSET(Hba[:], 0.0, [], ['Hba'], eng='dve')
                ctxs = []
                for t in range(1):
                    c = {'bo': 4 * t, 't': t}
                    n_ = lambda s_: "%s_%d" % (s_, t)
                    c['tmp'] = [sb(n_("tmp%d" % i), [128, 512], F32, e2) for i in range(4)]
                    c['sc'] = [sb(n_("sc%d" % i), [128, 512], F32, e2) for i in range(7)]
                    c['bonus'] = sb(n_("bonus"), [128, 512], F32, e2)
                    c['gs'] = sb(n_("gs"), [128, 512], F32, e2)
                    c['pc'] = sb(n_("pc"), [128, 4], F32, e2)
                    c['AR'] = sb(n_("AR"), [128, 2, 512], BF16, e2)
                    c['KB'] = sb(n_("KB"), [128, 2, 512], BF16, e2)
                    c['KH'] = sb(n_("KH"), [128, 3, 512], BF16, e2)
                    c['sqb'] = sb(n_("sqb"), [128, 512], BF16, e2)
                    c['MS'] = [sb(n_("MS%d" % i), [128, 4, 512], BF16, e2) for i in range(2)]
                    c['Yk'] = [sb(n_("Yk%d" % i), [128, 4, 128], BF16, e2) for i in range(2)]
                    c['YTk'] = [sb(n_("YTk%d" % i), [128, 4, 128], BF16, e2) for i in range(2)]
                    c['G'] = [sb(n_("G%d" % i), [128, 4, 128], BF16, e2) for i in range(2)]
                    c['TOK'] = [sb(n_("TOK%d" % i), [128, 4, 3, 64], BF16, e2) for i in range(2)]
                    c['Wsb'] = sb(n_("Wsb"), [128, 64], BF16, e2)
                    c['Un'] = sb(n_("Un"), [128, 64], BF16, e2)
                    c['ycat'] = sb(n_("ycat"), [128, 2, 512], BF16, e2)
                    c['yout'] = [sb(n_("yout%d" % i), [128, 512], BF16, e2) for i in range(2)]
                    c['n'] = 0
                    ctxs.append(c)

                def sigmoid_(dst, kd, src, ks, extra_r=(), **kw):
                    ACT(dst, src, AF.Exp, [ks] + list(extra_r), [kd], **kw)
                    ACT(dst, dst, AF.Ln, [kd], [kd], bias=1.0)
                    ACT(dst, dst, AF.Exp, [kd], [kd], scale=-1.0)

                def rw_pair(q, mt, hb, c):
                    t = c['t']
                    o = c['bo']
                    K_ = lambda s_: "%s_%d" % (s_, t)
                    Bk = lambda i: B[o + i]
                    Kb = lambda i: Bn[o + i]
                    ql = 0
                    ts_ = slice(mt * 512, (mt + 1) * 512)
                    hk = hTr(hb)
                    tmp, sc, bonus, gs, pc = c['tmp'], c['sc'], c['bonus'], c['gs'], c['pc']
                    AR, KB, KH, sqb, Yk, YTk = c['AR'], c['KB'], c['KH'], c['sqb'], c['Yk'], c['YTk']
                    Wsb, Un, ycat, yout = c['Wsb'], c['Un'], c['ycat'], c['yout']
                    sk = [K_("sc%d" % i) for i in range(7)]
                    kt = [K_("tmp%d" % i) for i in range(4)]
                    kAR, kKB, kKH = K_('AR'), K_('KB'), K_('KH')
                    P = lambda i: par2[:, q, i:i + 1]
                    first = (mt == 0)
                    for g in range(4):
                        for kc in range(16):
                            MM(Bk(g)[:], Whg[:, ql, kc, g * 128:(g + 1) * 128], hTt[hb][:, kc, :], kc == 0, kc == 15,
                               ['Whg'] + hk, [Kb(g)], inc=(kc == 15))
                    for g in range(4):
                        shift_evac(o + g, tmp[g], kt[g], mu4[:, g, q, 0:1], mu4[:, g, q, 1:2], cara[:, q, g:g + 1], ('car', q, g), first)
                    r_, k_, v_, g_ = tmp[0][:], tmp[1][:], tmp[2][:], tmp[3][:]
                    MM(Bk(0)[:], lup[0:64, q * 128:(q + 1) * 128], loraT[0:64, ts_], True, True, ['lup', ('loraT', mt)], [Kb(0)])
                    MM(Bk(1)[:], lup[64:128, q * 128:(q + 1) * 128], loraT[64:128, ts_], True, True, ['lup', ('loraT', mt)], [Kb(1)])
                    sigmoid_(gs[:], K_('gs'), g_, kt[3], scale=-1.0)
                    TT(gs[:], gs[:], g_, ALU.mult, [K_('gs'), kt[3]], [K_('gs')])
                    sigmoid_(sc[1][:], sk[1], Bk(0)[:], Kb(0), ['par2'], scale=-1.0, bias=P(0))
                    sigmoid_(sc[2][:], sk[2], Bk(1)[:], Kb(1), ['par2'], scale=-1.0, bias=P(1))
                    SCAN(sc[3][:], rmask[:], sc[1][:], 0.0, ['rmask', sk[1]], [sk[3]])
                    TT(sc[4][:], sc[3][:], sc[1][:], ALU.subtract, [sk[3], sk[1]], [sk[4]])
                    ACT(sc[1][:], sc[3][:], AF.Exp, [sk[3]], [sk[1]], scale=-C0)
                    ACT(sc[5][:], sc[3][:], AF.Exp, [sk[3]], [sk[5]], scale=C0)
                    ACT(sc[4][:], sc[4][:], AF.Exp, [sk[4]], [sk[4]], scale=-C0)
                    CP(pc[:], sc[1][:].rearrange("p (c t) -> p c t", t=128)[:, :, 127], [sk[1]], [K_('pc')])
                    TS(sc[3][:], k_, P(2), None, ALU.mult, ALU.bypass, [kt[1], 'par2'], [sk[3]])
                    TT(sqb[:], sc[3][:], sc[3][:], ALU.mult, [sk[3]], [K_('sqb')])
                    MM(Bk(2)[:], onesbd[:], sqb[:], True, True, ['onesbd', K_('sqb')], [Kb(2)])
                    TS(sc[6][:], Bk(2)[:], 1e-24, None, ALU.max, ALU.bypass, [Kb(2)], [sk[6]])
                    ACT(sc[6][:], sc[6][:], AF.Ln, [sk[6]], [sk[6]], scale=float(2.0 ** 40))
                    ACT(sc[6][:], sc[6][:], AF.Exp, [sk[6], 'cb'], [sk[6]], scale=-0.5, bias=cb[:, 0:1])
                    TT(sc[3][:], sc[3][:], sc[6][:], ALU.mult, [sk[3], sk[6]], [sk[3]])
                    TS(sc[6][:], sc[2][:], P(3), P(4), ALU.mult, ALU.add, [sk[2], 'par2'], [sk[6]])
                    TT(sc[6][:], sc[6][:], k_, ALU.mult, [sk[6], kt[1]], [sk[6]])
                    TT(sc[2][:], sc[3][:], sc[2][:], ALU.mult, [sk[3], sk[2]], [sk[2]])
                    TT(AR[:, 0, :], sc[3][:], sc[4][:], ALU.mult, [sk[3], sk[4]], [kAR])
                    TT(AR[:, 1, :], r_, sc[1][:], ALU.mult, [kt[0], sk[1]], [kAR])
                    TT(sc[3][:], r_, sc[6][:], ALU.mult, [kt[0], sk[6]], [sk[3]])
                    ACT(sqb[:], sc[3][:], AF.Identity, [sk[3], 'par2'], [K_('sqb')], scale=P(5))
                    MM(Bk(3)[:], onesbd[:], sqb[:], True, True, ['onesbd', K_('sqb')], [Kb(3)])
                    TT(bonus[:], Bk(3)[:], v_, ALU.mult, [Kb(3), kt[2]], [K_('bonus')])
                    TT(sc[4][:], sc[6][:], sc[5][:], ALU.mult, [sk[6], sk[5]], [sk[4]])
                    TT(sc[1][:], sc[2][:], sc[5][:], ALU.mult, [sk[2], sk[5], K_('pc')], [sk[1]])
                    ACT(KB[:, 0, :], sc[4][:], AF.Identity, [sk[4]], [kKB])
                    ACT(KB[:, 1, :], sc[1][:], AF.Identity, [sk[1]], [kKB])
                    pcb = pc[:].unsqueeze(2).to_broadcast([128, 4, 128])
                    TT(KH[:, 0, :].rearrange("p (c t) -> p c t", t=128), sc[4][:].rearrange("p (c t) -> p c t", t=128), pcb,
                       ALU.mult, [sk[4], K_('pc')], [kKH])
                    TT(KH[:, 1, :].rearrange("p (c t) -> p c t", t=128), sc[1][:].rearrange("p (c t) -> p c t", t=128), pcb,
                       ALU.mult, [sk[1], K_('pc')], [kKH])
                    CP(KH[:, 2, :], v_, [kt[2]], [kKH])
                    for hh in range(2):
                        hs = slice(hh * 64, (hh + 1) * 64)
                        pt = b16(o + hh)[:, 0:768].rearrange("p (c g k) -> p c g k", c=4, g=3)
                        for cc in range(4):
                            for g3 in range(3):
                                TR(pt[:, cc, g3, :], KH[hs, g3, cc * 128:(cc + 1) * 128], ident[hs, hs], [kKH, 'ident'], [Kb(hh)],
                                   inc=(cc == 3 and g3 == 2))
                        CP(c['TOK'][hh][:], pt, [Kb(hh)], [K_('TOK%d' % hh)])
                    for hh in range(2):
                        h = 2 * q + hh
                        hs = slice(hh * 64, (hh + 1) * 64)
                        MS, G, TOK = c['MS'][hh], c['G'][hh], c['TOK'][hh]
                        kMSn, kG, kTOK = K_('MS%d' % hh), K_('G%d' % hh), K_('TOK%d' % hh)
                        kH32, kHb = ('H32', h), ('Hb', h)
                        H32 = H32a[hs, q, :]
                        Hb = Hba[hs, q, :]
                        for cc in range(4):
                            cs = slice(cc * 128, (cc + 1) * 128)
                            MM(Bk(2)[:, 0:256].rearrange('p (a t) -> p a t', a=2), KB[hs, 0, cs], AR[hs, :, cs], True, True, [kKB, kAR], [Kb(2)], inc=False)
                            MM(Bk(2)[:, 256:512].rearrange('p (a t) -> p a t', a=2), KB[hs, 1, cs], AR[hs, :, cs], True, True, [kKB, kAR], [Kb(2)])
                            MM(Bk(1)[:, cs], AR[hs, 0, cs], KB[hs, 1, cs], True, True, [kKB, kAR], [Kb(1)])
                            TT(MS[:, cc, :], Bk(2)[:], mask512[:], ALU.mult, [Kb(2), 'mask512'], [(kMSn, cc)])
                        mskeys = [(kMSn, cc) for cc in range(4)]
                        X0 = MS[:, :, 256:384]
                        TT(YTk[0][:], Bk(1)[:].rearrange("p (c t) -> p c t", t=128), maskL[:], ALU.mult, [Kb(1), 'maskL'], [K_('YTk0')])
                        CP(Yk[0][:], X0, mskeys, [K_('Yk0')])
                        TT(G[:], X0, ident4[:], ALU.add, mskeys + ['ident4'], [kG])
                        for k in range(1, 7):
                            a_, b_ = (k - 1) % 2, k % 2
                            kYa, kYTa, kYb, kYTb = K_('Yk%d' % a_), K_('YTk%d' % a_), K_('Yk%d' % b_), K_('YTk%d' % b_)
                            if k <= 5:
                                for cc in range(4):
                                    MM(Bk(0)[:, cc * 128:(cc + 1) * 128], YTk[a_][:, cc, :], Yk[a_][:, cc, :], True, True, [kYa, kYTa], [Kb(0)], inc=(cc == 3))
                            for cc in range(4):
                                MM(Bk(1)[:, cc * 128:(cc + 1) * 128], Yk[a_][:, cc, :], YTk[a_][:, cc, :], True, True, [kYa, kYTa], [Kb(1)], inc=(cc == 3))
                            if k <= 5:
                                ACT(Yk[b_][:], Bk(0)[:].rearrange("p (c t) -> p c t", t=128), AF.Identity, [Kb(0)], [kYb])
                            CP(YTk[b_][:], Bk(1)[:].rearrange("p (c t) -> p c t", t=128), [Kb(1)], [kYTb])
                            for cc in range(4):
                                MM(Bk(2)[:, cc * 128:(cc + 1) * 128], YTk[b_][:, cc, :], G[:, cc, :], True, True, [kYTb, kG], [Kb(2)], inc=(cc == 3))
                            TT(G[:], Bk(2)[:].rearrange("p (c t) -> p c t", t=128), G[:], ALU.add, [Kb(2), kG], [kG])
                        for cc in range(4):
                            cs = slice(cc * 128, (cc + 1) * 128)
                            kMS = (kMSn, cc)
                            MM(Bk(3)[:, 0:64], AR[hs, 0, cs], Hb, True, False, [kAR, kHb], [Kb(3)], inc=False)
                            MM(Bk(3)[:, 0:64], MS[:, cc, 0:128], TOK[:, cc, 2, :], False, True, [kMS, kTOK], [Kb(3)])
                            ACT(Wsb[:], Bk(3)[:, 0:64], AF.Identity, [Kb(3)], [K_('Wsb')])
                            MM(Bk(3)[:, 64:128], G[:, cc, :], Wsb[:], True, True, [kG, K_('Wsb')], [Kb(3)])
                            ACT(Un[:], Bk(3)[:, 64:128], AF.Identity, [Kb(3)], [K_('Un')], scale=-1.0)
                            MM(Bk(4)[hs, cs], Hb, AR[hs, 1, cs], True, False, [kHb, kAR], [Kb(4)], inc=False)
                            MM(Bk(4)[hs, cs], TOK[:, cc, 2, :], MS[:, cc, 128:256], False, False, [kTOK, kMS], [Kb(4)], inc=False)
                            MM(Bk(4)[hs, cs], Un[:], MS[:, cc, 384:512], False, True, [K_('Un'), kMS], [Kb(4)])
                            MM(Bk(3)[hs, 128:192], TOK[:, cc, 0, :], TOK[:, cc, 2, :], True, False, [kTOK], [Kb(3)], inc=False)
                            MM(Bk(3)[hs, 128:192], TOK[:, cc, 1, :], Un[:], False, True, [kTOK, K_('Un')], [Kb(3)])
                            STT(H32, H32, pc[hs, cc:cc + 1], Bk(3)[hs, 128:192], ALU.mult, ALU.add, [kH32, K_('pc'), Kb(3)], [kH32])
                            ACT(Hb, H32, AF.Identity, [kH32], [kHb])
                    ACT(ycat[:, 0, :], Bk(4)[:], AF.Identity, [Kb(4)], [K_('ycat')])
                    ACT(ycat[:, 1, :], Bk(4)[:], AF.Square, [Kb(4)], [K_('ycat')])
                    MM(Bk(1)[:], onesbdm[:], ycat[:, 0, :], True, True, ['onesbdm', K_('ycat')], [Kb(1)])
                    MM(Bk(2)[:], onesbdm[:], ycat[:, 1, :], True, True, ['onesbdm', K_('ycat')], [Kb(2)])
                    ACT(sc[1][:], Bk(1)[:], AF.Identity, [Kb(1)], [sk[1]])
                    TT(sc[2][:], sc[1][:], sc[1][:], ALU.mult, [sk[1]], [sk[2]])
                    TT(sc[2][:], Bk(2)[:], sc[2][:], ALU.subtract, [Kb(2), sk[2]], [sk[2]])
                    ACT(sc[2][:], sc[2][:], AF.Ln, [sk[2], 'cb'], [sk[2]], bias=cb[:, 1:2])
                    ACT(sc[2][:], sc[2][:], AF.Exp, [sk[2]], [sk[2]], scale=-0.5)
                    TT(sc[3][:], Bk(4)[:], sc[1][:], ALU.subtract, [Kb(4), sk[1]], [sk[3]])
                    TT(sc[3][:], sc[3][:], sc[2][:], ALU.mult, [sk[3], sk[2]], [sk[3]])
                    ACT(sc[3][:], sc[3][:], AF.Identity, [sk[3], 'par2'], [sk[3]], scale=P(6), bias=P(7))
                    TT(sc[3][:], sc[3][:], bonus[:], ALU.add, [sk[3], K_('bonus')], [sk[3]])
                    yi = c['n'] % 2
                    c['n'] += 1
                    TT(yout[yi][:], sc[3][:], gs[:], ALU.mult, [sk[3], K_('gs')], [K_('yout%d' % yi)])
                    DMA('sp', yT_d[q * 128:(q + 1) * 128, ts_], yout[yi][:], [K_('yout%d' % yi)], [('yT_d', mt)])

                Wfg = sb("Wfg", [128, 2, 16, 256], BF16, e2)
                kTa = [sb("kTa%d" % i, [70, T], BF16, e2) for i in range(2)]
                Va = [sb("Va%d" % i, [128, T // 128, 65], BF16, e2) for i in range(2)]
                for i in range(2):
                    MSET(Va[i][:], 1.0, [], [('Va', i)])
                    MSET(kTa[i][64:70, :], 1.0, [], [('kTa', i)])
                fctx = []
                for t in range(1, 2):
                    c = {'bo': 4 * t, 't': t, 'n': 0}
                    n_ = lambda s_: "%s_f%d" % (s_, t)
                    c['qTa'] = [sb(n_("qTa%d" % i), [70, 512], BF16, e2) for i in range(2)]
                    for i in range(2):
                        MSET(c['qTa'][i][64:70, :], 1.0, [], [n_("qTa%d" % i)])
                    c['vb'] = sb(n_("vb"), [64, 512], BF16, e2)
                    c['gsf'] = sb(n_("gsf"), [64, 512], F32, e2)
                    c['PTt'] = [sb(n_("PTt%d" % i), [128, 512], BF16, e2) for i in range(2)]
                    c['rec'] = sb(n_("rec"), [65, 512], F32, e2)
                    c['fo'] = sb(n_("fo"), [64, 512], F32, e2)
                    c['yf'] = [sb(n_("yf%d" % i), [64, 512], BF16, e2) for i in range(2)]
                    fctx.append(c)

                def fox_unit(h, I, hb, c):
                    t = c['t']
                    o = c['bo']
                    K_ = lambda s_: "%s_f%d" % (s_, t)
                    fb = {0: 5, 1: 5, 2: 6, 3: 7}
                    Bk = lambda i: B[fb[i]]
                    Kb = lambda i: Bn[fb[i]]
                    hl = h % 2
                    ts_ = slice(I * 512, (I + 1) * 512)
                    hk = hTr(hb)
                    u = c['n']
                    c['n'] += 1
                    qT = c['qTa'][u % 2]
                    kq = K_("qTa%d" % (u % 2))
                    kT, kkT = kTa[hl], ('kTa', hl)
                    V_, kV = Va[hl], ('Va', hl)
                    vb, gsf, PTt, rec, fo, yf = c['vb'], c['gsf'], c['PTt'], c['rec'], c['fo'], c['yf']
                    for kc in range(16):
                        MM(Bk(0)[:], Wfg[:, hl, kc, 0:128], hTt[hb][:, kc, :], kc == 0, kc == 15, ['Wfg'] + hk, [Kb(0)], inc=(kc == 15))
                    ACT(qT[0:64, :], Bk(0)[0:64, :], AF.Identity, [Kb(0)], [kq], scale=0.125)
                    DMA('sp', qT[64:67, :], ncaug_d[h, :, ts_], ['ncaug_d'], [kq])
                    CP(kT[0:64, ts_], Bk(0)[64:128, :], [Kb(0)], [kkT])
                    for kc in range(16):
                        MM(Bk(1)[:], Wfg[:, hl, kc, 128:256], hTt[hb][:, kc, :], kc == 0, kc == 15, ['Wfg'] + hk, [Kb(1)], inc=(kc == 15))
                    CP(vb[:], Bk(1)[0:64, :], [Kb(1)], [K_('vb')])
                    CP(gsf[:], Bk(1)[64:128, :], [Kb(1)], [K_('gsf')])
                    ACT(fo[:], gsf[:], AF.Exp, [K_('gsf')], [K_('fo')], scale=-1.0)
                    ACT(fo[:], fo[:], AF.Ln, [K_('fo')], [K_('fo')], bias=1.0)
                    ACT(fo[:], fo[:], AF.Exp, [K_('fo')], [K_('fo')], scale=-1.0)
                    TT(gsf[:], gsf[:], fo[:], ALU.mult, [K_('gsf'), K_('fo')], [K_('gsf')], eng='dve')
                    pt = b16(5)[:, 0:256].rearrange("p (c k) -> p c k", c=4)
                    for cc in range(4):
                        TR(pt[:, cc, :], vb[:, cc * 128:(cc + 1) * 128], ident[0:64, 0:64], [K_('vb'), 'ident'], [Kb(0)], inc=(cc == 3))
                    CP(V_[:, I * 4:(I + 1) * 4, 0:64], pt, [Kb(0)], [kV])
                    nj = 4 * I + 4

                    def qk(j):
                        lo = max(0, j - 4 * I) * 128
                        bk = 2 + (j % 2)
                        diag = j >= 4 * I
                        MM(Bk(bk)[:, lo:512], kT[0:70, j * 128:(j + 1) * 128], qT[0:70, lo:512], True, not diag, [kkT, kq], [Kb(bk)], inc=not diag)
                        if diag:
                            MM(Bk(bk)[:, lo:lo + 128], ident[:], maskneg[:], False, True, ['ident', 'maskneg'], [Kb(bk)])
                    qk(0)
                    for j in range(nj):
                        lo = max(0, j - 4 * I) * 128
                        bk = 2 + (j % 2)
                        pi = j % 2
                        if j + 1 < nj:
                            qk(j + 1)
                        ACT(PTt[pi][:, lo:512], Bk(bk)[:, lo:512], AF.Exp, [Kb(bk)], [K_('PTt%d' % pi)])
                        MM(Bk(1)[0:65, lo:512], V_[:, j, :], PTt[pi][:, lo:512], j == 0, j == nj - 1, [kV, K_('PTt%d' % pi)], [Kb(1)], inc=(j == nj - 1))
                    ACT(rec[64:65, :], Bk(1)[64:65, :], AF.Ln, [Kb(1)], [K_('rec')])
                    ACT(rec[64:65, :], rec[64:65, :], AF.Exp, [K_('rec')], [K_('rec')], scale=-1.0)
                    MM(B[6][0:64, :], onesf[64:65, :], rec[64:65, :], True, True, ['onesf', K_('rec')], [Bn[6]])
                    TT(fo[:], Bk(1)[0:64, :], gsf[:], ALU.mult, [Kb(1), K_('gsf')], [K_('fo')])
                    yi = u % 2
                    TT(yf[yi][:], fo[:], B[6][0:64, :], ALU.mult, [K_('fo'), Bn[6]], [K_('yf%d' % yi)])
                    DMA('sp', yT_d[1024 + h * 64:1024 + (h + 1) * 64, ts_], yf[yi][:], [K_('yf%d' % yi)], [('yT_d', I)])


                for g8 in range(8):
                    DMA('sp', Whg[:, 0], wb_rw[:, g8, :].rearrange("(k p) n -> p k n", p=128), ['wb_rw'], ['Whg'])
                    for hl_ in range(2):
                        DMA('sp', Wfg[:, hl_], wb_fox[:, g8 * 2 + hl_, :].rearrange("(k p) n -> p k n", p=128), ['wb_fox'], ['Wfg'])
                        DMA('sp', kTa[hl_][67:70, :], caug_d[g8 * 2 + hl_], ['caug_d'], [('kTa', hl_)])
                    DMA('sp', hTt[0][:].rearrange("p k t -> p (k t)"), hT_d[0], [('hT_d', 0)], hTr(0))
                    for mt in range(NM):
                        hb = mt % 2
                        if mt + 1 < NM:
                            DMA('sp', hTt[1 - hb][:].rearrange("p k t -> p (k t)"), hT_d[mt + 1], [('hT_d', mt + 1)], hTr(1 - hb))

                        def thA(mt=mt, hb=hb, g8=g8):
                            rw_pair(g8, mt, hb, ctxs[0])

                        def thB(mt=mt, hb=hb, g8=g8):
                            for hh in range(2):
                                fox_unit(g8 * 2 + hh, mt, hb, fctx[0])
                        run_threads([thA, thB], stagger=0.0)
            S.barrier()
        esl.close()

        if do_post:
            with ExitStack() as e4:
                gp = sb("gp", [128, D], F32, e4)
                gf = sb("gf", [128, D], F32, e4)
                gn = sb("gn", [128, D], F32, e4)
                role = sb("role_sb", [128, 2], F32, e4)
                DMA('sp', gn[:], norm_g.partition_broadcast(128), [], ['gn'])
                DMA('sp', role[:], role_in, [], ['role'])
                DMA('sp', gp[:], ple_ng.partition_broadcast(128), [], ['gp'])
                DMA('sp', gf[:], fin_g.partition_broadcast(128), [], ['gf'])
                x1 = sb("x1", [128, 4, D], F32, e4)
                xnb = sb("xnb4", [128, D], BF16, e4)
                st = sb("st4", [128, 4], F32, e4)
                yTt = sb("yTt", [128, 16, 512], BF16, e4)
                mT = sb("mT", [128, 16, 512], BF16, e4)
                pT = sb("pT", [128, 2, 512], BF16, e4)
                pt32 = sb("pt32", [128, 256], F32, e4)
                ptb = sb("ptb", [128, 256], BF16, e4)
                wsm = [sb("wsm%d" % i, [128, 16, 128], BF16, e4) for i in range(7)]
                wbg = [sb("wbg%d" % i, [128, 16, 512], BF16, e4) for i in range(2)]
                wpp = sb("wpp", [128, 2, 512], BF16, e4)
                sg1 = [sb("sg1_%d" % i, [128, 512], F32, e4) for i in range(2)]
                sg2 = [sb("sg2_%d" % i, [128, 512], F32, e4) for i in range(2)]
                nsm = [0]
                nbg = [0]

                def rms_rstd(src_ap, ksrc):
                    ACT(xnb[:], src_ap, AF.Square, [ksrc], ['xnb4', 'st4'], accum_out=st[:, 0:1])
                    TS(st[:, 1:2], st[:, 0:1], 1.0 / D, 1e-6, ALU.mult, ALU.add, ['st4'], ['st4'])
                    ACT(st[:, 2:3], st[:, 1:2], AF.Sqrt, ['st4'], ['st4'])
                    RCP(st[:, 3:4], st[:, 2:3], ['st4'], ['st4'])

                def norm_T(src_ap, ksrc, gtile, kg, dstT, kdst, s_):
                    rms_rstd(src_ap, ksrc)
                    STT(xnb[:], src_ap, st[:, 3:4], gtile[:], ALU.mult, ALU.mult, [ksrc, 'st4', kg], ['xnb4'])
                    for j in range(16):
                        bk = j // 8
                        TR(b16(bk)[:, (j % 8) * 128:(j % 8 + 1) * 128], xnb[:, j * 128:(j + 1) * 128], ident[:],
                           ['xnb4', 'ident'], [Bn[bk]], inc=(j % 8 == 7))
                    ACT(dstT[:, 0:8, s_ * 128:(s_ + 1) * 128], b16(0)[:].rearrange("p (j t) -> p j t", t=128),
                        AF.Identity, [Bn[0]], kdst[0:8])
                    CP(dstT[:, 8:16, s_ * 128:(s_ + 1) * 128], b16(1)[:].rearrange("p (j t) -> p j t", t=128),
                       [Bn[1]], kdst[8:16])

                NH = NM // 2
                for TT_ in range(NH):
                    ts_ = slice(TT_ * 512, (TT_ + 1) * 512)
                    tsB = slice((TT_ + NH) * 512, (TT_ + NH + 1) * 512)
                    hb = 0
                    hT4 = hTt[0]
                    yB = hTt[1]
                    DMA('sp', yTt[:], yT_d[:, ts_].rearrange("(k p) t -> p k t", p=128), [('yT_d', TT_)], ['yTt'])
                    DMA('sp', yB[:], yT_d[:, tsB].rearrange("(k p) t -> p k t", p=128), [('yT_d', TT_ + NH)], hTr(1))
                    yf_ = yTt[:].rearrange("p k t -> p (k t)")
                    TS(yf_, yf_, role[:, 0:1], None, ALU.mult, ALU.bypass, ['yTt', 'role'], ['yTt'])
                    STT(yf_, yB[:].rearrange("p k t -> p (k t)"), role[:, 1:2], yf_, ALU.mult, ALU.add, hTr(1) + ['yTt', 'role'], ['yTt'])
                    for s_ in range(4):
                        tok = slice(TT_ * 512 + s_ * 128, TT_ * 512 + (s_ + 1) * 128)
                        DMA('sp', x1[:, s_, :], x_post[tok, :], [], [('x1', s_)])
                        norm_T(x1[:, s_, :], ('x1', s_), gn, 'gn', hT4, [hTw(0, s_, 0)] * 8 + [hTw(0, s_, 1)] * 8, s_)
                        DMA('sp', pt32[:], p_post[tok, :], [], ['pt32'])
                        CP(ptb[:], pt32[:], ['pt32'], ['ptb'])
                        for j in range(2):
                            TR(b16(2)[:, j * 128:(j + 1) * 128], ptb[:, j * 128:(j + 1) * 128], ident[:], ['ptb', 'ident'],
                               [Bn[2]], inc=(j == 1))
                        CP(pT[:, :, s_ * 128:(s_ + 1) * 128], b16(2)[:, 0:256].rearrange("p (j t) -> p j t", t=128),
                           [Bn[2]], ['pT'])
                    for oc in range(16):
                        ocs = slice(oc * 128, (oc + 1) * 128)
                        par_ = oc % 2
                        w = []
                        for (src, nk, kk_) in ((wb_uprw[:, ocs], 8, 'wb_uprw'), (wb_upfox[:, ocs], 8, 'wb_upfox'),
                                               (wb_gate[:, ocs], 16, 'wb_gate'),
                                               (wb_gate[:, 2048 + oc * 128:2048 + (oc + 1) * 128], 16, 'wb_gate')):
                            wi = nsm[0] % 7
                            nsm[0] += 1
                            DMA('sp', wsm[wi][:, 0:nk, :], src.rearrange("(k p) n -> p k n", p=128), [kk_], ['wsm%d' % wi])
                            w.append(wi)
                        bb = 4 * par_
                        for kc in range(8):
                            MM(B[bb][:], wsm[w[0]][:, kc, :], yTt[:, kc, :], kc == 0, kc == 7, ['wsm%d' % w[0], 'yTt'], [Bn[bb]], inc=(kc == 7))
                        for kc in range(8):
                            MM(B[bb + 1][:], wsm[w[1]][:, kc, :], yTt[:, 8 + kc, :], kc == 0, kc == 7, ['wsm%d' % w[1], 'yTt'], [Bn[bb + 1]], inc=(kc == 7))
                        for kc in range(16):
                            MM(B[bb + 2][:], wsm[w[2]][:, kc, :], hT4[:, kc, :], kc == 0, kc == 15, ['wsm%d' % w[2]] + hTr(hb), [Bn[bb + 2]], inc=(kc == 15))
                        for kc in range(16):
                            MM(B[bb + 3][:], wsm[w[3]][:, kc, :], hT4[:, kc, :], kc == 0, kc == 15, ['wsm%d' % w[3]] + hTr(hb), [Bn[bb + 3]], inc=(kc == 15))
                        k1, k2 = 'sg1_%d' % par_, 'sg2_%d' % par_
                        ACT(sg1[par_][:], B[bb + 2][:], AF.Sigmoid, [Bn[bb + 2]], [k1])
                        ACT(sg2[par_][:], B[bb + 3][:], AF.Sigmoid, [Bn[bb + 3]], [k2])
                        TT(sg1[par_][:], B[bb][:], sg1[par_][:], ALU.mult, [Bn[bb], k1], [k1])
                        TT(sg2[par_][:], B[bb + 1][:], sg2[par_][:], ALU.mult, [Bn[bb + 1], k2], [k2])
                        TT(mT[:, oc, :], sg1[par_][:], sg2[par_][:], ALU.add, [k1, k2], [('mT', oc)], eng='pool')
                    mTk = [('mT', oc) for oc in range(16)]
                    for cg in range(4):
                        cgs = slice(cg * 512, (cg + 1) * 512)
                        wi = nbg[0] % 2
                        nbg[0] += 1
                        DMA('sp', wbg[wi][:], wb_out[:, cgs].rearrange("(k p) n -> p k n", p=128), ['wb_out'], ['wbg%d' % wi])
                        for s_ in range(4):
                            bk = s_ % 4
                            for kc in range(16):
                                MM(B[bk][:], mT[:, kc, s_ * 128:(s_ + 1) * 128], wbg[wi][:, kc, :], kc == 0, kc == 15,
                                   mTk + ['wbg%d' % wi], [Bn[bk]], inc=(kc == 15))
                            TT(x1[:, s_, cgs], x1[:, s_, cgs], B[bk][:], ALU.add, [('x1', s_), Bn[bk]], [('x1', s_)])
                    for s_ in range(4):
                        norm_T(x1[:, s_, :], ('x1', s_), gp, 'gp', mT, mTk, s_)
                    for cg in range(4):
                        cgs = slice(cg * 512, (cg + 1) * 512)
                        wi = nbg[0] % 2
                        nbg[0] += 1
                        DMA('sp', wbg[wi][:], wb_pg[:, cgs].rearrange("(k p) n -> p k n", p=128), ['wb_pg'], ['wbg%d' % wi])
                        DMA('sp', wpp[:], wb_pp[:, cgs].rearrange("(k p) n -> p k n", p=128), ['wb_pp'], ['wpp'])
                        for s_ in range(4):
                            bk = 2 + (s_ % 2)
                            bp = 4 + (s_ % 2)
                            ks_ = 'sg1_%d' % (s_ % 2)
                            for kc in range(16):
                                MM(B[bk][:], mT[:, kc, s_ * 128:(s_ + 1) * 128], wbg[wi][:, kc, :], kc == 0, kc == 15,
                                   mTk + ['wbg%d' % wi], [Bn[bk]], inc=(kc == 15))
                            for kc in range(2):
                                MM(B[bp][:], pT[:, kc, s_ * 128:(s_ + 1) * 128], wpp[:, kc, :], kc == 0, kc == 1,
                                   ['pT', 'wpp'], [Bn[bp]], inc=(kc == 1))
                            ACT(sg1[s_ % 2][:], B[bk][:], AF.Sigmoid, [Bn[bk]], [ks_])
                            TT(sg1[s_ % 2][:], B[bp][:], sg1[s_ % 2][:], ALU.mult, [Bn[bp], ks_], [ks_])
                            TT(x1[:, s_, cgs], x1[:, s_, cgs], sg1[s_ % 2][:], ALU.add, [('x1', s_), ks_], [('x1', s_)], eng='pool')
                    for s_ in range(4):
                        tok = slice(TT_ * 512 + s_ * 128, TT_ * 512 + (s_ + 1) * 128)
                        rms_rstd(x1[:, s_, :], ('x1', s_))
                        STT(x1[:, s_, :], x1[:, s_, :], st[:, 3:4], gf[:], ALU.mult, ALU.mult, [('x1', s_), 'st4', 'gf'], [('x1', s_)])
                        DMA('sp', out[tok, :], x1[:, s_, :], [('x1', s_)], ['out'])
        S.barrier(final=True)
        with nc.Block() as block:
            S.emit(block)
    return nc

_CACHE = {}


def _prep(inputs, b, T, r=0):
    f = lambda a: np.ascontiguousarray(np.asarray(a, dtype=np.float32))
    H = T // 2
    role = np.zeros((128, 2), np.float32)
    role[:, r] = 1.0
    m = {
        "x": f(inputs["x"][b, :T]),
        "p": f(inputs["p"][0, b, :T]),
        "x_post": f(inputs["x"][b, r * H:(r + 1) * H]),
        "p_post": f(inputs["p"][0, b, r * H:(r + 1) * H]),
        "role": role,
        "norm_g": f(inputs["norm_g"][0:1]),
        "w_in": f(inputs["w_in"][0]),
        "rw_shift_mu": f(inputs["rw_shift_mu"][0:1]),
        "rw_w0": f(inputs["rw_w0"][0:1]),
        "rw_w_lora_up": f(inputs["rw_w_lora_up"][0]),
        "rw_a0": f(inputs["rw_a0"][0:1]),
        "rw_a_lora_up": f(inputs["rw_a_lora_up"][0]),
        "rw_k_k": f(inputs["rw_k_k"][0:1]),
        "rw_k_a": f(inputs["rw_k_a"][0:1]),
        "rw_r_k": f(np.asarray(inputs["rw_r_k"][0]).reshape(1, 1024)),
        "rw_ln_g": f(inputs["rw_ln_g"][0:1]),
        "rw_ln_b": f(inputs["rw_ln_b"][0:1]),
        "fox_b_f": f(inputs["fox_b_f"][0:1]),
        "w_up_rwkv": f(inputs["w_up_rwkv"][0]),
        "w_up_fox": f(inputs["w_up_fox"][0]),
        "w_out": f(inputs["w_out"][0]),
        "ple_proj": f(inputs["ple_proj"][0]),
        "ple_gate_w": f(inputs["ple_gate_w"][0]),
        "ple_norm_g": f(inputs["ple_norm_g"][0:1]),
        "final_norm_g": f(np.asarray(inputs["final_norm_g"]).reshape(1, D)),
    }
    return m


def kernel(**inputs):
    T = 4096
    if T not in _CACHE:
        _CACHE[T] = build(T)
    nc = _CACHE[T]
    in_maps = [_prep(inputs, c % 4, T, c // 4) for c in range(8)]
    res = run_bass_kernel_spmd(nc, in_maps, core_ids=list(range(8)))
    H = T // 2
    out = np.empty((4, T, D), np.float32)
    for c in range(8):
        out[c % 4, (c // 4) * H:(c // 4 + 1) * H] = np.asarray(res.results[c]["out"], dtype=np.float32)
    return out
```

```python
import numpy as np
import concourse.bass as bass
import concourse.mybir as mybir
from concourse.bass_utils import run_bass_kernel_spmd
from contextlib import ExitStack

F32 = mybir.dt.float32
BF16 = mybir.dt.bfloat16
AF = mybir.ActivationFunctionType
ALU = mybir.AluOpType

ENGS = ('pe', 'act', 'dve', 'pool', 'sp')
NDSEM = 8
D = 2048
NIN = 12432
C0 = 0.6065306597126334
NEG = -30000.0


class Sched:
    def __init__(self, nc, es):
        self.nc = nc
        self.q = {e: [] for e in ENGS}
        self.cnt = {e: 0 for e in ENGS}
        self.seen = {e: {} for e in ENGS}
        self.state = {}
        self.dma_n = {e: 0 for e in ENGS}
        self.sem = {}
        for e in ('pe', 'act', 'dve', 'pool'):
            self.sem[e] = es.enter_context(nc.semaphore('s_' + e))
        for e in ('sp', 'pool', 'act'):
            for i in range(NDSEM):
                self.sem['d%s%d' % (e, i)] = es.enter_context(nc.semaphore('sd_%s%d' % (e, i)))

    def _deps(self, eng, reads, writes):
        deps = {}

        def need(sv):
            if sv is None:
                return
            s, v = sv
            if eng == 'pe' and s == 'pe':
                return
            if deps.get(s, 0) < v:
                deps[s] = v
        for k in reads:
            st = self.state.get(k)
            if st:
                need(st[0])
        for k in writes:
            st = self.state.get(k)
            if st:
                need(st[0])
                for s, v in st[1].items():
                    need((s, v))
        out = []
        for s, v in deps.items():
            if self.seen[eng].get(s, 0) < v:
                self.seen[eng][s] = v
                out.append((s, v))
        return out

    def _commit(self, token, reads, writes):
        for k in writes:
            self.state[k] = [token, {}]
        s, v = token
        for k in reads:
            st = self.state.setdefault(k, [None, {}])
            if st[1].get(s, 0) < v:
                st[1][s] = v

    def op(self, eng, fn, reads=(), writes=(), inc=True):
        waits = self._deps(eng, reads, writes)
        if inc:
            self.cnt[eng] += 1
            token = (eng, self.cnt[eng])
        else:
            token = (eng, self.cnt[eng] + 1)
        self.q[eng].append((fn, waits, (eng, 1) if inc else None))
        self._commit(token, reads, writes)

    def dma(self, q, fn, reads=(), writes=()):
        n = self.dma_n[q]
        self.dma_n[q] += 1
        slot, gen = n % NDSEM, n // NDSEM
        sn = 'd%s%d' % (q, slot)
        waits = self._deps(q, reads, writes)
        if gen > 0 and self.seen[q].get(sn, 0) < 16 * gen:
            self.seen[q][sn] = 16 * gen
            waits.append((sn, 16 * gen))
        token = (sn, 16 * (gen + 1))
        self.q[q].append((fn, waits, (sn, 16)))
        self._commit(token, reads, writes)

    def _allvals(self):
        vals = {e: self.cnt[e] for e in ('pe', 'act', 'dve', 'pool')}
        for q in ('sp', 'pool', 'act'):
            n = self.dma_n[q]
            for slot in range(min(n, NDSEM)):
                gens = (n - slot + NDSEM - 1) // NDSEM
                vals['d%s%d' % (q, slot)] = 16 * gens
        return vals

    def barrier(self, skip_pool=False, final=False):
        vals = self._allvals()
        if skip_pool:
            vals = {k: v for k, v in vals.items() if not k.startswith('dpool')}
        for e in ENGS:
            waits = []
            for s, v in vals.items():
                if e == 'pe' and s == 'pe':
                    continue
                if v > 0 and self.seen[e].get(s, 0) < v:
                    self.seen[e][s] = v
                    waits.append((s, v))
            self.q[e].append((None, waits, None))

    def emit(self, block):
        sem = self.sem

        def run(name, e):
            for fn, waits, inc in self.q[name]:
                for s, v in waits:
                    e.wait_ge(sem[s], v)
                if fn is None:
                    continue
                ins = fn(e)
                if inc is not None:
                    ins.then_inc(sem[inc[0]], inc[1])

        @block.tensor
        def _(e):
            run('pe', e)

        @block.scalar
        def _(e):
            run('act', e)

        @block.vector
        def _(e):
            run('dve', e)

        @block.gpsimd
        def _(e):
            run('pool', e)

        @block.sync
        def _(e):
            run('sp', e)


def build(T, dbg=False, do_rw=True, do_fox=True, do_post=True):
    nc = bass.Bass("TRN2", target_bir_lowering=False)
    NT = T // 128
    NM = T // 512

    def din(name, shape):
        return nc.dram_tensor(name, shape, F32, kind="ExternalInput").ap()
    x = din("x", [T, D])
    p = din("p", [T, 256])
    norm_g = din("norm_g", [1, D])
    w_in = din("w_in", [D, NIN])
    mu = din("rw_shift_mu", [1, 4224])
    rw_w0 = din("rw_w0", [1, 1024])
    rw_wl = din("rw_w_lora_up", [64, 1024])
    rw_a0 = din("rw_a0", [1, 1024])
    rw_al = din("rw_a_lora_up", [64, 1024])
    rw_kk = din("rw_k_k", [1, 1024])
    rw_ka = din("rw_k_a", [1, 1024])
    rw_rk = din("rw_r_k", [1, 1024])
    rw_lng = din("rw_ln_g", [1, 1024])
    rw_lnb = din("rw_ln_b", [1, 1024])
    fox_bf = din("fox_b_f", [1, 16])
    w_uprw = din("w_up_rwkv", [1024, D])
    w_upfox = din("w_up_fox", [1024, D])
    w_out = din("w_out", [D, D])
    ple_proj = din("ple_proj", [256, D])
    ple_gw = din("ple_gate_w", [D, D])
    ple_ng = din("ple_norm_g", [1, D])
    fin_g = din("final_norm_g", [1, D])
    x_post = din("x_post", [T // 2, D])
    p_post = din("p_post", [T // 2, 256])
    role_in = din("role", [128, 2])
    out = nc.dram_tensor("out", [T // 2, D], F32, kind="ExternalOutput").ap()

    def dsc(name, shape, dt=BF16, ext=False):
        if ext:
            return nc.dram_tensor(name, shape, dt, kind="ExternalOutput").ap()
        return nc.dram_tensor(name, shape, dt).ap()
    wb_rw = dsc("wb_rw", [D, 8, 512])
    wb_lo = dsc("wb_lo", [D, 128])
    wb_fox = dsc("wb_fox", [D, 16, 256])
    wb_fl = dsc("wb_fl", [D, 16])
    wb_gate = dsc("wb_gate", [D, 4096])
    wb_uprw = dsc("wb_uprw", [1024, D])
    wb_upfox = dsc("wb_upfox", [1024, D])
    wb_out = dsc("wb_out", [D, D])
    wb_pp = dsc("wb_pp", [256, D])
    wb_pg = dsc("wb_pg", [D, D])
    wb_wl = dsc("wb_wl", [64, 1024])
    wb_al = dsc("wb_al", [64, 1024])
    yT_d = dsc("yT_d", [2048, T], BF16, ext=dbg)
    hT_d = dsc("hT_d", [T // 512, 128, 16 * 512])
    caug_d = dsc("caug_d", [16, 3, T])
    ncaug_d = dsc("ncaug_d", [16, 3, T])

    with ExitStack() as es:
        S = Sched(nc, es)

        def sb(name, shape, dt, st=es):
            return st.enter_context(nc.sbuf_tensor(name, shape, dt))

        cur = [None]

        def fsz(ap):
            n = 1
            for d_ in ap.shape[1:]:
                n *= int(d_)
            return n

        def OP(eng, fn, r, w, inc=True, cost=300.0):
            if cur[0] is None:
                S.op(eng, fn, r, w, inc)
            else:
                cur[0].append((0, eng, fn, tuple(r), tuple(w), inc, cost))

        def DM(q, fn, r, w):
            if cur[0] is None:
                S.dma(q, fn, r, w)
            else:
                cur[0].append((1, q, fn, tuple(r), tuple(w), True, 60.0))

        DMA_LAT = 3500.0
        XLAT = 400.0

        def run_threads(fns, stagger=0.0):
            import heapq
            lists = []
            for f in fns:
                cur[0] = []
                f()
                lists.append(cur[0])
                cur[0] = None
            th = []
            for L in lists:
                atoms, a = [], []
                for it in L:
                    a.append(it)
                    if it[5]:
                        atoms.append(a)
                        a = []
                assert not a
                n = len(atoms)
                eng = [at[0][1] for at in atoms]
                dur = [sum(it[6] for it in at) for at in atoms]
                isd = [at[0][0] == 1 for at in atoms]
                deps = [set() for _ in range(n)]
                state = {}
                for i, at in enumerate(atoms):
                    for it in at:
                        for k in it[3]:
                            st = state.get(k)
                            if st and st[0] is not None:
                                deps[i].add(st[0])
                        for k in it[4]:
                            st = state.get(k)
                            if st:
                                if st[0] is not None:
                                    deps[i].add(st[0])
                                deps[i].update(st[1])
                    for it in at:
                        for k in it[4]:
                            state[k] = [i, set()]
                        for k in it[3]:
                            state.setdefault(k, [None, set()])[1].add(i)
                    deps[i].discard(i)
                succ = [[] for _ in range(n)]
                for i in range(n):
                    for d_ in deps[i]:
                        succ[d_].append(i)
                th.append({'atoms': atoms, 'eng': eng, 'dur': dur, 'isd': isd, 'deps': deps, 'succ': succ,
                           'indeg': [len(deps[i]) for i in range(n)], 'fin': [0.0] * n, 'rdy': [0.0] * n})
            heaps = {}
            for ti, t_ in enumerate(th):
                for i in range(len(t_['atoms'])):
                    if t_['indeg'][i] == 0:
                        t_['rdy'][i] = ti * stagger
                        heapq.heappush(heaps.setdefault(t_['eng'][i], []), (t_['rdy'][i], i, ti))
            efree = {}
            order = []
            total = sum(len(t_['atoms']) for t_ in th)
            while total:
                best = None
                for e, hp in heaps.items():
                    if not hp:
                        continue
                    ef = efree.get(e, 0.0)
                    cand = hp[0]
                    if cand[0] <= ef:
                        k_ = min((c_ for c_ in hp if c_[0] <= ef), key=lambda c_: (c_[2], c_[1]))
                        st_ = ef
                    else:
                        k_ = cand
                        st_ = cand[0]
                    if best is None or st_ < best[0]:
                        best = (st_, e, k_)
                assert best is not None
                st_, e, k_ = best
                hp = heaps[e]
                hp.remove(k_)
                heapq.heapify(hp)
                _, i, ti = k_
                t_ = th[ti]
                efree[e] = st_ + t_['dur'][i]
                t_['fin'][i] = st_ + (DMA_LAT if t_['isd'][i] else t_['dur'][i])
                total -= 1
                order.append(t_['atoms'][i])
                for j in t_['succ'][i]:
                    lat = 60.0 if t_['eng'][j] == e else XLAT
                    if t_['fin'][i] + lat > t_['rdy'][j]:
                        t_['rdy'][j] = t_['fin'][i] + lat
                    t_['indeg'][j] -= 1
                    if t_['indeg'][j] == 0:
                        heapq.heappush(heaps.setdefault(t_['eng'][j], []), (t_['rdy'][j], j, ti))
            for at in order:
                for it in at:
                    if it[0] == 0:
                        S.op(it[1], it[2], it[3], it[4], it[5])
                    else:
                        S.dma(it[1], it[2], it[3], it[4])

        def ACT(out_, in_, func, r, w, **kw):
            OP('act', lambda e: e.activation(out=out_, in_=in_, func=func, **kw), r, w, cost=220.0 + fsz(out_) / 1.4)

        def TT(out_, a, b, op, r, w, eng='dve'):
            OP(eng, lambda e: e.tensor_tensor(out=out_, in0=a, in1=b, op=op), r, w,
               cost=(70.0 + fsz(out_) * 1.1) * (3.0 if eng == 'pool' else 1.0))

        def TS(out_, a, s1, s2, op0, op1, r, w, eng='dve'):
            OP(eng, lambda e: e.tensor_scalar(out=out_, in0=a, scalar1=s1, scalar2=s2, op0=op0, op1=op1), r, w,
               cost=70.0 + fsz(out_) * 0.75)

        def STT(out_, a, sc_, b, op0, op1, r, w):
            OP('dve', lambda e: e.scalar_tensor_tensor(out=out_, in0=a, scalar=sc_, in1=b, op0=op0, op1=op1), r, w,
               cost=70.0 + fsz(out_) * 1.1)

        def CP(out_, in_, r, w, eng='dve'):
            OP(eng, lambda e: e.tensor_copy(out=out_, in_=in_), r, w,
               cost=(70.0 + fsz(out_) * 1.0) * (3.0 if eng == 'pool' else 1.0))

        def MM(out_, lhsT, rhs, start, stop, r, w, inc=True):
            n_ = max(64, fsz(rhs))
            c_ = n_ / 1.2 * (4.0 if rhs.dtype == F32 else 1.0) + 20.0
            OP('pe', lambda e: e.matmul(out_, lhsT=lhsT, rhs=rhs, start=start, stop=stop), r, w, inc=inc, cost=c_)

        def TR(out_, in_, ident_, r, w, inc=True):
            OP('pe', lambda e: e.transpose(out_, in_, ident_), r, w, inc=inc, cost=70.0)

        def DMA(q, out_, in_, r, w, **kw):
            DM(q, lambda e: e.dma_start(out=out_, in_=in_, **kw), r, w)

        def RCP(out_, in_, r, w):
            OP('dve', lambda e: e.reciprocal(out=out_, in_=in_), r, w, cost=100.0 + fsz(out_) * 5.5)

        def SCAN(out_, d0, d1, init, r, w):
            OP('dve', lambda e: e.tensor_tensor_scan(out=out_, data0=d0, data1=d1, initial=init,
                                                    op0=ALU.mult, op1=ALU.add), r, w, cost=70.0 + fsz(out_) * 2.1)

        def MSET(ap, val, r, w, eng='pool'):
            OP(eng, lambda e: e.memset(ap, val), r, w, cost=100.0 + fsz(ap) * 0.5)

        B = [es.enter_context(nc.psum_tensor("B%d" % i, [128, 512], F32)) for i in range(8)]
        Bn = ["B%d" % i for i in range(8)]

        def b16(i):
            return B[i][:].bitcast(BF16)

        DMA('pool', wb_lo, w_in[:, 4096:4224], [], ['wb_lo'])
        DMA('pool', wb_fl, w_in[:, 8320:8336], [], ['wb_fl'])
        DMA('pool', wb_wl, rw_wl, [], ['wb_wl'])
        DMA('pool', wb_al, rw_al, [], ['wb_al'])
        identf = sb("identf", [128, 128], F32)
        ident = sb("ident", [128, 128], BF16)
        mask512 = sb("mask512", [128, 512], BF16)
        maskL = sb("maskL", [128, 4, 128], BF16)
        ident4 = sb("ident4", [128, 4, 128], BF16)
        maskneg = sb("maskneg", [128, 128], BF16)
        ones64 = sb("ones64", [64, 64], BF16)
        onesm = sb("onesm", [64, 64], BF16)
        onesf = sb("onesf", [128, 64], F32)
        rmask = sb("rmask", [128, 512], F32)
        onesbd = sb("onesbd", [128, 128], BF16)
        onesbdm = sb("onesbdm", [128, 128], BF16)
        cb = sb("cb", [128, 2], F32)
        mtmp = sb("mtmp", [128, 128], F32)

        def asel(tile_ap, cmp, base, cm, pat, fill, key):
            S.op('pool', lambda e: e.affine_select(out=tile_ap, in_=tile_ap, pattern=pat, compare_op=cmp,
                                                   fill=fill, base=base, channel_multiplier=cm), [key], [key])
        MSET(identf[:], 1.0, [], ['identf'])
        asel(identf[:], ALU.is_equal, 0, 1, [[-1, 128]], 0.0, 'identf')
        CP(ident[:], identf[:], ['identf'], ['ident'])
        for c in range(4):
            CP(ident4[:, c, :], identf[:], ['identf'], ['ident4'])
        MSET(mtmp[:], 1.0, [], ['mtmp'])
        asel(mtmp[:], ALU.is_gt, 0, -1, [[1, 128]], 0.0, 'mtmp')
        CP(mask512[:, 0:128], mtmp[:], ['mtmp'], ['mask512'])
        TS(mask512[:, 256:384], mtmp[:], -1.0, None, ALU.mult, ALU.bypass, ['mtmp'], ['mask512'])
        MSET(mtmp[:], 1.0, ['mtmp'], ['mtmp'])
        asel(mtmp[:], ALU.is_ge, 0, -1, [[1, 128]], 0.0, 'mtmp')
        CP(mask512[:, 128:256], mtmp[:], ['mtmp'], ['mask512'])
        CP(mask512[:, 384:512], mtmp[:], ['mtmp'], ['mask512'])
        TS(maskneg[:], mtmp[:], -1.0, -NEG, ALU.add, ALU.mult, ['mtmp'], ['maskneg'])
        MSET(mtmp[:], -1.0, ['mtmp'], ['mtmp'])
        asel(mtmp[:], ALU.is_gt, 0, 1, [[-1, 128]], 0.0, 'mtmp')
        for c in range(4):
            CP(maskL[:, c, :], mtmp[:], ['mtmp'], ['maskL'])
        MSET(ones64[:], 1.0, [], ['ones64'])
        MSET(onesm[:], 1.0 / 64.0, [], ['onesm'])
        MSET(onesf[:], 1.0, [], ['onesf'])
        MSET(onesbd[:], 0.0, [], ['onesbd'])
        MSET(onesbdm[:], 0.0, [], ['onesbdm'])
        for hh_ in range(2):
            hs_ = slice(hh_ * 64, (hh_ + 1) * 64)
            MSET(onesbd[hs_, hs_], 1.0, ['onesbd'], ['onesbd'])
            MSET(onesbdm[hs_, hs_], 1.0 / 64.0, ['onesbdm'], ['onesbdm'])
        MSET(cb[:, 0:1], 13.862943611198906, [], ['cb'])
        MSET(cb[:, 1:2], 64e-5, ['cb'], ['cb'])
        MSET(rmask[:], 1.0, [], ['rmask'])
        MSET(rmask[:].rearrange("p (c t) -> p c t", t=128)[:, :, 0:1], 0.0, ['rmask'], ['rmask'])

        for rg in range(16):
            rs = slice(rg * 128, (rg + 1) * 128)
            for g4 in range(4):
                DMA('pool', wb_rw[rs, :, g4 * 128:(g4 + 1) * 128],
                    w_in[rs, g4 * 1024:(g4 + 1) * 1024].rearrange("r (q c) -> r q c", c=128), [], ['wb_rw'])
        for rg in range(16):
            rs = slice(rg * 128, (rg + 1) * 128)
            for g4 in range(4):
                DMA('pool', wb_fox[rs, :, g4 * 64:(g4 + 1) * 64],
                    w_in[rs, 4224 + g4 * 1024:4224 + (g4 + 1) * 1024].rearrange("r (h c) -> r h c", c=64), [], ['wb_fox'])
        for rg in range(16):
            rs = slice(rg * 128, (rg + 1) * 128)
            DMA('pool', wb_gate[rs].rearrange("r (a c) -> r a c", c=1024),
                w_in[rs, 8336:12432].rearrange("r (a c) -> r a c", c=1024), [], ['wb_gate'])
        for (dst, src, nm, nr) in ((wb_uprw, w_uprw, 'wb_uprw', 1024), (wb_upfox, w_upfox, 'wb_upfox', 1024),
                                   (wb_out, w_out, 'wb_out', D), (wb_pp, ple_proj, 'wb_pp', 256),
                                   (wb_pg, ple_gw, 'wb_pg', D)):
            for r0 in range(0, nr, 256):
                DMA('pool', dst[r0:r0 + 256].rearrange("r (a c) -> r a c", c=1024),
                    src[r0:r0 + 256].rearrange("r (a c) -> r a c", c=1024), [], [nm])


        par2 = sb("par2", [128, 8, 8], F32)
        for i, src in enumerate((rw_w0, rw_a0, rw_kk, rw_ka, rw_ka, rw_rk, rw_lng, rw_lnb)):
            DMA('sp', par2[:, :, i:i + 1], src.rearrange("o (q p) -> p q o", p=128), [], ['par2'],
                allow_slow_non_contiguous=True)
        TS(par2[:, :, 0:2], par2[:, :, 0:2], -1.0, None, ALU.mult, ALU.bypass, ['par2'], ['par2'])
        TS(par2[:, :, 4:5], par2[:, :, 4:5], -1.0, 1.0, ALU.mult, ALU.add, ['par2'], ['par2'])
        mu4 = sb("mu4", [128, 4, 8, 2], F32)
        for g4 in range(4):
            DMA('sp', mu4[:, g4, :, 0:1], mu[:, g4 * 1024:(g4 + 1) * 1024].rearrange("o (q p) -> p q o", p=128), [], ['mu4'],
                allow_slow_non_contiguous=True)
        TS(mu4[:, :, :, 1:2], mu4[:, :, :, 0:1], -1.0, 1.0, ALU.mult, ALU.add, ['mu4'], ['mu4'])
        mu_lo = sb("mu_lo", [128, 2], F32)
        DMA('sp', mu_lo[:, 0:1], mu[:, 4096:4224].rearrange("o (p q) -> p (o q)", q=1), [], ['mu_lo'],
            allow_slow_non_contiguous=True)
        TS(mu_lo[:, 1:2], mu_lo[:, 0:1], -1.0, 1.0, ALU.mult, ALU.add, ['mu_lo'], ['mu_lo'])
        nbf = sb("nbf", [16, 1], F32)
        DMA('sp', nbf[:], fox_bf.rearrange("o h -> h o"), [], ['nbf'], allow_slow_non_contiguous=True)
        TS(nbf[:], nbf[:], -1.0, None, ALU.mult, ALU.bypass, ['nbf'], ['nbf'])

        hTt = [sb("hTt%d" % i, [128, 16, 512], BF16) for i in range(2)]
        esl = ExitStack()
        loraT = sb("loraT", [128, T], BF16, esl)

        def hTw(hb, t4, half):
            return ('hTt', hb, t4, half)

        def hTr(hb):
            return [('hTt', hb, t4, half) for t4 in range(4) for half in range(2)]

        def shift_evac(bank, tmp, ktmp, mucol, omucol, cap, kcar, first):
            ACT(tmp[:], B[bank][:], AF.Identity, [Bn[bank]], [ktmp], scale=omucol)
            STT(tmp[:, 1:512], B[bank][:, 0:511], mucol, tmp[:, 1:512], ALU.mult, ALU.add, [Bn[bank], ktmp], [ktmp])
            if not first:
                STT(tmp[:, 0:1], cap, mucol, tmp[:, 0:1], ALU.mult, ALU.add, [kcar, ktmp], [ktmp])
            CP(cap, B[bank][:, 511:512], [Bn[bank]], [kcar])

        with ExitStack() as esp:
            grep = sb("grep", [128, D], F32, esp)
            DMA('sp', grep[:], norm_g.partition_broadcast(128), [], ['grep'])
            xt = [sb("xt%d" % i, [128, D], F32, esp) for i in range(2)]
            xnb1 = [sb("xnb%d" % i, [128, D], BF16, esp) for i in range(2)]
            st1 = [sb("st%d" % i, [128, 4], F32, esp) for i in range(2)]
            wlo = sb("wlo", [128, 16, 128], BF16, esp)
            DMA('sp', wlo[:], wb_lo.rearrange("(k p) n -> p k n", p=128), ['wb_lo'], ['wlo'])
            wfl = sb("wfl", [128, 16, 16], BF16, esp)
            DMA('sp', wfl[:], wb_fl.rearrange("(k p) n -> p k n", p=128), ['wb_fl'], ['wfl'])
            tmpLO = sb("tmpLO", [128, 512], F32, esp)
            carlo = sb("carlo", [128, 1], F32, esp)
            cS = sb("cS", [16, 512], F32, esp)
            cprev = sb("cprev", [16, 1], F32, esp)
            ct = [sb("ct%d" % i, [16, 512], F32, esp) for i in range(2)]
            c3 = sb("c3", [16, 3, 512], BF16, esp)
            n3 = sb("n3", [16, 3, 512], BF16, esp)
            onesr = sb("onesr", [16, 512], F32, esp)
            MSET(onesr[:], 1.0, [], ['onesr'], eng='dve')
            MSET(cprev[:], 0.0, [], ['cprev'], eng='dve')
            for mt in range(NM):
                hb = mt % 2
                ts_ = slice(mt * 512, (mt + 1) * 512)
                for t4 in range(4):
                    tt = mt * 4 + t4
                    i = tt % 2
                    kx, kn, ks = 'xt%d' % i, 'xnb%d' % i, 'st%d' % i
                    DMA('sp', xt[i][:], x[tt * 128:(tt + 1) * 128, :], [], [kx])
                    ACT(xnb1[i][:], xt[i][:], AF.Square, [kx], [kn, ks], accum_out=st1[i][:, 0:1])
                    TS(st1[i][:, 1:2], st1[i][:, 0:1], 1.0 / D, 1e-6, ALU.mult, ALU.add, [ks], [ks])
                    ACT(st1[i][:, 2:3], st1[i][:, 1:2], AF.Sqrt, [ks], [ks])
                    RCP(st1[i][:, 3:4], st1[i][:, 2:3], [ks], [ks])
                    STT(xnb1[i][:], xt[i][:], st1[i][:, 3:4], grep[:], ALU.mult, ALU.mult, [kx, ks, 'grep'], [kn])
                    bi = 2 * i
                    for j in range(16):
                        bk = bi + j // 8
                        TR(b16(bk)[:, (j % 8) * 128:(j % 8 + 1) * 128], xnb1[i][:, j * 128:(j + 1) * 128], ident[:],
                           [kn, 'ident'], [Bn[bk]], inc=(j % 8 == 7))
                    ACT(hTt[hb][:, 0:8, t4 * 128:(t4 + 1) * 128], b16(bi)[:].rearrange("p (j t) -> p j t", t=128),
                        AF.Identity, [Bn[bi]], [hTw(hb, t4, 0)])
                    CP(hTt[hb][:, 8:16, t4 * 128:(t4 + 1) * 128], b16(bi + 1)[:].rearrange("p (j t) -> p j t", t=128),
                       [Bn[bi + 1]], [hTw(hb, t4, 1)])
                DMA('sp', hT_d[mt], hTt[hb][:].rearrange("p k t -> p (k t)"), hTr(hb), [('hT_d', mt)])
                for kc in range(16):
                    MM(B[4][:], wlo[:, kc, :], hTt[hb][:, kc, :], kc == 0, kc == 15, ['wlo'] + hTr(hb), [Bn[4]], inc=(kc == 15))
                shift_evac(4, tmpLO, 'tmpLO', mu_lo[:, 0:1], mu_lo[:, 1:2], carlo[:, 0:1], 'carlo', mt == 0)
                ACT(loraT[0:64, ts_], tmpLO[0:64, :], AF.Tanh, ['tmpLO'], [('loraT', mt)])
                CP(loraT[64:128, ts_], tmpLO[64:128, :], ['tmpLO'], [('loraT', mt)])
                for kc in range(16):
                    MM(B[5][0:16, :], wfl[:, kc, :], hTt[hb][:, kc, :], kc == 0, kc == 15, ['wfl'] + hTr(hb), [Bn[5]], inc=(kc == 15))
                ACT(ct[0][:], B[5][0:16, :], AF.Exp, [Bn[5], 'nbf'], ['ct0'], scale=-1.0, bias=nbf[:, 0:1])
                ACT(ct[0][:], ct[0][:], AF.Ln, ['ct0'], ['ct0'], bias=1.0)
                SCAN(cS[:], onesr[:], ct[0][:], cprev[:, 0:1], ['onesr', 'ct0', 'cprev'], ['cS'])
                CP(cprev[:], cS[:, 511:512], ['cS'], ['cprev'])
                CP(c3[:, 0, :], cS[:], ['cS'], ['c3'])
                TT(ct[0][:], cS[:], c3[:, 0, :], ALU.subtract, ['cS', 'c3'], ['ct0'])
                CP(c3[:, 1, :], ct[0][:], ['ct0'], ['c3'])
                TT(ct[1][:], ct[0][:], c3[:, 1, :], ALU.subtract, ['ct0', 'c3'], ['ct1'])
                CP(c3[:, 2, :], ct[1][:], ['ct1'], ['c3'])
                TS(n3[:], c3[:], -1.0, None, ALU.mult, ALU.bypass, ['c3'], ['n3'])
                DMA('sp', caug_d[:, :, ts_], c3[:], ['c3'], ['caug_d'])
                DMA('sp', ncaug_d[:, :, ts_], n3[:], ['n3'], ['ncaug_d'])
        S.barrier(skip_pool=True)

        if True:
            with ExitStack() as e2:
                lup = sb("lup", [128, 1024], BF16, e2)
                DMA('sp', lup[0:64, :], wb_wl, ['wb_wl'], ['lup'])
                DMA('sp', lup[64:128, :], wb_al, ['wb_al'], ['lup'])
                Whg = sb("Whg", [128, 1, 16, 512], BF16, e2)
                H32a = sb("H32a", [128, 8, 64], F32, e2)
                Hba = sb("Hba", [128, 8, 64], BF16, e2)
                cara = sb("cara", [128, 8, 4], F32, e2)
                MSET(H32a[:], 0.0, [], ['H32a'], eng='dve')
                MSET(Hba[:], 0.0, [], ['Hba'], eng='dve')
                ctxs = []
                for t in range(1):
                    c = {'bo': 4 * t, 't': t}
                    n_ = lambda s_: "%s_%d" % (s_, t)
                    c['tmp'] = [sb(n_("tmp%d" % i), [128, 512], F32, e2) for i in range(4)]
                    c['sc'] = [sb(n_("sc%d" % i), [128, 512], F32, e2) for i in range(7)]
                    c['bonus'] = sb(n_("bonus"), [128, 512], F32, e2)
                    c['gs'] = sb(n_("gs"), [128, 512], F32, e2)
                    c['pc'] = sb(n_("pc"), [128, 4], F32, e2)
                    c['AR'] = sb(n_("AR"), [128, 2, 512], BF16, e2)
                    c['KB'] = sb(n_("KB"), [128, 2, 512], BF16, e2)
                    c['KH'] = sb(n_("KH"), [128, 3, 512], BF16, e2)
                    c['sqb'] = sb(n_("sqb"), [128, 512], BF16, e2)
                    c['MS'] = [sb(n_("MS%d" % i), [128, 4, 512], BF16, e2) for i in range(2)]
                    c['Yk'] = [sb(n_("Yk%d" % i), [128, 4, 128], BF16, e2) for i in range(2)]
                    c['YTk'] = [sb(n_("YTk%d" % i), [128, 4, 128], BF16, e2) for i in range(2)]
                    c['G'] = [sb(n_("G%d" % i), [128, 4, 128], BF16, e2) for i in range(2)]
                    c['TOK'] = [sb(n_("TOK%d" % i), [128, 4, 3, 64], BF16, e2) for i in range(2)]
                    c['Wsb'] = sb(n_("Wsb"), [128, 64], BF16, e2)
                    c['Un'] = sb(n_("Un"), [128, 64], BF16, e2)
                    c['ycat'] = sb(n_("ycat"), [128, 2, 512], BF16, e2)
                    c['yout'] = [sb(n_("yout%d" % i), [128, 512], BF16, e2) for i in range(2)]
                    c['n'] = 0
                    ctxs.append(c)

                def sigmoid_(dst, kd, src, ks, extra_r=(), **kw):
                    ACT(dst, src, AF.Exp, [ks] + list(extra_r), [kd], **kw)
                    ACT(dst, dst, AF.Ln, [kd], [kd], bias=1.0)
                    ACT(dst, dst, AF.Exp, [kd], [kd], scale=-1.0)

                def rw_pair(q, mt, hb, c):
                    t = c['t']
                    o = c['bo']
                    K_ = lambda s_: "%s_%d" % (s_, t)
                    Bk = lambda i: B[o + i]
                    Kb = lambda i: Bn[o + i]
                    ql = 0
                    ts_ = slice(mt * 512, (mt + 1) * 512)
                    hk = hTr(hb)
                    tmp, sc, bonus, gs, pc = c['tmp'], c['sc'], c['bonus'], c['gs'], c['pc']
                    AR, KB, KH, sqb, Yk, YTk = c['AR'], c['KB'], c['KH'], c['sqb'], c['Yk'], c['YTk']
                    Wsb, Un, ycat, yout = c['Wsb'], c['Un'], c['ycat'], c['yout']
                    sk = [K_("sc%d" % i) for i in range(7)]
                    kt = [K_("tmp%d" % i) for i in range(4)]
                    kAR, kKB, kKH = K_('AR'), K_('KB'), K_('KH')
                    P = lambda i: par2[:, q, i:i + 1]
                    first = (mt == 0)
                    for g in range(4):
                        for kc in range(16):
                            MM(Bk(g)[:], Whg[:, ql, kc, g * 128:(g + 1) * 128], hTt[hb][:, kc, :], kc == 0, kc == 15,
                               ['Whg'] + hk, [Kb(g)], inc=(kc == 15))
                    for g in range(4):
                        shift_evac(o + g, tmp[g], kt[g], mu4[:, g, q, 0:1], mu4[:, g, q, 1:2], cara[:, q, g:g + 1], ('car', q, g), first)
                    r_, k_, v_, g_ = tmp[0][:], tmp[1][:], tmp[2][:], tmp[3][:]
                    MM(Bk(0)[:], lup[0:64, q * 128:(q + 1) * 128], loraT[0:64, ts_], True, True, ['lup', ('loraT', mt)], [Kb(0)])
                    MM(Bk(1)[:], lup[64:128, q * 128:(q + 1) * 128], loraT[64:128, ts_], True, True, ['lup', ('loraT', mt)], [Kb(1)])
                    sigmoid_(gs[:], K_('gs'), g_, kt[3], scale=-1.0)
                    TT(gs[:], gs[:], g_, ALU.mult, [K_('gs'), kt[3]], [K_('gs')])
                    sigmoid_(sc[1][:], sk[1], Bk(0)[:], Kb(0), ['par2'], scale=-1.0, bias=P(0))
                    sigmoid_(sc[2][:], sk[2], Bk(1)[:], Kb(1), ['par2'], scale=-1.0, bias=P(1))
                    SCAN(sc[3][:], rmask[:], sc[1][:], 0.0, ['rmask', sk[1]], [sk[3]])
                    TT(sc[4][:], sc[3][:], sc[1][:], ALU.subtract, [sk[3], sk[1]], [sk[4]])
                    ACT(sc[1][:], sc[3][:], AF.Exp, [sk[3]], [sk[1]], scale=-C0)
                    ACT(sc[5][:], sc[3][:], AF.Exp, [sk[3]], [sk[5]], scale=C0)
                    ACT(sc[4][:], sc[4][:], AF.Exp, [sk[4]], [sk[4]], scale=-C0)
                    CP(pc[:], sc[1][:].rearrange("p (c t) -> p c t", t=128)[:, :, 127], [sk[1]], [K_('pc')])
                    TS(sc[3][:], k_, P(2), None, ALU.mult, ALU.bypass, [kt[1], 'par2'], [sk[3]])
                    TT(sqb[:], sc[3][:], sc[3][:], ALU.mult, [sk[3]], [K_('sqb')])
                    MM(Bk(2)[:], onesbd[:], sqb[:], True, True, ['onesbd', K_('sqb')], [Kb(2)])
                    TS(sc[6][:], Bk(2)[:], 1e-24, None, ALU.max, ALU.bypass, [Kb(2)], [sk[6]])
                    ACT(sc[6][:], sc[6][:], AF.Ln, [sk[6]], [sk[6]], scale=float(2.0 ** 40))
                    ACT(sc[6][:], sc[6][:], AF.Exp, [sk[6], 'cb'], [sk[6]], scale=-0.5, bias=cb[:, 0:1])
                    TT(sc[3][:], sc[3][:], sc[6][:], ALU.mult, [sk[3], sk[6]], [sk[3]])
                    TS(sc[6][:], sc[2][:], P(3), P(4), ALU.mult, ALU.add, [sk[2], 'par2'], [sk[6]])
                    TT(sc[6][:], sc[6][:], k_, ALU.mult, [sk[6], kt[1]], [sk[6]])
                    TT(sc[2][:], sc[3][:], sc[2][:], ALU.mult, [sk[3], sk[2]], [sk[2]])
                    TT(AR[:, 0, :], sc[3][:], sc[4][:], ALU.mult, [sk[3], sk[4]], [kAR])
                    TT(AR[:, 1, :], r_, sc[1][:], ALU.mult, [kt[0], sk[1]], [kAR])
                    TT(sc[3][:], r_, sc[6][:], ALU.mult, [kt[0], sk[6]], [sk[3]])
                    ACT(sqb[:], sc[3][:], AF.Identity, [sk[3], 'par2'], [K_('sqb')], scale=P(5))
                    MM(Bk(3)[:], onesbd[:], sqb[:], True, True, ['onesbd', K_('sqb')], [Kb(3)])
                    TT(bonus[:], Bk(3)[:], v_, ALU.mult, [Kb(3), kt[2]], [K_('bonus')])
                    TT(sc[4][:], sc[6][:], sc[5][:], ALU.mult, [sk[6], sk[5]], [sk[4]])
                    TT(sc[1][:], sc[2][:], sc[5][:], ALU.mult, [sk[2], sk[5], K_('pc')], [sk[1]])
                    ACT(KB[:, 0, :], sc[4][:], AF.Identity, [sk[4]], [kKB])
                    ACT(KB[:, 1, :], sc[1][:], AF.Identity, [sk[1]], [kKB])
                    pcb = pc[:].unsqueeze(2).to_broadcast([128, 4, 128])
                    TT(KH[:, 0, :].rearrange("p (c t) -> p c t", t=128), sc[4][:].rearrange("p (c t) -> p c t", t=128), pcb,
                       ALU.mult, [sk[4], K_('pc')], [kKH])
                    TT(KH[:, 1, :].rearrange("p (c t) -> p c t", t=128), sc[1][:].rearrange("p (c t) -> p c t", t=128), pcb,
                       ALU.mult, [sk[1], K_('pc')], [kKH])
                    CP(KH[:, 2, :], v_, [kt[2]], [kKH])
                    for hh in range(2):
                        hs = slice(hh * 64, (hh + 1) * 64)
                        pt = b16(o + hh)[:, 0:768].rearrange("p (c g k) -> p c g k", c=4, g=3)
                        for cc in range(4):
                            for g3 in range(3):
                                TR(pt[:, cc, g3, :], KH[hs, g3, cc * 128:(cc + 1) * 128], ident[hs, hs], [kKH, 'ident'], [Kb(hh)],
                                   inc=(cc == 3 and g3 == 2))
                        CP(c['TOK'][hh][:], pt, [Kb(hh)], [K_('TOK%d' % hh)])
                    for hh in range(2):
                        h = 2 * q + hh
                        hs = slice(hh * 64, (hh + 1) * 64)
                        MS, G, TOK = c['MS'][hh], c['G'][hh], c['TOK'][hh]
                        kMSn, kG, kTOK = K_('MS%d' % hh), K_('G%d' % hh), K_('TOK%d' % hh)
                        kH32, kHb = ('H32', h), ('Hb', h)
                        H32 = H32a[hs, q, :]
                        Hb = Hba[hs, q, :]
                        for cc in range(4):
                            cs = slice(cc * 128, (cc + 1) * 128)
                            MM(Bk(2)[:, 0:256].rearrange('p (a t) -> p a t', a=2), KB[hs, 0, cs], AR[hs, :, cs], True, True, [kKB, kAR], [Kb(2)], inc=False)
                            MM(Bk(2)[:, 256:512].rearrange('p (a t) -> p a t', a=2), KB[hs, 1, cs], AR[hs, :, cs], True, True, [kKB, kAR], [Kb(2)])
                            MM(Bk(1)[:, cs], AR[hs, 0, cs], KB[hs, 1, cs], True, True, [kKB, kAR], [Kb(1)])
                            TT(MS[:, cc, :], Bk(2)[:], mask512[:], ALU.mult, [Kb(2), 'mask512'], [(kMSn, cc)])
                        mskeys = [(kMSn, cc) for cc in range(4)]
                        X0 = MS[:, :, 256:384]
                        TT(YTk[0][:], Bk(1)[:].rearrange("p (c t) -> p c t", t=128), maskL[:], ALU.mult, [Kb(1), 'maskL'], [K_('YTk0')])
                        CP(Yk[0][:], X0, mskeys, [K_('Yk0')])
                        TT(G[:], X0, ident4[:], ALU.add, mskeys + ['ident4'], [kG])
                        for k in range(1, 7):
                            a_, b_ = (k - 1) % 2, k % 2
                            kYa, kYTa, kYb, kYTb = K_('Yk%d' % a_), K_('YTk%d' % a_), K_('Yk%d' % b_), K_('YTk%d' % b_)
                            if k <= 5:
                                for cc in range(4):
                                    MM(Bk(0)[:, cc * 128:(cc + 1) * 128], YTk[a_][:, cc, :], Yk[a_][:, cc, :], True, True, [kYa, kYTa], [Kb(0)], inc=(cc == 3))
                            for cc in range(4):
                                MM(Bk(1)[:, cc * 128:(cc + 1) * 128], Yk[a_][:, cc, :], YTk[a_][:, cc, :], True, True, [kYa, kYTa], [Kb(1)], inc=(cc == 3))
                            if k <= 5:
                                ACT(Yk[b_][:], Bk(0)[:].rearrange("p (c t) -> p c t", t=128), AF.Identity, [Kb(0)], [kYb])
                            CP(YTk[b_][:], Bk(1)[:].rearrange("p (c t) -> p c t", t=128), [Kb(1)], [kYTb])
                            for cc in range(4):
                                MM(Bk(2)[:, cc * 128:(cc + 1) * 128], YTk[b_][:, cc, :], G[:, cc, :], True, True, [kYTb, kG], [Kb(2)], inc=(cc == 3))
                            TT(G[:], Bk(2)[:].rearrange("p (c t) -> p c t", t=128), G[:], ALU.add, [Kb(2), kG], [kG])
                        for cc in range(4):
                            cs = slice(cc * 128, (cc + 1) * 128)
                            kMS = (kMSn, cc)
                            MM(Bk(3)[:, 0:64], AR[hs, 0, cs], Hb, True, False, [kAR, kHb], [Kb(3)], inc=False)
                            MM(Bk(3)[:, 0:64], MS[:, cc, 0:128], TOK[:, cc, 2, :], False, True, [kMS, kTOK], [Kb(3)])
                            ACT(Wsb[:], Bk(3)[:, 0:64], AF.Identity, [Kb(3)], [K_('Wsb')])
                            MM(Bk(3)[:, 64:128], G[:, cc, :], Wsb[:], True, True, [kG, K_('Wsb')], [Kb(3)])
                            ACT(Un[:], Bk(3)[:, 64:128], AF.Identity, [Kb(3)], [K_('Un')], scale=-1.0)
                            MM(Bk(4)[hs, cs], Hb, AR[hs, 1, cs], True, False, [kHb, kAR], [Kb(4)], inc=False)
                            MM(Bk(4)[hs, cs], TOK[:, cc, 2, :], MS[:, cc, 128:256], False, False, [kTOK, kMS], [Kb(4)], inc=False)
                            MM(Bk(4)[hs, cs], Un[:], MS[:, cc, 384:512], False, True, [K_('Un'), kMS], [Kb(4)])
                            MM(Bk(3)[hs, 128:192], TOK[:, cc, 0, :], TOK[:, cc, 2, :], True, False, [kTOK], [Kb(3)], inc=False)
                            MM(Bk(3)[hs, 128:192], TOK[:, cc, 1, :], Un[:], False, True, [kTOK, K_('Un')], [Kb(3)])
                            STT(H32, H32, pc[hs, cc:cc + 1], Bk(3)[hs, 128:192], ALU.mult, ALU.add, [kH32, K_('pc'), Kb(3)], [kH32])
                            ACT(Hb, H32, AF.Identity, [kH32], [kHb])
                    ACT(ycat[:, 0, :], Bk(4)[:], AF.Identity, [Kb(4)], [K_('ycat')])
                    ACT(ycat[:, 1, :], Bk(4)[:], AF.Square, [Kb(4)], [K_('ycat')])
                    MM(Bk(1)[:], onesbdm[:], ycat[:, 0, :], True, True, ['onesbdm', K_('ycat')], [Kb(1)])
                    MM(Bk(2)[:], onesbdm[:], ycat[:, 1, :], True, True, ['onesbdm', K_('ycat')], [Kb(2)])
                    ACT(sc[1][:], Bk(1)[:], AF.Identity, [Kb(1)], [sk[1]])
                    TT(sc[2][:], sc[1][:], sc[1][:], ALU.mult, [sk[1]], [sk[2]])
                    TT(sc[2][:], Bk(2)[:], sc[2][:], ALU.subtract, [Kb(2), sk[2]], [sk[2]])
                    ACT(sc[2][:], sc[2][:], AF.Ln, [sk[2], 'cb'], [sk[2]], bias=cb[:, 1:2])
                    ACT(sc[2][:], sc[2][:], AF.Exp, [sk[2]], [sk[2]], scale=-0.5)
                    TT(sc[3][:], Bk(4)[:], sc[1][:], ALU.subtract, [Kb(4), sk[1]], [sk[3]])
                    TT(sc[3][:], sc[3][:], sc[2][:], ALU.mult, [sk[3], sk[2]], [sk[3]])
                    ACT(sc[3][:], sc[3][:], AF.Identity, [sk[3], 'par2'], [sk[3]], scale=P(6), bias=P(7))
                    TT(sc[3][:], sc[3][:], bonus[:], ALU.add, [sk[3], K_('bonus')], [sk[3]])
                    yi = c['n'] % 2
                    c['n'] += 1
                    TT(yout[yi][:], sc[3][:], gs[:], ALU.mult, [sk[3], K_('gs')], [K_('yout%d' % yi)])
                    DMA('sp', yT_d[q * 128:(q + 1) * 128, ts_], yout[yi][:], [K_('yout%d' % yi)], [('yT_d', mt)])

                Wfg = sb("Wfg", [128, 2, 16, 256], BF16, e2)
                kTa = [sb("kTa%d" % i, [70, T], BF16, e2) for i in range(2)]
                Va = [sb("Va%d" % i, [128, T // 128, 65], BF16, e2) for i in range(2)]
                for i in range(2):
                    MSET(Va[i][:], 1.0, [], [('Va', i)])
                    MSET(kTa[i][64:70, :], 1.0, [], [('kTa', i)])
                fctx = []
                for t in range(1, 2):
                    c = {'bo': 4 * t, 't': t, 'n': 0}
                    n_ = lambda s_: "%s_f%d" % (s_, t)
                    c['qTa'] = [sb(n_("qTa%d" % i), [70, 512], BF16, e2) for i in range(2)]
                    for i in range(2):
                        MSET(c['qTa'][i][64:70, :], 1.0, [], [n_("qTa%d" % i)])
                    c['vb'] = sb(n_("vb"), [64, 512], BF16, e2)
                    c['gsf'] = sb(n_("gsf"), [64, 512], F32, e2)
                    c['PTt'] = [sb(n_("PTt%d" % i), [128, 512], BF16, e2) for i in range(2)]
                    c['rec'] = sb(n_("rec"), [65, 512], F32, e2)
                    c['fo'] = sb(n_("fo"), [64, 512], F32, e2)
                    c['yf'] = [sb(n_("yf%d" % i), [64, 512], BF16, e2) for i in range(2)]
                    fctx.append(c)

                def fox_unit(h, I, hb, c):
                    t = c['t']
                    o = c['bo']
                    K_ = lambda s_: "%s_f%d" % (s_, t)
                    fb = {0: 5, 1: 5, 2: 6, 3: 7}
                    Bk = lambda i: B[fb[i]]
                    Kb = lambda i: Bn[fb[i]]
                    hl = h % 2
                    ts_ = slice(I * 512, (I + 1) * 512)
                    hk = hTr(hb)
                    u = c['n']
                    c['n'] += 1
                    qT = c['qTa'][u % 2]
                    kq = K_("qTa%d" % (u % 2))
                    kT, kkT = kTa[hl], ('kTa', hl)
                    V_, kV = Va[hl], ('Va', hl)
                    vb, gsf, PTt, rec, fo, yf = c['vb'], c['gsf'], c['PTt'], c['rec'], c['fo'], c['yf']
                    for kc in range(16):
                        MM(Bk(0)[:], Wfg[:, hl, kc, 0:128], hTt[hb][:, kc, :], kc == 0, kc == 15, ['Wfg'] + hk, [Kb(0)], inc=(kc == 15))
                    ACT(qT[0:64, :], Bk(0)[0:64, :], AF.Identity, [Kb(0)], [kq], scale=0.125)
                    DMA('sp', qT[64:67, :], ncaug_d[h, :, ts_], ['ncaug_d'], [kq])
                    CP(kT[0:64, ts_], Bk(0)[64:128, :], [Kb(0)], [kkT])
                    for kc in range(16):
                        MM(Bk(1)[:], Wfg[:, hl, kc, 128:256], hTt[hb][:, kc, :], kc == 0, kc == 15, ['Wfg'] + hk, [Kb(1)], inc=(kc == 15))
                    CP(vb[:], Bk(1)[0:64, :], [Kb(1)], [K_('vb')])
                    CP(gsf[:], Bk(1)[64:128, :], [Kb(1)], [K_('gsf')])
                    ACT(fo[:], gsf[:], AF.Exp, [K_('gsf')], [K_('fo')], scale=-1.0)
                    ACT(fo[:], fo[:], AF.Ln, [K_('fo')], [K_('fo')], bias=1.0)
                    ACT(fo[:], fo[:], AF.Exp, [K_('fo')], [K_('fo')], scale=-1.0)
                    TT(gsf[:], gsf[:], fo[:], ALU.mult, [K_('gsf'), K_('fo')], [K_('gsf')], eng='dve')
                    pt = b16(5)[:, 0:256].rearrange("p (c k) -> p c k", c=4)
                    for cc in range(4):
                        TR(pt[:, cc, :], vb[:, cc * 128:(cc + 1) * 128], ident[0:64, 0:64], [K_('vb'), 'ident'], [Kb(0)], inc=(cc == 3))
                    CP(V_[:, I * 4:(I + 1) * 4, 0:64], pt, [Kb(0)], [kV])
                    nj = 4 * I + 4

                    def qk(j):
                        lo = max(0, j - 4 * I) * 128
                        bk = 2 + (j % 2)
                        diag = j >= 4 * I
                        MM(Bk(bk)[:, lo:512], kT[0:70, j * 128:(j + 1) * 128], qT[0:70, lo:512], True, not diag, [kkT, kq], [Kb(bk)], inc=not diag)
                        if diag:
                            MM(Bk(bk)[:, lo:lo + 128], ident[:], maskneg[:], False, True, ['ident', 'maskneg'], [Kb(bk)])
                    qk(0)
                    for j in range(nj):
                        lo = max(0, j - 4 * I) * 128
                        bk = 2 + (j % 2)
                        pi = j % 2
                        if j + 1 < nj:
                            qk(j + 1)
                        ACT(PTt[pi][:, lo:512], Bk(bk)[:, lo:512], AF.Exp, [Kb(bk)], [K_('PTt%d' % pi)])
                        MM(Bk(1)[0:65, lo:512], V_[:, j, :], PTt[pi][:, lo:512], j == 0, j == nj - 1, [kV, K_('PTt%d' % pi)], [Kb(1)], inc=(j == nj - 1))
                    ACT(rec[64:65, :], Bk(1)[64:65, :], AF.Ln, [Kb(1)], [K_('rec')])
                    ACT(rec[64:65, :], rec[64:65, :], AF.Exp, [K_('rec')], [K_('rec')], scale=-1.0)
                    MM(B[6][0:64, :], onesf[64:65, :], rec[64:65, :], True, True, ['onesf', K_('rec')], [Bn[6]])
                    TT(fo[:], Bk(1)[0:64, :], gsf[:], ALU.mult, [Kb(1), K_('gsf')], [K_('fo')])
                    yi = u % 2
                    TT(yf[yi][:], fo[:], B[6][0:64, :], ALU.mult, [K_('fo'), Bn[6]], [K_('yf%d' % yi)])
                    DMA('sp', yT_d[1024 + h * 64:1024 + (h + 1) * 64, ts_], yf[yi][:], [K_('yf%d' % yi)], [('yT_d', I)])


                for g8 in range(8):
                    DMA('sp', Whg[:, 0], wb_rw[:, g8, :].rearrange("(k p) n -> p k n", p=128), ['wb_rw'], ['Whg'])
                    for hl_ in range(2):
                        DMA('sp', Wfg[:, hl_], wb_fox[:, g8 * 2 + hl_, :].rearrange("(k p) n -> p k n", p=128), ['wb_fox'], ['Wfg'])
                        DMA('sp', kTa[hl_][67:70, :], caug_d[g8 * 2 + hl_], ['caug_d'], [('kTa', hl_)])
                    DMA('sp', hTt[0][:].rearrange("p k t -> p (k t)"), hT_d[0], [('hT_d', 0)], hTr(0))
                    for mt in range(NM):
                        hb = mt % 2
                        if mt + 1 < NM:
                            DMA('sp', hTt[1 - hb][:].rearrange("p k t -> p (k t)"), hT_d[mt + 1], [('hT_d', mt + 1)], hTr(1 - hb))

                        def thA(mt=mt, hb=hb, g8=g8):
                            rw_pair(g8, mt, hb, ctxs[0])

                        def thB(mt=mt, hb=hb, g8=g8):
                            for hh in range(2):
                                fox_unit(g8 * 2 + hh, mt, hb, fctx[0])
                        run_threads([thA, thB], stagger=0.0)
            S.barrier()
        esl.close()

        if do_post:
            with ExitStack() as e4:
                gp = sb("gp", [128, D], F32, e4)
                gf = sb("gf", [128, D], F32, e4)
                gn = sb("gn", [128, D], F32, e4)
                role = sb("role_sb", [128, 2], F32, e4)
                DMA('sp', gn[:], norm_g.partition_broadcast(128), [], ['gn'])
                DMA('sp', role[:], role_in, [], ['role'])
                DMA('sp', gp[:], ple_ng.partition_broadcast(128), [], ['gp'])
                DMA('sp', gf[:], fin_g.partition_broadcast(128), [], ['gf'])
                x1 = sb("x1", [128, 4, D], F32, e4)
                xnb = sb("xnb4", [128, D], BF16, e4)
                st = sb("st4", [128, 4], F32, e4)
                yTt = sb("yTt", [128, 16, 512], BF16, e4)
                mT = sb("mT", [128, 16, 512], BF16, e4)
                pT = sb("pT", [128, 2, 512], BF16, e4)
                pt32 = sb("pt32", [128, 256], F32, e4)
                ptb = sb("ptb", [128, 256], BF16, e4)
                wsm = [sb("wsm%d" % i, [128, 16, 128], BF16, e4) for i in range(5)]
                wbg = [sb("wbg%d" % i, [128, 16, 512], BF16, e4) for i in range(2)]
                wpp = sb("wpp", [128, 2, 512], BF16, e4)
                sg1 = [sb("sg1_%d" % i, [128, 512], F32, e4) for i in range(2)]
                sg2 = [sb("sg2_%d" % i, [128, 512], F32, e4) for i in range(2)]
                ot = sb("ot0", [128, D], F32, e4)
                nsm = [0]
                nbg = [0]

                def rms_rstd(src_ap, ksrc):
                    ACT(xnb[:], src_ap, AF.Square, [ksrc], ['xnb4', 'st4'], accum_out=st[:, 0:1])
                    TS(st[:, 1:2], st[:, 0:1], 1.0 / D, 1e-6, ALU.mult, ALU.add, ['st4'], ['st4'])
                    ACT(st[:, 2:3], st[:, 1:2], AF.Sqrt, ['st4'], ['st4'])
                    RCP(st[:, 3:4], st[:, 2:3], ['st4'], ['st4'])

                def norm_T(src_ap, ksrc, gtile, kg, dstT, kdst, s_):
                    rms_rstd(src_ap, ksrc)
                    STT(xnb[:], src_ap, st[:, 3:4], gtile[:], ALU.mult, ALU.mult, [ksrc, 'st4', kg], ['xnb4'])
                    for j in range(16):
                        bk = j // 8
                        TR(b16(bk)[:, (j % 8) * 128:(j % 8 + 1) * 128], xnb[:, j * 128:(j + 1) * 128], ident[:],
                           ['xnb4', 'ident'], [Bn[bk]], inc=(j % 8 == 7))
                    ACT(dstT[:, 0:8, s_ * 128:(s_ + 1) * 128], b16(0)[:].rearrange("p (j t) -> p j t", t=128),
                        AF.Identity, [Bn[0]], kdst[0:8])
                    CP(dstT[:, 8:16, s_ * 128:(s_ + 1) * 128], b16(1)[:].rearrange("p (j t) -> p j t", t=128),
                       [Bn[1]], kdst[8:16])

                NH = NM // 2
                for TT_ in range(NH):
                    ts_ = slice(TT_ * 512, (TT_ + 1) * 512)
                    tsB = slice((TT_ + NH) * 512, (TT_ + NH + 1) * 512)
                    hb = 0
                    hT4 = hTt[0]
                    yB = hTt[1]
                    DMA('sp', yTt[:], yT_d[:, ts_].rearrange("(k p) t -> p k t", p=128), [('yT_d', TT_)], ['yTt'])
                    DMA('sp', yB[:], yT_d[:, tsB].rearrange("(k p) t -> p k t", p=128), [('yT_d', TT_ + NH)], hTr(1))
                    yf_ = yTt[:].rearrange("p k t -> p (k t)")
                    TS(yf_, yf_, role[:, 0:1], None, ALU.mult, ALU.bypass, ['yTt', 'role'], ['yTt'])
                    STT(yf_, yB[:].rearrange("p k t -> p (k t)"), role[:, 1:2], yf_, ALU.mult, ALU.add, hTr(1) + ['yTt', 'role'], ['yTt'])
                    for s_ in range(4):
                        tok = slice(TT_ * 512 + s_ * 128, TT_ * 512 + (s_ + 1) * 128)
                        DMA('sp', x1[:, s_, :], x_post[tok, :], [], [('x1', s_)])
                        norm_T(x1[:, s_, :], ('x1', s_), gn, 'gn', hT4, [hTw(0, s_, 0)] * 8 + [hTw(0, s_, 1)] * 8, s_)
                        DMA('sp', pt32[:], p_post[tok, :], [], ['pt32'])
                        CP(ptb[:], pt32[:], ['pt32'], ['ptb'])
                        for j in range(2):
                            TR(b16(2)[:, j * 128:(j + 1) * 128], ptb[:, j * 128:(j + 1) * 128], ident[:], ['ptb', 'ident'],
                               [Bn[2]], inc=(j == 1))
                        CP(pT[:, :, s_ * 128:(s_ + 1) * 128], b16(2)[:, 0:256].rearrange("p (j t) -> p j t", t=128),
                           [Bn[2]], ['pT'])
                    for oc in range(16):
                        ocs = slice(oc * 128, (oc + 1) * 128)
                        par_ = oc % 2
                        w = []
                        for (src, nk, kk_) in ((wb_uprw[:, ocs], 8, 'wb_uprw'), (wb_upfox[:, ocs], 8, 'wb_upfox'),
                                               (wb_gate[:, ocs], 16, 'wb_gate'),
                                               (wb_gate[:, 2048 + oc * 128:2048 + (oc + 1) * 128], 16, 'wb_gate')):
                            wi = nsm[0] % 5
                            nsm[0] += 1
                            DMA('sp', wsm[wi][:, 0:nk, :], src.rearrange("(k p) n -> p k n", p=128), [kk_], ['wsm%d' % wi])
                            w.append(wi)
                        bb = 4 * par_
                        for kc in range(8):
                            MM(B[bb][:], wsm[w[0]][:, kc, :], yTt[:, kc, :], kc == 0, kc == 7, ['wsm%d' % w[0], 'yTt'], [Bn[bb]], inc=(kc == 7))
                        for kc in range(8):
                            MM(B[bb + 1][:], wsm[w[1]][:, kc, :], yTt[:, 8 + kc, :], kc == 0, kc == 7, ['wsm%d' % w[1], 'yTt'], [Bn[bb + 1]], inc=(kc == 7))
                        for kc in range(16):
                            MM(B[bb + 2][:], wsm[w[2]][:, kc, :], hT4[:, kc, :], kc == 0, kc == 15, ['wsm%d' % w[2]] + hTr(hb), [Bn[bb + 2]], inc=(kc == 15))
                        for kc in range(16):
                            MM(B[bb + 3][:], wsm[w[3]][:, kc, :], hT4[:, kc, :], kc == 0, kc == 15, ['wsm%d' % w[3]] + hTr(hb), [Bn[bb + 3]], inc=(kc == 15))
                        k1, k2 = 'sg1_%d' % par_, 'sg2_%d' % par_
                        ACT(sg1[par_][:], B[bb + 2][:], AF.Sigmoid, [Bn[bb + 2]], [k1])
                        ACT(sg2[par_][:], B[bb + 3][:], AF.Sigmoid, [Bn[bb + 3]], [k2])
                        TT(sg1[par_][:], B[bb][:], sg1[par_][:], ALU.mult, [Bn[bb], k1], [k1])
                        TT(sg2[par_][:], B[bb + 1][:], sg2[par_][:], ALU.mult, [Bn[bb + 1], k2], [k2])
                        TT(mT[:, oc, :], sg1[par_][:], sg2[par_][:], ALU.add, [k1, k2], [('mT', oc)], eng='pool')
                    mTk = [('mT', oc) for oc in range(16)]
                    for cg in range(4):
                        cgs = slice(cg * 512, (cg + 1) * 512)
                        wi = nbg[0] % 2
                        nbg[0] += 1
                        DMA('sp', wbg[wi][:], wb_out[:, cgs].rearrange("(k p) n -> p k n", p=128), ['wb_out'], ['wbg%d' % wi])
                        for s_ in range(4):
                            bk = s_ % 4
                            for kc in range(16):
                                MM(B[bk][:], mT[:, kc, s_ * 128:(s_ + 1) * 128], wbg[wi][:, kc, :], kc == 0, kc == 15,
                                   mTk + ['wbg%d' % wi], [Bn[bk]], inc=(kc == 15))
                            TT(x1[:, s_, cgs], x1[:, s_, cgs], B[bk][:], ALU.add, [('x1', s_), Bn[bk]], [('x1', s_)])
                    for s_ in range(4):
                        norm_T(x1[:, s_, :], ('x1', s_), gp, 'gp', mT, mTk, s_)
                    for cg in range(4):
                        cgs = slice(cg * 512, (cg + 1) * 512)
                        wi = nbg[0] % 2
                        nbg[0] += 1
                        DMA('sp', wbg[wi][:], wb_pg[:, cgs].rearrange("(k p) n -> p k n", p=128), ['wb_pg'], ['wbg%d' % wi])
                        DMA('sp', wpp[:], wb_pp[:, cgs].rearrange("(k p) n -> p k n", p=128), ['wb_pp'], ['wpp'])
                        for s_ in range(4):
                            bk = 2 + (s_ % 2)
                            bp = 4 + (s_ % 2)
                            ks_ = 'sg1_%d' % (s_ % 2)
                            for kc in range(16):
                                MM(B[bk][:], mT[:, kc, s_ * 128:(s_ + 1) * 128], wbg[wi][:, kc, :], kc == 0, kc == 15,
                                   mTk + ['wbg%d' % wi], [Bn[bk]], inc=(kc == 15))
                            for kc in range(2):
                                MM(B[bp][:], pT[:, kc, s_ * 128:(s_ + 1) * 128], wpp[:, kc, :], kc == 0, kc == 1,
                                   ['pT', 'wpp'], [Bn[bp]], inc=(kc == 1))
                            ACT(sg1[s_ % 2][:], B[bk][:], AF.Sigmoid, [Bn[bk]], [ks_])
                            TT(sg1[s_ % 2][:], B[bp][:], sg1[s_ % 2][:], ALU.mult, [Bn[bp], ks_], [ks_])
                            TT(x1[:, s_, cgs], x1[:, s_, cgs], sg1[s_ % 2][:], ALU.add, [('x1', s_), ks_], [('x1', s_)], eng='pool')
                    for s_ in range(4):
                        tok = slice(TT_ * 512 + s_ * 128, TT_ * 512 + (s_ + 1) * 128)
                        rms_rstd(x1[:, s_, :], ('x1', s_))
                        STT(ot[:], x1[:, s_, :], st[:, 3:4], gf[:], ALU.mult, ALU.mult, [('x1', s_), 'st4', 'gf'], ['ot0'])
                        DMA('sp', out[tok, :], ot[:], ['ot0'], ['out'])
        S.barrier(final=True)
        with nc.Block() as block:
            S.emit(block)
    return nc

_CACHE = {}


def _prep(inputs, b, T, r=0):
    f = lambda a: np.ascontiguousarray(np.asarray(a, dtype=np.float32))
    H = T // 2
    role = np.zeros((128, 2), np.float32)
    role[:, r] = 1.0
    m = {
        "x": f(inputs["x"][b, :T]),
        "p": f(inputs["p"][0, b, :T]),
        "x_post": f(inputs["x"][b, r * H:(r + 1) * H]),
        "p_post": f(inputs["p"][0, b, r * H:(r + 1) * H]),
        "role": role,
        "norm_g": f(inputs["norm_g"][0:1]),
        "w_in": f(inputs["w_in"][0]),
        "rw_shift_mu": f(inputs["rw_shift_mu"][0:1]),
        "rw_w0": f(inputs["rw_w0"][0:1]),
        "rw_w_lora_up": f(inputs["rw_w_lora_up"][0]),
        "rw_a0": f(inputs["rw_a0"][0:1]),
        "rw_a_lora_up": f(inputs["rw_a_lora_up"][0]),
        "rw_k_k": f(inputs["rw_k_k"][0:1]),
        "rw_k_a": f(inputs["rw_k_a"][0:1]),
        "rw_r_k": f(np.asarray(inputs["rw_r_k"][0]).reshape(1, 1024)),
        "rw_ln_g": f(inputs["rw_ln_g"][0:1]),
        "rw_ln_b": f(inputs["rw_ln_b"][0:1]),
        "fox_b_f": f(inputs["fox_b_f"][0:1]),
        "w_up_rwkv": f(inputs["w_up_rwkv"][0]),
        "w_up_fox": f(inputs["w_up_fox"][0]),
        "w_out": f(inputs["w_out"][0]),
        "ple_proj": f(inputs["ple_proj"][0]),
        "ple_gate_w": f(inputs["ple_gate_w"][0]),
        "ple_norm_g": f(inputs["ple_norm_g"][0:1]),
        "final_norm_g": f(np.asarray(inputs["final_norm_g"]).reshape(1, D)),
    }
    return m


def kernel(**inputs):
    T = 4096
    if T not in _CACHE:
        _CACHE[T] = build(T)
    nc = _CACHE[T]
    in_maps = [_prep(inputs, c % 4, T, c // 4) for c in range(8)]
    res = run_bass_kernel_spmd(nc, in_maps, core_ids=list(range(8)))
    H = T // 2
    out = np.empty((4, T, D), np.float32)
    for c in range(8):
        out[c % 4, (c // 4) * H:(c // 4 + 1) * H] = np.asarray(res.results[c]["out"], dtype=np.float32)
    return out
```
